# Optimizing a Trainium2 kernel written in Bass

```python
import math
import jax
import jax.numpy as jnp
from jax import lax
import numpy as np

D_MODEL = 1024
BATCH = 8
SEQ = 2048
DEPTH = 4

N_MIXERS = 3
EPS = 1e-6
CONV_WIDTH = 3
E_CONV = 2 * D_MODEL
HEAD_DIM = 64
ATT_HEADS = D_MODEL // HEAD_DIM
E_ATT = ATT_HEADS * HEAD_DIM
DILATION_PATTERNS = ((128, 1), (512, 4), (2048, 16))
N_DIL_GROUPS = len(DILATION_PATTERNS)
ATT_BLOCK = 128
REL_BUCKETS = 32
REL_MAX_DIST = 2048
NEG_INF = -1e30
E_SSM = D_MODEL
SSM_GROUP = 16
SSM_GROUPS = E_SSM // SSM_GROUP
SSM_STATE = 64
DT_MIN = 0.001
DT_MAX = 0.1

kernel_name = 'hybrid_conv_dilattn_s5_adaln_trunk'


def rmsnorm(x, g=None):
    xf = x.astype(jnp.float32)
    y = xf * lax.rsqrt(jnp.mean(xf * xf, axis=-1, keepdims=True) + EPS)
    if g is not None:
        y = y * g.astype(jnp.float32)
    return y.astype(x.dtype)


def ada_modulation(c, w, b):
    mod = jax.nn.silu(c) @ w + b
    shift, scale, gate = jnp.split(mod, 3, axis=-1)
    return shift[:, None, :], scale[:, None, :], gate[:, None, :]


def short_conv_mixer(h, w_in, conv_w, conv_b, w_out):
    u, gc, gb, z = jnp.split(h @ w_in, 4, axis=-1)
    v = gc * u
    conv = lax.conv_general_dilated(
        v, conv_w[:, None, :], window_strides=(1,), padding=[(CONV_WIDTH - 1, 0)],
        dimension_numbers=('NWC', 'WIO', 'NWC'), feature_group_count=v.shape[-1])
    y = gb * (conv + conv_b) * jax.nn.silu(z)
    return y @ w_out


def t5_bucket(dist):
    max_exact = REL_BUCKETS // 2
    d = np.maximum(dist, 0)
    ratio = np.log(np.maximum(d, 1) / max_exact) / math.log(REL_MAX_DIST / max_exact)
    large = np.minimum(max_exact + (ratio * (REL_BUCKETS - max_exact)).astype(np.int64), REL_BUCKETS - 1)
    return np.where(d < max_exact, d, large).astype(np.int32)


def dilated_window_attention(q, k, v, bias_h, window, dil):
    B, S, H, Dh = q.shape
    L = S // dil
    wsub = window // dil
    assert wsub <= ATT_BLOCK
    nb = -(-L // ATT_BLOCK)
    Lp = nb * ATT_BLOCK

    def to_sub(a):
        a = a.reshape(B, L, dil, H, Dh).transpose(0, 2, 1, 3, 4)
        a = jnp.pad(a, ((0, 0), (0, 0), (0, Lp - L), (0, 0), (0, 0)))
        return a.reshape(B, dil, nb, ATT_BLOCK, H, Dh)

    def with_prev(a):
        prev = jnp.pad(a[:, :, :-1], ((0, 0), (0, 0), (1, 0), (0, 0), (0, 0), (0, 0)))
        return jnp.concatenate([prev, a], axis=3)

    qb = to_sub(q)
    kw = with_prev(to_sub(k))
    vw = with_prev(to_sub(v))
    s = jnp.einsum('brnqhd,brnkhd->brnhqk', qb, kw).astype(jnp.float32) / math.sqrt(Dh)

    qi = np.arange(ATT_BLOCK)[:, None]
    ki = np.arange(2 * ATT_BLOCK)[None, :]
    dist = ATT_BLOCK + qi - ki
    in_band = (dist >= 0) & (dist <= wsub)
    first = in_band & (ki >= ATT_BLOCK)
    valid = np.concatenate([first[None], np.broadcast_to(in_band, (nb - 1,) + in_band.shape)], axis=0)
    bucket = t5_bucket(dist * dil)
    bias = jnp.transpose(bias_h[bucket], (2, 0, 1)).astype(jnp.float32)

    s = jnp.where(valid[None, None, :, None], s + bias, NEG_INF)
    m = jnp.max(s, axis=-1, keepdims=True)
    p = jnp.exp(s - m)
    den = jnp.sum(p, axis=-1)
    o = jnp.einsum('brnhqk,brnkhd->brnqhd', p.astype(vw.dtype), vw)
    o = o / jnp.swapaxes(den, -1, -2)[..., None]
    lse = jnp.swapaxes(m[..., 0] + jnp.log(den), -1, -2)

    o = o.reshape(B, dil, Lp, H, Dh)[:, :, :L].transpose(0, 2, 1, 3, 4).reshape(B, S, H, Dh)
    lse = lse.reshape(B, dil, Lp, H)[:, :, :L].transpose(0, 2, 1, 3).reshape(B, S, H)
    return o, lse


def dilated_attention_mixer(h, w_in, w_out, rel_bias):
    B, S, _ = h.shape
    proj = h @ w_in
    n_qkv = 3 * N_DIL_GROUPS * E_ATT
    qkv = proj[..., :n_qkv].reshape(B, S, N_DIL_GROUPS, 3, ATT_HEADS, HEAD_DIM)
    z = proj[..., n_qkv:]
    outs = []
    lses = []
    for g, (window, dil) in enumerate(DILATION_PATTERNS):
        o, lse = dilated_window_attention(
            qkv[:, :, g, 0], qkv[:, :, g, 1], qkv[:, :, g, 2],
            rel_bias[:, g * ATT_HEADS:(g + 1) * ATT_HEADS], window, dil)
        outs.append(o)
        lses.append(lse)
    wgt = jax.nn.softmax(jnp.stack(lses, axis=0), axis=0)
    o = jnp.einsum('gbsh,gbshd->bshd', wgt, jnp.stack(outs, axis=0).astype(jnp.float32))
    y = o.reshape(B, S, E_ATT).astype(h.dtype) * jax.nn.silu(z)
    return y @ w_out


def s5_mixer(h, w_in, log_dt, lambda_re, lambda_im, b_re, b_im, c_re, c_im, d_skip, w_glu, b_glu, w_out):
    B, S, _ = h.shape
    u, z = jnp.split(h @ w_in, 2, axis=-1)
    uf = u.astype(jnp.float32)
    ug = uf.reshape(B, S, SSM_GROUPS, SSM_GROUP)
    dt = jnp.exp(log_dt.astype(jnp.float32))[:, None]
    lr = lambda_re.astype(jnp.float32)
    li = lambda_im.astype(jnp.float32)
    mag = jnp.exp(lr * dt)
    ar = mag * jnp.cos(li * dt)
    ai = mag * jnp.sin(li * dt)
    inv = 1.0 / (lr * lr + li * li)
    qr = ((ar - 1.0) * lr + ai * li) * inv
    qi = (ai * lr - (ar - 1.0) * li) * inv
    br = b_re.astype(jnp.float32)
    bi = b_im.astype(jnp.float32)
    bbr = qr[..., None] * br - qi[..., None] * bi
    bbi = qr[..., None] * bi + qi[..., None] * br
    xr = jnp.einsum('bsgk,gpk->bsgp', ug, bbr)
    xi = jnp.einsum('bsgk,gpk->bsgp', ug, bbi)
    a_r = jnp.broadcast_to(ar[None, None], (1, S, SSM_GROUPS, SSM_STATE))
    a_i = jnp.broadcast_to(ai[None, None], (1, S, SSM_GROUPS, SSM_STATE))

    def combine(e1, e2):
        a1r, a1i, b1r, b1i = e1
        a2r, a2i, b2r, b2i = e2
        return (a2r * a1r - a2i * a1i, a2r * a1i + a2i * a1r,
                a2r * b1r - a2i * b1i + b2r, a2r * b1i + a2i * b1r + b2i)

    _, _, sr, si = lax.associative_scan(combine, (a_r, a_i, xr, xi), axis=1)
    y = (jnp.einsum('bsgp,gkp->bsgk', sr, c_re.astype(jnp.float32))
         - jnp.einsum('bsgp,gkp->bsgk', si, c_im.astype(jnp.float32)))
    y = y.reshape(B, S, E_SSM) + d_skip.astype(jnp.float32) * uf
    g = jax.nn.gelu(y)
    y = g * jax.nn.sigmoid(g @ w_glu.astype(jnp.float32) + b_glu.astype(jnp.float32))
    y = y.astype(h.dtype) * jax.nn.silu(z)
    return y @ w_out


def setup_inputs(seed: int = 0) -> dict:
    key = jax.random.key(seed)
    ks = iter(jax.random.split(key, 64))

    def nrm(shape, scale):
        return jax.random.normal(next(ks), shape, jnp.float32) * scale

    D = D_MODEL
    inp = {}
    inp['x'] = nrm((BATCH, SEQ, D), 1.0)
    inp['c'] = nrm((BATCH, D), 1.0)
    inp['rel_bias'] = nrm((REL_BUCKETS, N_DIL_GROUPS * ATT_HEADS), 0.1)
    inp['final_g'] = 1.0 + nrm((D,), 0.01)

    def ada(prefix):
        inp[prefix + 'ada_w'] = nrm((D, 3 * D), 0.5 * D ** -0.5)
        inp[prefix + 'ada_b'] = nrm((3 * D,), 0.01)

    def conv_layer(prefix):
        ada(prefix)
        inp[prefix + 'w_in'] = nrm((D, 4 * E_CONV), D ** -0.5)
        inp[prefix + 'conv_w'] = nrm((CONV_WIDTH, E_CONV), CONV_WIDTH ** -0.5)
        inp[prefix + 'conv_b'] = nrm((E_CONV,), 0.01)
        inp[prefix + 'w_out'] = nrm((E_CONV, D), E_CONV ** -0.5)

    conv_layer('l0_')
    ada('l1_')
    inp['l1_w_in'] = nrm((D, (3 * N_DIL_GROUPS + 1) * E_ATT), D ** -0.5)
    inp['l1_w_out'] = nrm((E_ATT, D), E_ATT ** -0.5)
    ada('l2_')
    inp['l2_w_in'] = nrm((D, 2 * E_SSM), D ** -0.5)
    inp['l2_log_dt'] = jax.random.uniform(next(ks), (SSM_GROUPS,), jnp.float32, math.log(DT_MIN), math.log(DT_MAX))
    inp['l2_lambda_re'] = -0.5 + nrm((SSM_GROUPS, SSM_STATE), 0.01)
    inp['l2_lambda_im'] = math.pi * jnp.arange(SSM_STATE, dtype=jnp.float32)[None, :] + nrm((SSM_GROUPS, SSM_STATE), 0.01)
    inp['l2_b_re'] = nrm((SSM_GROUPS, SSM_STATE, SSM_GROUP), (2 * SSM_GROUP) ** -0.5)
    inp['l2_b_im'] = nrm((SSM_GROUPS, SSM_STATE, SSM_GROUP), (2 * SSM_GROUP) ** -0.5)
    inp['l2_c_re'] = nrm((SSM_GROUPS, SSM_GROUP, SSM_STATE), SSM_STATE ** -0.5)
    inp['l2_c_im'] = nrm((SSM_GROUPS, SSM_GROUP, SSM_STATE), SSM_STATE ** -0.5)
    inp['l2_d_skip'] = nrm((E_SSM,), 1.0)
    inp['l2_w_glu'] = nrm((E_SSM, E_SSM), E_SSM ** -0.5)
    inp['l2_b_glu'] = nrm((E_SSM,), 0.01)
    inp['l2_w_out'] = nrm((E_SSM, D), E_SSM ** -0.5)
    conv_layer('l3_')
    return inp


def reference(x, c, rel_bias, final_g,
              l0_ada_w, l0_ada_b, l0_w_in, l0_conv_w, l0_conv_b, l0_w_out,
              l1_ada_w, l1_ada_b, l1_w_in, l1_w_out,
              l2_ada_w, l2_ada_b, l2_w_in, l2_log_dt, l2_lambda_re, l2_lambda_im,
              l2_b_re, l2_b_im, l2_c_re, l2_c_im, l2_d_skip, l2_w_glu, l2_b_glu, l2_w_out,
              l3_ada_w, l3_ada_b, l3_w_in, l3_conv_w, l3_conv_b, l3_w_out):
    mixers = (short_conv_mixer, dilated_attention_mixer, s5_mixer)
    layers = (
        (l0_ada_w, l0_ada_b, (l0_w_in, l0_conv_w, l0_conv_b, l0_w_out)),
        (l1_ada_w, l1_ada_b, (l1_w_in, l1_w_out, rel_bias)),
        (l2_ada_w, l2_ada_b, (l2_w_in, l2_log_dt, l2_lambda_re, l2_lambda_im, l2_b_re, l2_b_im,
                              l2_c_re, l2_c_im, l2_d_skip, l2_w_glu, l2_b_glu, l2_w_out)),
        (l3_ada_w, l3_ada_b, (l3_w_in, l3_conv_w, l3_conv_b, l3_w_out)),
    )
    for i in range(DEPTH):
        ada_w, ada_b, mix_params = layers[i]
        shift, scale, gate = ada_modulation(c, ada_w, ada_b)
        h = rmsnorm(x) * (1.0 + scale) + shift
        x = x + gate * mixers[i % N_MIXERS](h, *mix_params)
    return rmsnorm(x, final_g)
```

```python
import contextlib
import os
import numpy as np
import concourse.bass as bass
import concourse.mybir as mybir
from concourse.bass_utils import run_bass_kernel_spmd

F32 = mybir.dt.float32
BF16 = mybir.dt.bfloat16
AF = mybir.ActivationFunctionType
ALU = mybir.AluOpType

D = 1024
S = 2048
NCH = 8
TT = 512
NTT = S // TT
EPS = 1e-6
DEBUG = bool(int(os.environ.get("KDEBUG", "0")))
NCORES = int(os.environ.get("KCORES", "8"))


class _Op:
    __slots__ = ("eng", "fn", "deps", "dma", "needs_inc", "count", "dsem", "dval")

    def __init__(self, eng, fn, deps, dma):
        self.eng = eng
        self.fn = fn
        self.deps = deps
        self.dma = dma
        self.needs_inc = False
        self.count = 0
        self.dsem = -1
        self.dval = 0


class Sched:
    NDS = 24
    BAR = "__bar__"

    def __init__(self):
        self.ops = []
        self.last_w = {}
        self.readers = {}

    def add(self, eng, fn, reads=(), writes=(), dma=False, is_barrier=False):
        idx = len(self.ops)
        if not is_barrier:
            reads = list(reads) + [Sched.BAR]
        deps = set()
        for k in reads:
            j = self.last_w.get(k)
            if j is not None:
                deps.add(j)
        for k in writes:
            j = self.last_w.get(k)
            if j is not None:
                deps.add(j)
            for r in self.readers.get(k, ()):
                deps.add(r)
        self.ops.append(_Op(eng, fn, deps, dma))
        for k in reads:
            lst = self.readers.setdefault(k, [])
            if not dma:
                lst[:] = [r for r in lst if self.ops[r].dma or self.ops[r].eng != eng]
            lst.append(idx)
        for k in writes:
            self.last_w[k] = idx
            self.readers[k] = []
        return idx

    def emit(self, nc, engines, sems, dsems, final_waits=True):
        ops = self.ops
        for op in ops:
            for j in op.deps:
                pj = ops[j]
                if pj.dma:
                    continue
                if pj.eng != op.eng or op.dma or op.eng != "pe":
                    pj.needs_inc = True
        cnt = {e: 0 for e in engines}
        ndma = 0
        for op in ops:
            if op.dma:
                op.dsem = ndma % self.NDS
                op.dval = 16 * (ndma // self.NDS + 1)
                ndma += 1
            elif op.needs_inc:
                cnt[op.eng] += 1
                op.count = cnt[op.eng]
        per_eng = {e: [] for e in engines}
        for i, op in enumerate(ops):
            per_eng[op.eng].append(i)
        last_dma = [None] * self.NDS

        def run_engine(ename, e):
            waited = {}

            def wait(sem_key, sem, val):
                if waited.get(sem_key, 0) >= val:
                    return
                waited[sem_key] = val
                e.wait_ge(sem, val)

            for i in per_eng[ename]:
                op = ops[i]
                for j in sorted(op.deps):
                    pj = ops[j]
                    if pj.dma:
                        wait(("d", pj.dsem), dsems[pj.dsem], pj.dval)
                    elif pj.eng != op.eng or op.dma or op.eng != "pe":
                        wait(("c", pj.eng), sems[pj.eng], pj.count)
                if op.dma:
                    if op.dval > 16:
                        wait(("d", op.dsem), dsems[op.dsem], op.dval - 16)
                    ins = op.fn(e)
                    ins.then_inc(dsems[op.dsem], 16)
                else:
                    ins = op.fn(e)
                    if op.needs_inc:
                        ins.then_inc(sems[op.eng], 1)
            if ename == "sp" and final_waits:
                fin = {}
                for op in ops:
                    if op.dma:
                        fin[op.dsem] = max(fin.get(op.dsem, 0), op.dval)
                for s_, v_ in fin.items():
                    wait(("d", s_), dsems[s_], v_)

        return run_engine


def tile_w(W):
    K, N = W.shape
    A = K // 1024
    NT = N // 128
    t = W.reshape(A, 8, 128, NT, 128).transpose(0, 3, 2, 1, 4)
    return np.ascontiguousarray(t).reshape(A * NT, 128, 1024)


def colmajor(v, nch):
    return np.ascontiguousarray(np.asarray(v).reshape(nch, 128).T)


SMALL_COLS = {}

DILS = ((128, 1), (512, 4), (2048, 16))
NEGV = -30000.0


def _t5_bucket(dist):
    import math
    max_exact = 16
    d = np.maximum(dist, 0)
    ratio = np.log(np.maximum(d, 1) / max_exact) / math.log(2048 / max_exact)
    large = np.minimum(max_exact + (ratio * (32 - max_exact)).astype(np.int64), 31)
    return np.where(d < max_exact, d, large).astype(np.int32)


def attn_static():
    oh = np.zeros((32, 3, 384), np.float32)
    negm = np.full((128, 384), NEGV, np.float32)
    w = np.arange(127, 256)
    negm[:, 127:256] = 0.0
    for g, (win, dil) in enumerate(DILS):
        bk = _t5_bucket((w - 127) * dil)
        oh[bk, g, w] = 1.0
    return oh.reshape(32, 3 * 384), negm


def _small_layout():
    off = 0
    lay = {}

    def put(name, n):
        nonlocal off
        lay[name] = (off, n)
        off += n

    put("c", 8)
    put("final_g", 8)
    for l in range(4):
        put(f"l{l}_ada_b", 24)
    for l in (0, 3):
        put(f"l{l}_conv_w", 48)
        put(f"l{l}_conv_b", 16)
    put("l2_d_skip", 8)
    put("l2_b_glu", 8)
    return lay, off


SMALL_LAY, SMALL_N = _small_layout()


def pack_small(inp, b):
    sm = np.zeros((128, SMALL_N), np.float32)

    def put(name, arr):
        o, n = SMALL_LAY[name]
        assert arr.shape == (128, n), (name, arr.shape, n)
        sm[:, o:o + n] = arr

    put("c", colmajor(inp["c"][b], 8))
    put("final_g", colmajor(inp["final_g"], 8))
    for l in range(4):
        put(f"l{l}_ada_b", colmajor(inp[f"l{l}_ada_b"], 24))
    for l in (0, 3):
        cw = np.asarray(inp[f"l{l}_conv_w"])
        put(f"l{l}_conv_w", np.ascontiguousarray(cw.T.reshape(16, 128, 3).transpose(1, 0, 2)).reshape(128, 48))
        put(f"l{l}_conv_b", colmajor(inp[f"l{l}_conv_b"], 16))
    put("l2_d_skip", colmajor(inp["l2_d_skip"], 8))
    put("l2_b_glu", colmajor(inp["l2_b_glu"], 8))
    return sm


def s5_host_inputs(inp):
    f = lambda k: np.asarray(inp[k], np.float32)
    lre, lim, ldt = f("l2_lambda_re"), f("l2_lambda_im"), f("l2_log_dt")
    bre, bim, cre, cim = f("l2_b_re"), f("l2_b_im"), f("l2_c_re"), f("l2_c_im")
    l1 = np.zeros((32, 128, 67), np.float32)
    for q in range(32):
        for g2 in range(2):
            g = 2 * q + g2
            rows = slice(g2 * 64, g2 * 64 + 64)
            l1[q, rows, 0:16] = cre[g].T
            l1[q, rows, 16:32] = cim[g].T
            l1[q, rows, 32:48] = bre[g]
            l1[q, rows, 48:64] = bim[g]
            l1[q, rows, 64] = lre[g]
            l1[q, rows, 65] = lim[g]
            l1[q, rows, 66] = ldt[g]
    l2 = np.zeros((8, 128, 257), np.float32)
    for fc in range(8):
        for gl in range(8):
            g = fc * 8 + gl
            rows = slice(gl * 16, gl * 16 + 16)
            l2[fc, rows, 0:64] = bre[g].T
            l2[fc, rows, 64:128] = bim[g].T
            l2[fc, rows, 128:192] = lre[g][None, :]
            l2[fc, rows, 192:256] = lim[g][None, :]
            l2[fc, rows, 256] = ldt[g]
    ident = np.eye(128, dtype=np.float32)
    gmask = np.zeros((128, 8), np.float32)
    for gl in range(8):
        gmask[gl * 16:(gl + 1) * 16, gl] = 1.0
    return {"s5_l1": l1, "s5_l2": l2, "s5_ident": ident, "s5_gmask": gmask}


def _bc(ap, pos, n):
    dims = [list(d) for d in ap.ap]
    dims.insert(pos, [0, n])
    return bass.AP(ap.tensor, ap.offset, dims)

class Builder:
    def __init__(self, layers, do_final):
        self.layers = layers
        self.do_final = do_final
        self.nc = bass.Bass("TRN2", target_bir_lowering=False)
        self.sch = Sched()
        self.bank_ctr = 0
        self.stg_ctr = 0
        self.wb_ctr = 0

    def op(self, eng, fn, reads=(), writes=()):
        return self.sch.add(eng, fn, reads, writes)

    def dma(self, out, in_, reads=(), writes=()):
        return self.sch.add("sp", lambda e: e.dma_start(out=out, in_=in_), reads, writes, dma=True)

    def next_bank(self):
        b = self.bank_ctr % 8
        self.bank_ctr += 1
        return b

    def psb(self, b):
        return self.ps[:, b, :]

    def load_wtile(self, dram_ap_tile, cast_eng="pool", want_bf16=True):
        s = self.stg_ctr % self.NSTG
        self.stg_ctr += 1
        self.dma(self.stg[:, s, :], dram_ap_tile, writes=[("stg", s)])
        if not want_bf16:
            return ("stg", s), self.stg[:, s, :]
        w = self.wb_ctr % self.NWB
        self.wb_ctr += 1
        src = self.stg[:, s, :]
        dst = self.wb[:, w, :]
        if cast_eng == "act":
            self.op("act", lambda e: e.activation(out=dst, in_=src, func=AF.Copy), reads=[("stg", s)], writes=[("wb", w)])
        else:
            self.op(cast_eng, lambda e: e.tensor_copy(out=dst, in_=src), reads=[("stg", s)], writes=[("wb", w)])
        return ("wb", w), self.wb[:, w, :]

    def wstream(self, reqs, la):
        self._ws_reqs = list(reqs)
        self._ws_issued = 0
        self._ws_taken = 0
        self._ws_q = []
        self._ws_la = la

    def wget(self):
        la = self._ws_la
        for k_ in range(self._ws_taken, min(len(self._ws_reqs), self._ws_taken + 1 + la)):
            if not self._ws_reqs[k_][2]:
                la = min(la, self.NSTG - 1)
        want = min(len(self._ws_reqs), self._ws_taken + 1 + la)
        while self._ws_issued < want:
            ap, ceng, bf = self._ws_reqs[self._ws_issued]
            self._ws_q.append(self.load_wtile(ap, cast_eng=ceng, want_bf16=bf))
            self._ws_issued += 1
        self._ws_taken += 1
        return self._ws_q.pop(0)

    def small(self, name, c0=0, n=None):
        o, nn = SMALL_LAY[name]
        if n is None:
            n = nn - c0
        return self.sm[:, o + c0:o + c0 + n]

    def emit_ada_mm(self, ada_w):
        b = self.next_bank()
        for j in range(24):
            key, wt = self.wget()
            for kc in range(8):
                lhsT = wt[:, kc * 128:(kc + 1) * 128]
                rhs = self.sc[:, kc:kc + 1]
                o = self.ps[:, b, j:j + 1]
                self.op("pe", lambda e, o=o, lhsT=lhsT, rhs=rhs, kc=kc: e.matmul(o, lhsT, rhs, start=(kc == 0), stop=(kc == 7)),
                        reads=[key, "sc"], writes=[("ps", b)])
        return b

    def emit_ada_fin(self, l, b, mod, mk):
        ps = self.ps[:, b, 0:24]
        ab = self.small(f"l{l}_ada_b")
        self.op("dve", lambda e: e.tensor_tensor(out=mod[:, 0:24], in0=ps, in1=ab, op=ALU.add),
                reads=[("ps", b), "sm"], writes=[mk])
        self.op("dve", lambda e: e.tensor_scalar_add(mod[:, 8:16], mod[:, 8:16], 1.0), reads=[mk], writes=[mk])

    def emit_ada(self, l, ada_w):
        if self.pre_ada == l:
            self.mod, self.modk = self.modB, "modB"
            return
        self.mod, self.modk = self.modA, "mod"
        b = self.emit_ada_mm(ada_w)
        self.emit_ada_fin(l, b, self.modA, "mod")

    def emit_norm(self, scale_ap, bias_ap, out_tile, out_key, to_x=False):
        xt = self.xt
        for tt in range(NTT):
            ts = slice(tt * TT, (tt + 1) * TT)
            b = self.next_bank()
            for kc in range(NCH):
                sq = self.sq[:, kc % self.NNT, :]
                xin = xt[:, kc, ts]
                self.op("act", lambda e, sq=sq, xin=xin: e.activation(out=sq, in_=xin, func=AF.Square),
                        reads=[("x", kc, tt)], writes=[("sq", kc % self.NNT)])
                o = self.psb(b)
                self.op("pe", lambda e, o=o, sq=sq, kc=kc: e.matmul(o, self.onesb16[:, :], sq, start=(kc == 0), stop=(kc == NCH - 1)),
                        reads=[("sq", kc % self.NNT), "onesb16"], writes=[("ps", b)])
            rstd = self.rstd
            pb = self.psb(b)
            self.op("act", lambda e, pb=pb: e.activation(out=rstd[:, :], in_=pb, func=AF.Sqrt, scale=1.0 / D, bias=self.epsc[:, 0:1]),
                    reads=[("ps", b), "epsc"], writes=["rstd"])
            self.op("dve", lambda e: e.reciprocal(out=rstd[:, :], in_=rstd[:, :]), reads=["rstd"], writes=["rstd"])
            for kc in range(NCH):
                bt = self.next_bank()
                tmp = self.psb(bt)
                xin = xt[:, kc, ts]
                self.op("dve", lambda e, tmp=tmp, xin=xin: e.tensor_tensor(out=tmp, in0=xin, in1=rstd[:, :], op=ALU.mult),
                        reads=[("x", kc, tt), "rstd"], writes=[("ps", bt)])
                o = out_tile[:, kc, ts]
                sc_ = scale_ap[:, kc:kc + 1]
                if bias_ap is not None:
                    bi_ = bias_ap[:, kc:kc + 1]
                    self.op("act", lambda e, o=o, tmp=tmp, sc_=sc_, bi_=bi_: e.activation(out=o, in_=tmp, func=AF.Identity, scale=sc_, bias=bi_),
                            reads=[("ps", bt), self.modk, "sm"], writes=[(out_key, kc, tt)])
                else:
                    self.op("act", lambda e, o=o, tmp=tmp, sc_=sc_: e.activation(out=o, in_=tmp, func=AF.Identity, scale=sc_),
                            reads=[("ps", bt), self.modk, "sm"], writes=[(out_key, kc, tt)])

    def emit_wout(self, w_out_tiles, tile_idx_fn, y_tile, y_key, n_k, gate_ap, dc_order=None):
        xt = self.xt
        for dc in (dc_order if dc_order is not None else range(NCH)):
            key, wt = self.wget()
            for tt in range(NTT):
                ts = slice(tt * TT, (tt + 1) * TT)
                b = self.next_bank()
                for kc in range(n_k):
                    lhsT = wt[:, kc * 128:(kc + 1) * 128]
                    rhs = y_tile[:, kc, ts]
                    o = self.psb(b)
                    self.op("pe", lambda e, o=o, lhsT=lhsT, rhs=rhs, kc=kc: e.matmul(o, lhsT, rhs, start=(kc == 0), stop=(kc == n_k - 1)),
                            reads=[key, (y_key, kc, tt)], writes=[("ps", b)])
                xo = xt[:, dc, ts]
                pb = self.psb(b)
                g_ = gate_ap[:, dc:dc + 1]
                self.op("dve", lambda e, xo=xo, pb=pb, g_=g_: e.scalar_tensor_tensor(out=xo, in0=pb, scalar=g_, in1=xo, op0=ALU.mult, op1=ALU.add),
                        reads=[("ps", b), self.modk, ("x", dc, tt)], writes=[("x", dc, tt)])

    def emit_conv_layer(self, l, ada_w, w_in, w_out):
        reqs = [(ada_w[j], "pool", False) for j in range(24)] if self.pre_ada != l else []
        for half in range(2):
            for el in range(8):
                for q in range(4):
                    reqs.append((w_in[q * 16 + half * 8 + el], "pool" if q % 2 == 0 else "act", True))
            for dc in range(NCH):
                reqs.append((w_out[half * 8 + dc], "pool", True))
        self.wstream(reqs, 2)
        self.emit_ada(l, ada_w)
        mod = self.mod
        self.emit_norm(mod[:, 8:16], mod[:, 0:8], self.h, "h")
        cw = self.small(f"l{l}_conv_w")
        cb = self.small(f"l{l}_conv_b")
        h = self.h
        vb = self.vbuf
        for half in range(2):
            for el in range(8):
                e_ = half * 8 + el
                wk = []
                for q in range(4):
                    wk.append(self.wget())
                for tt in range(NTT):
                    ts = slice(tt * TT, (tt + 1) * TT)
                    banks = []
                    for q in range(4):
                        b = self.next_bank()
                        banks.append(b)
                        key, wt = wk[q]
                        for kc in range(NCH):
                            lhsT = wt[:, kc * 128:(kc + 1) * 128]
                            rhs = h[:, kc, ts]
                            o = self.psb(b)
                            self.op("pe", lambda e, o=o, lhsT=lhsT, rhs=rhs, kc=kc: e.matmul(o, lhsT, rhs, start=(kc == 0), stop=(kc == NCH - 1)),
                                    reads=[key, ("h", kc, tt)], writes=[("ps", b)])
                    bu, bgc, bgb, bz = banks
                    usb = self.usb
                    pu = self.psb(bu)
                    self.op("act", lambda e, pu=pu: e.activation(out=usb[:, :], in_=pu, func=AF.Copy), reads=[("ps", bu)], writes=["usb"])
                    vcur = vb[:, 2 + tt * TT: 2 + (tt + 1) * TT]
                    pgc = self.psb(bgc)
                    self.op("dve", lambda e, vcur=vcur, pgc=pgc: e.tensor_tensor(out=vcur, in0=pgc, in1=usb[:, :], op=ALU.mult),
                            reads=[("ps", bgc), "usb"], writes=[("v", tt)])
                    v0 = vb[:, tt * TT: (tt + 1) * TT]
                    v1 = vb[:, 1 + tt * TT: 1 + (tt + 1) * TT]
                    cacc = self.cacc
                    w0 = cw[:, e_ * 3 + 0:e_ * 3 + 1]
                    w1 = cw[:, e_ * 3 + 1:e_ * 3 + 2]
                    w2 = cw[:, e_ * 3 + 2:e_ * 3 + 3]
                    rv = [("v", tt)] + ([("v", tt - 1)] if tt > 0 else [])
                    self.op("pool", lambda e, v0=v0, w0=w0: e.tensor_scalar(out=cacc[:, :], in0=v0, scalar1=w0, scalar2=0.0, op0=ALU.mult, op1=ALU.add),
                            reads=rv + ["sm"], writes=["cacc"])
                    self.op("dve", lambda e, v1=v1, w1=w1: e.scalar_tensor_tensor(out=cacc[:, :], in0=v1, scalar=w1, in1=cacc[:, :], op0=ALU.mult, op1=ALU.add),
                            reads=rv + ["sm", "cacc"], writes=["cacc"])
                    self.op("dve", lambda e, vcur=vcur, w2=w2: e.scalar_tensor_tensor(out=cacc[:, :], in0=vcur, scalar=w2, in1=cacc[:, :], op0=ALU.mult, op1=ALU.add),
                            reads=rv + ["sm", "cacc"], writes=["cacc"])
                    pgb = self.psb(bgb)
                    cb_ = cb[:, e_:e_ + 1]
                    gt = self.gtmp
                    self.op("dve", lambda e, pgb=pgb, cb_=cb_: e.scalar_tensor_tensor(out=gt[:, :], in0=cacc[:, :], scalar=cb_, in1=pgb, op0=ALU.add, op1=ALU.mult),
                            reads=["cacc", "sm", ("ps", bgb)], writes=["gtmp"])
                    sz = self.sz
                    pz = self.psb(bz)
                    self.op("act", lambda e, pz=pz: e.activation(out=sz[:, :], in_=pz, func=AF.Silu), reads=[("ps", bz)], writes=["sz"])
                    yo = self.y[:, el, ts]
                    self.op("pool", lambda e, yo=yo: e.tensor_tensor(out=yo, in0=gt[:, :], in1=sz[:, :], op=ALU.mult),
                            reads=["gtmp", "sz"], writes=[("y", el, tt)])
            self.emit_wout(w_out, lambda dc, half=half: half * 8 + dc, self.y, "y", 8, mod[:, 16:24])


    def emit_attn_tables(self):
        self.tables_done = True
        if getattr(self, "in_attn", False):
            self.tsoh, self.trrep = self.soh, self.rrep
        else:
            self.tsoh = self.carve([128, 2, 384], F32)
            self.trrep = self.carve([128, 2, 384], F32)
        self.trb = self.carve([128, 48], F32)
        self.toh = self.carve([128, 3 * 384], F32)
        self.tnegm = self.carve([128, 384], F32)
        self.dma(self.trb[0:32, :], self.rb_in, writes=["rb"])
        self.dma(self.toh[0:32, :], self.oh_in, writes=["oh"])
        self.dma(self.tnegm[:, :], self.negm_in, writes=["negm"])
        for gh in range(48):
            g = gh // 16
            soh = self.tsoh[0:32, gh % 2, :]
            ohg = self.toh[0:32, g * 384:(g + 1) * 384]
            rbc = self.trb[0:32, gh:gh + 1]
            self.op("dve", lambda e, soh=soh, ohg=ohg, rbc=rbc: e.tensor_scalar(out=soh, in0=ohg, scalar1=rbc, scalar2=0.0, op0=ALU.mult, op1=ALU.add),
                    reads=["oh", "rb"], writes=[("soh", gh % 2)])
            b = self.next_bank()
            o = self.ps[:, b, 0:384]
            self.op("pe", lambda e, o=o, soh=soh: e.matmul(o, self.ones[0:32, :], soh, start=True, stop=True),
                    reads=[("soh", gh % 2), "ones"], writes=[("ps", b)])
            rr = self.trrep[:, gh % 2, :]
            self.op("dve", lambda e, rr=rr, o=o: e.tensor_tensor(out=rr, in0=o, in1=self.tnegm[:, :], op=ALU.add),
                    reads=[("ps", b), "negm"], writes=[("rrep", gh % 2)])
            self.dma(self.rscr[gh], rr, reads=[("rrep", gh % 2)], writes=[("rscr", gh)])


    def emit_attn_layer(self, ada_w, w_in, w_outk):
        l = 1
        reqs = [(ada_w[j], "pool", False) for j in range(24)]
        for hp in range(8):
            reqs.append((w_in[72 + hp], "act", True))
            for g in range(3):
                for t_ in range(3):
                    reqs.append((w_in[(g * 3 + t_) * 8 + hp], "pool" if t_ != 1 else "act", True))
                if g == 0 and hp > 0:
                    reqs.append((w_outk[hp - 1], "act", True))
        reqs.append((w_outk[7], "act", True))
        self.wstream(reqs, 4)
        self.emit_ada(l, ada_w)
        mod = self.mod
        self.emit_norm(mod[:, 8:16], mod[:, 0:8], self.h, "h")
        h = self.h
        xt = self.xt
        if not self.tables_done:
            self.in_attn = True
            self.emit_attn_tables()
        qT, kA, vA, ND, szb = self.qT, self.kA, self.vA, self.ND, self.szb
        STAGE = int(os.environ.get("KATT_STAGE", "9"))
        if STAGE == 0:
            return
        def emit_wout_hp(hp):
            yhp = self.yhp[:, hp % 2, :]
            key, wt = self.wget()
            for dc in range(NCH):
                for tt in range(NTT):
                    ts = slice(tt * TT, (tt + 1) * TT)
                    b = self.next_bank()
                    o = self.psb(b)
                    lhsT = wt[:, dc * 128:(dc + 1) * 128]
                    rhs = yhp[:, ts]
                    self.op("pe", lambda e, o=o, lhsT=lhsT, rhs=rhs: e.matmul(o, lhsT, rhs, start=True, stop=True),
                            reads=[key, ("yhp", hp % 2, tt)], writes=[("ps", b)])
                    xo = xt[:, dc, ts]
                    g_ = mod[:, 16 + dc:17 + dc]
                    self.op("dve", lambda e, xo=xo, o=o, g_=g_: e.scalar_tensor_tensor(out=xo, in0=o, scalar=g_, in1=xo, op0=ALU.mult, op1=ALU.add),
                            reads=[("ps", b), self.modk, ("x", dc, tt)], writes=[("x", dc, tt)])

        for hp in range(8 if STAGE >= 9 else 1):
            key, wt = self.wget()
            for tt in range(NTT):
                ts = slice(tt * TT, (tt + 1) * TT)
                b = self.next_bank()
                for kc in range(NCH):
                    o = self.psb(b)
                    lhsT = wt[:, kc * 128:(kc + 1) * 128]
                    rhs = h[:, kc, ts]
                    self.op("pe", lambda e, o=o, lhsT=lhsT, rhs=rhs, kc=kc: e.matmul(o, lhsT, rhs, start=(kc == 0), stop=(kc == NCH - 1)),
                            reads=[key, ("h", kc, tt)], writes=[("ps", b)])
                pb = self.psb(b)
                so = szb[:, ts]
                self.op("act", lambda e, so=so, pb=pb: e.activation(out=so, in_=pb, func=AF.Silu), reads=[("ps", b)], writes=[("szb", tt)])
            for g, (win, dil) in enumerate(DILS):
                tbi = (hp * 3 + g) % 2
                tabf = self.tab[:, tbi, :]
                tabk = ("tab", tbi)
                for hd in range(2):
                    gh = g * 16 + hp * 2 + hd
                    for blk, off in ((0, 255), (1, 127)):
                        src = bass.AP(self.rscr_h, gh * 128 * 384 + off, [[383, 128], [1, 128]])
                        dstt = tabf[:, (hd * 2 + blk) * 128:(hd * 2 + blk + 1) * 128]
                        self.dma(dstt, src, reads=[("rscr", gh)], writes=[tabk])
                self.op("act", lambda e, tabf=tabf: e.activation(out=tabf, in_=tabf, func=AF.Exp), reads=[tabk], writes=[tabk])
                for t_, dkey, scl in ((0, "qT", 0.125), (1, "kT", 1.0)):
                    key, wt = self.wget()
                    for tt in range(NTT):
                        ts = slice(tt * TT, (tt + 1) * TT)
                        b = self.next_bank()
                        for kc in range(NCH):
                            o = self.psb(b)
                            lhsT = wt[:, kc * 128:(kc + 1) * 128]
                            rhs = h[:, kc, ts]
                            self.op("pe", lambda e, o=o, lhsT=lhsT, rhs=rhs, kc=kc: e.matmul(o, lhsT, rhs, start=(kc == 0), stop=(kc == NCH - 1)),
                                    reads=[key, ("h", kc, tt)], writes=[("ps", b)])
                        pb = self.psb(b)
                        if t_ == 0:
                            do = qT[:, ts]
                            self.op("act", lambda e, do=do, pb=pb, scl=scl: e.activation(out=do, in_=pb, func=AF.Copy, scale=scl),
                                    reads=[("ps", b)], writes=[dkey])
                        else:
                            for hd in range(2):
                                do = kA[hd * 64:(hd + 1) * 64, hd, ts]
                                pi = self.ps[hd * 64:(hd + 1) * 64, b, :]
                                self.op("act", lambda e, do=do, pi=pi: e.activation(out=do, in_=pi, func=AF.Copy),
                                        reads=[("ps", b)], writes=[dkey])
                nb = (S // dil) // 128
                key, wt = self.wget()

                def tok(ti, dil=dil, nb=nb):
                    r, n = ti // nb, ti % nb
                    st_ = r + dil * 128 * n
                    return slice(st_, st_ + dil * 127 + 1, dil)

                for tb in range(4):
                    b = self.next_bank()
                    for tj in range(4):
                        ti = tb * 4 + tj
                        for kc in range(NCH):
                            o = self.ps[:, b, tj * 128:(tj + 1) * 128]
                            lhsT = h[:, kc, tok(ti)]
                            rhs = wt[:, kc * 128:(kc + 1) * 128]
                            self.op("pe", lambda e, o=o, lhsT=lhsT, rhs=rhs, kc=kc: e.matmul(o, lhsT, rhs, start=(kc == 0), stop=(kc == NCH - 1)),
                                    reads=[key] + [("h", kc, t4) for t4 in range(NTT)], writes=[("ps", b)])
                    pb4 = self.psb(b).rearrange("p (a q) -> p a q", a=4)
                    for hd in range(2):
                        vo = vA[:, hd, tb * 512:(tb + 1) * 512].rearrange("p (a q) -> p a q", a=4)[:, :, hd * 64:(hd + 1) * 64]
                        pi = pb4[:, :, hd * 64:(hd + 1) * 64]
                        self.op("act", lambda e, vo=vo, pi=pi: e.activation(out=vo, in_=pi, func=AF.Copy), reads=[("ps", b)], writes=[("vtm", tb)])
                if g == 0 and hp > 0:
                    emit_wout_hp(hp - 1)
                EW = [("rrep", 0), ("rrep", 1), ("soh", 0), ("soh", 1)]
                NB3 = 4

                def emit_scores(ti):
                    n = ti % nb
                    b = self.next_bank()
                    blks = (0, 1) if n > 0 else (1,)
                    for hd in range(2):
                        for blk in blks:
                            kti = ti if blk == 1 else ti - 1
                            o = self.ps[:, b, (hd * 2 + blk) * 128:(hd * 2 + blk + 1) * 128]
                            lhsT = kA[:, hd, tok(kti)]
                            rhs = qT[:, tok(ti)]
                            self.op("pe", lambda e, o=o, lhsT=lhsT, rhs=rhs: e.matmul(o, lhsT, rhs, start=True, stop=True),
                                    reads=["qT", "kT"], writes=[("ps", b)])
                    return b

                def emit_softmax(ti, b):
                    n = ti % nb
                    bu = ti % NB3
                    E = self.E[:, bu, :]
                    pT = self.pT[:, bu, :]
                    if n > 0:
                        pb = self.psb(b)
                        self.op("act", lambda e, E=E, pb=pb: e.activation(out=E, in_=pb, func=AF.Exp),
                                reads=[("ps", b)], writes=[("E", bu)] + EW)
                        self.op("dve", lambda e, E=E, pT=pT, tabf=tabf: e.tensor_tensor(out=pT[:, 0:256], in0=E[:, 0:256], in1=tabf[:, 0:256], op=ALU.mult),
                                reads=[("E", bu), tabk], writes=[("pT", bu, 0)])
                        self.op("pool", lambda e, E=E, pT=pT, tabf=tabf: e.tensor_tensor(out=pT[:, 256:512], in0=E[:, 256:512], in1=tabf[:, 256:512], op=ALU.mult),
                                reads=[("E", bu), tabk], writes=[("pT", bu, 1)])
                    else:
                        for hd in range(2):
                            cs = slice((hd * 2 + 1) * 128, (hd * 2 + 2) * 128)
                            pb = self.ps[:, b, cs]
                            eo = E[:, cs]
                            self.op("act", lambda e, eo=eo, pb=pb: e.activation(out=eo, in_=pb, func=AF.Exp),
                                    reads=[("ps", b)], writes=[("E", bu)] + EW)
                            po = pT[:, cs]
                            tb_ = tabf[:, cs]
                            self.op("dve" if hd == 0 else "pool", lambda e, po=po, eo=eo, tb_=tb_: e.tensor_tensor(out=po, in0=eo, in1=tb_, op=ALU.mult),
                                    reads=[("E", bu), tabk], writes=[("pT", bu, hd)])

                def emit_pv(ti):
                    n = ti % nb
                    bu = ti % NB3
                    pT = self.pT[:, bu, :]
                    blks = (0, 1) if n > 0 else (1,)
                    b2 = self.next_bank()
                    for which in range(2):
                        o = self.ps[:, b2, which * 128:(which + 1) * 128]
                        combos = [(hd, blk) for hd in range(2) for blk in blks]
                        for ci, (hd, blk) in enumerate(combos):
                            kti = ti if blk == 1 else ti - 1
                            if which == 0:
                                lhsT = vA[:, hd, kti * 128:(kti + 1) * 128]
                                rk = [("vtm", kti // 4)]
                            else:
                                lhsT = self.onesA[:, hd, :]
                                rk = ["onesb"]
                            rhs = pT[:, (hd * 2 + blk) * 128:(hd * 2 + blk + 1) * 128]
                            self.op("pe", lambda e, o=o, lhsT=lhsT, rhs=rhs, ci=ci, ncb=len(combos): e.matmul(o, lhsT, rhs, start=(ci == 0), stop=(ci == ncb - 1)),
                                    reads=rk + [("pT", bu, 0), ("pT", bu, 1)], writes=[("ps", b2)])
                    ndo = ND[:, :, tok(ti)]
                    pin = self.ps[:, b2, 0:256].rearrange("p (a q) -> p a q", a=2)
                    if g == 0:
                        self.op("dve", lambda e, ndo=ndo, pin=pin: e.tensor_copy(out=ndo, in_=pin), reads=[("ps", b2)], writes=["ND"])
                    else:
                        self.op("dve", lambda e, ndo=ndo, pin=pin: e.tensor_tensor(out=ndo, in0=pin, in1=ndo, op=ALU.add),
                                reads=[("ps", b2), "ND"], writes=["ND"])

                sbank = {}
                for t0 in range(3):
                    sbank[t0] = emit_scores(t0)
                emit_softmax(0, sbank[0])
                emit_softmax(1, sbank[1])
                for ti in range(16):
                    if ti + 3 < 16:
                        sbank[ti + 3] = emit_scores(ti + 3)
                    if ti + 2 < 16:
                        emit_softmax(ti + 2, sbank[ti + 2])
                    emit_pv(ti)
            for tt in range(NTT):
                ts = slice(tt * TT, (tt + 1) * TT)
                rc = self.ssb[:, tt % 2, :]
                self.op("dve", lambda e, rc=rc, ts=ts: e.reciprocal(out=rc, in_=ND[:, 1, ts]), reads=["ND"], writes=[("ssb", tt % 2)])
                self.op("pool", lambda e, rc=rc, ts=ts: e.tensor_tensor(out=rc, in0=rc, in1=ND[:, 0, ts], op=ALU.mult), reads=["ND", ("ssb", tt % 2)], writes=[("ssb", tt % 2)])
                yo = self.yhp[:, hp % 2, ts]
                so = szb[:, ts]
                self.op("pool", lambda e, rc=rc, yo=yo, so=so: e.tensor_tensor(out=yo, in0=rc, in1=so, op=ALU.mult),
                        reads=[("ssb", tt % 2), ("szb", tt)], writes=[("yhp", hp % 2, tt)])
            pending_wout = hp
            if hp == 7:
                emit_wout_hp(7)

    def tt(self, o, a, b, op, eng="dve", extra=()):
        self.op(eng, lambda e: e.tensor_tensor(out=o[0], in0=a[0], in1=b[0], op=op), reads=[a[1], b[1]] + list(extra), writes=[o[1]])

    def tsc(self, o, a, s1, s2, op0, op1, eng="dve", extra=()):
        rk = [a[1]] + list(extra)
        s1v, s2v = s1, s2
        if isinstance(s1, tuple):
            rk.append(s1[1]); s1v = s1[0]
        if isinstance(s2, tuple):
            rk.append(s2[1]); s2v = s2[0]
        self.op(eng, lambda e: e.tensor_scalar(out=o[0], in0=a[0], scalar1=s1v, scalar2=s2v, op0=op0, op1=op1), reads=rk, writes=[o[1]])

    def stt(self, o, a, sc, b, op0, op1, eng="dve", extra=()):
        rk = [a[1], b[1]] + list(extra)
        scv = sc
        if isinstance(sc, tuple):
            rk.append(sc[1]); scv = sc[0]
        self.op(eng, lambda e: e.scalar_tensor_tensor(out=o[0], in0=a[0], scalar=scv, in1=b[0], op0=op0, op1=op1), reads=rk, writes=[o[1]])

    def actf(self, o, a, func, scale=1.0, bias=None, extra=()):
        rk = [a[1]] + list(extra)
        if bias is not None:
            rk.append(bias[1])
            bv = bias[0]
            self.op("act", lambda e: e.activation(out=o[0], in_=a[0], func=func, scale=scale, bias=bv), reads=rk, writes=[o[1]])
        else:
            self.op("act", lambda e: e.activation(out=o[0], in_=a[0], func=func, scale=scale), reads=rk, writes=[o[1]])

    def epoch(self, region, kill=()):
        d = self.dummy
        self.op("pool", lambda e: e.memset(d[:, 0:1], 0.0), writes=[("ep", region)] + list(kill))

    def emit_zoh(self, lre, lim, ldt, scr, tag):
        sl = lambda i: (scr[:, i, :], (tag, i))
        dt, x, mag, th, sn, cs, t1, t2, ar, ai, inv, am1, qr, qi = [sl(i) for i in range(14)]
        hp_ = (self.halfpi[:, 0:1], "halfpi")
        self.actf(dt, ldt, AF.Exp)
        self.tt(x, lre, dt, ALU.mult)
        self.actf(mag, x, AF.Exp)
        self.tt(th, lim, dt, ALU.mult)
        self.actf(sn, th, AF.Sin, scale=1.0 / 32.0)
        self.actf(cs, th, AF.Sin, scale=1.0 / 32.0, bias=hp_)
        for _ in range(5):
            self.tt(t1, cs, cs, ALU.mult)
            self.tt(t2, sn, sn, ALU.mult)
            self.stt(sn, cs, 2.0, sn, ALU.mult, ALU.mult)
            self.tt(cs, t1, t2, ALU.subtract)
        self.tt(ar, mag, cs, ALU.mult)
        self.tt(ai, mag, sn, ALU.mult)
        self.tt(t1, lre, lre, ALU.mult)
        self.tt(t2, lim, lim, ALU.mult)
        self.tt(t1, t1, t2, ALU.add)
        self.op("dve", lambda e: e.reciprocal(out=inv[0], in_=t1[0]), reads=[t1[1]], writes=[inv[1]])
        self.tsc(am1, ar, -1.0, None, ALU.add, ALU.bypass)
        self.tt(t1, am1, lre, ALU.mult)
        self.tt(t2, ai, lim, ALU.mult)
        self.tt(t1, t1, t2, ALU.add)
        self.tt(qr, t1, inv, ALU.mult)
        self.tt(t1, ai, lre, ALU.mult)
        self.tt(t2, am1, lim, ALU.mult)
        self.tt(t1, t1, t2, ALU.subtract)
        self.tt(qi, t1, inv, ALU.mult)
        return ar, ai, qr, qi, t1, t2

    def emit_powers(self, pw, tag, ar, ai, t1, t2, nmax):
        P = lambda m, ri: (pw[:, m, ri, :], (tag, m, ri))
        p0r, p0i = P(0, 0), P(0, 1)
        self.op("pool", lambda e: e.memset(p0r[0], 1.0), writes=[p0r[1]])
        self.op("pool", lambda e: e.memset(p0i[0], 0.0), writes=[p0i[1]])
        for m in range(1, nmax + 1):
            self.tt(t1, P(m - 1, 0), ar, ALU.mult)
            self.tt(t2, P(m - 1, 1), ai, ALU.mult)
            self.tt(P(m, 0), t1, t2, ALU.subtract)
            self.tt(t1, P(m - 1, 0), ai, ALU.mult)
            self.tt(t2, P(m - 1, 1), ar, ALU.mult)
            self.tt(P(m, 1), t1, t2, ALU.add)
        return P

    def emit_s5_layer(self, ada_w, w_in, w_glu, w_out, l1_in, l2_in):
        l = 2
        reqs = [(ada_w[j], "pool", False) for j in range(24)]
        reqs += [(w_in[fc], "act", True) for fc in range(8)]
        if self.next_ada is not None:
            reqs += [(self.next_ada[1][j], "pool", False) for j in range(24)]
        if int(os.environ.get("KS5_STAGE", "9")) >= 3:
            reqs += [(w_in[fc], "act", True) for fc in range(8)]
        if int(os.environ.get("KS5_STAGE", "9")) >= 4:
            for dc in range(NCH):
                reqs += [(w_glu[dc], "pool", True), (w_in[8 + dc], "act", True)]
        S5_DC = [4, 5, 6, 7, 0, 1, 2, 3]
        reqs += [(w_out[dc], "pool", True) for dc in S5_DC]
        self.wstream(reqs, 3)
        self.emit_ada(l, ada_w)
        mod = self.mod
        self.emit_norm(mod[:, 8:16], mod[:, 0:8], self.h, "h")
        h = self.h
        xkeys = [("x", kc, tt) for kc in range(NCH) for tt in range(NTT)]
        for kc in range(NCH):
            self.dma(self.xspill[:, kc, :], self.xt[:, kc, :], reads=[("x", kc, tt) for tt in range(NTT)], writes=[("xspill", kc)])
        self.epoch("bigf", kill=xkeys)
        EPF = ("ep", "bigf")
        XS = self.bigf[:, :].rearrange("p (c r q) -> p c r q", r=2, q=32)
        gbv = self.bigf.bitcast(BF16)[:, 0:NCH * S].rearrange("p (c t) -> p c t", c=NCH)
        Sbf = self.bigb[:, 0:256 * 64].rearrange("p (r q c) -> p r q c", r=2, q=32)
        y2b = self.bigb[:, 0:NCH * S].rearrange("p (c t) -> p c t", c=NCH)
        EPB = ("ep", "bigb")

        lam1 = self.lam1
        self.dma(lam1[:, :, :], l1_in[:, :, 64:67].rearrange("q p c -> p q c"), writes=["lam1"])
        L1 = lambda i: (lam1[:, :, i], "lam1")
        ar1, ai1, qr1, qi1, t1a, t2a = self.emit_zoh(L1(0), L1(1), L1(2), self.scr1, "scr1")
        q1 = self.q1
        self.op("dve", lambda e: e.tensor_copy(out=q1[:, 0, :], in_=qr1[0]), reads=[qr1[1]], writes=["q1"])
        self.op("dve", lambda e: e.tensor_copy(out=q1[:, 1, :], in_=qi1[0]), reads=[qi1[1]], writes=["q1"])
        P1 = self.emit_powers(self.pw1, "pw1", ar1, ai1, t1a, t2a, 8)
        pw1keys = [("pw1", m, ri) for m in range(9) for ri in range(2)]
        scr1keys = [("scr1", i) for i in range(16)]
        AA, AI = self.AA, self.AI
        for r_ in range(2):
            self.op("dve", lambda e, r_=r_: e.tensor_copy(out=AA[:, r_, :], in_=self.pw1[:, 8, 0, :]), reads=[("pw1", 8, 0)], writes=["AA"])
            sgn = -1.0 if r_ == 0 else 1.0
            self.op("dve", lambda e, r_=r_, sgn=sgn: e.tensor_scalar(out=AI[:, r_, :], in0=self.pw1[:, 8, 1, :], scalar1=sgn, scalar2=None, op0=ALU.mult), reads=[("pw1", 8, 1)], writes=["AI"])

        MC = self.MC
        MGb, MCC, BDf, BDall, MB = self.MGb, self.MCC, self.BDf, self.BDall, self.MB
        identb = self.identb
        self.op("act", lambda e: e.activation(out=identb[:, :], in_=self.ident[:, :], func=AF.Copy), reads=["ident"], writes=["identb"])
        Hs = self.Hs
        PT = self.PT
        self.op("pool", lambda e: e.memset(XS[:, 0, :, :], 0.0), reads=[EPF], writes=[("XS", 0)])
        self.op("pool", lambda e: e.memset(MGb[:, :, :, :, :], 0.0), writes=[("MG", 0), ("MG", 1)])
        self.op("pool", lambda e: e.memset(MCC[:, :, :, :], 0.0), writes=[("MCC", 0), ("MCC", 1)])
        hs = lambda i: (Hs[:, i, :, :], ("Hs", i))
        hs_dead = [("Hs", i) for i in range(6)]

        def pair_tile(q):
            pt = PT[:, q % 2, :]
            ptk = ("PT", q % 2)
            self.dma(pt, l1_in[q][:, 0:64], writes=[ptk])
            return pt, ptk, [(pt[:, a:a + 16], ptk) for a in (0, 16, 32, 48)]

        bc8 = lambda a: (_bc(a[0], 1, 8), a[1])
        Apw = lambda q, m0, ri: (_bc(self.pw1[:, m0:m0 + 8, ri, q], 2, 16), ("pw1", 8, 0))
        self.op("dve", lambda e: e.memset(Hs[:, 0, 0, 0:1], 0.0), writes=scr1keys + [("Hs", i) for i in range(6)])

        ub = self.ub16
        ub3 = self.ub16.rearrange("p (i c) -> p i c", i=8)

        def uproj(fc):
            key, wt = self.wget()
            for tt in range(NTT):
                ts = slice(tt * TT, (tt + 1) * TT)
                b = self.next_bank()
                for kc in range(NCH):
                    o = self.psb(b)
                    lhsT = wt[:, kc * 128:(kc + 1) * 128]
                    rhs = h[:, kc, ts]
                    self.op("pe", lambda e, o=o, lhsT=lhsT, rhs=rhs, kc=kc: e.matmul(o, lhsT, rhs, start=(kc == 0), stop=(kc == NCH - 1)),
                            reads=[key, ("h", kc, tt)], writes=[("ps", b)])
                pb = self.psb(b)
                uo = ub3[:, :, tt * 64:(tt + 1) * 64]
                pin_u = pb.rearrange("p (c i) -> p i c", i=8)
                self.op("act", lambda e, uo=uo, pin_u=pin_u: e.activation(out=uo, in_=pin_u, func=AF.Copy), reads=[("ps", b)], writes=["ub"])

        def stageA(q):
            fc, jp = q // 4, q % 4
            pt, ptk, (cre, cim, bre, bim) = pair_tile(q)
            ex = pw1keys
            qrs = (q1[:, 0, q:q + 1], "q1")
            qis = (q1[:, 1, q:q + 1], "q1")
            bbr = (self.bb1[:, 0, :], ("bb1", 0))
            bbi = (self.bb1[:, 1, :], ("bb1", 1))
            tb = (self.bb1[:, 2, :], ("bb1", 2))
            self.tsc(tb, bim, qis, None, ALU.mult, ALU.bypass)
            self.stt(bbr, bre, qrs, tb, ALU.mult, ALU.subtract)
            self.tsc(tb, bre, qis, None, ALU.mult, ALU.bypass)
            self.stt(bbi, bim, qrs, tb, ALU.mult, ALU.add)
            so = (q % 2) * 2
            self.tt(hs(0), bc8(bbr), Apw(q, 0, 0), ALU.mult, extra=ex)
            self.tt(hs(1), bc8(bbi), Apw(q, 0, 1), ALU.mult, extra=ex)
            self.tt(hs(2 + so), hs(0), hs(1), ALU.subtract)
            self.tt(hs(0), bc8(bbr), Apw(q, 0, 1), ALU.mult, extra=ex)
            self.tt(hs(1), bc8(bbi), Apw(q, 0, 0), ALU.mult, extra=ex)
            self.tt(hs(3 + so), hs(0), hs(1), ALU.add)
            mb_ = q % 2
            mgk, mck = ("MG", mb_), ("MCC", mb_)
            if q > 1:
                pj = (jp - 2) % 4
                for g2 in range(2):
                    rows = slice(g2 * 64, (g2 + 1) * 64)
                    cols = slice((2 * pj + g2) * 16, (2 * pj + g2 + 1) * 16)
                    self.op("pool", lambda e, rows=rows, cols=cols, mb_=mb_: e.memset(MGb[rows, mb_, :, :, cols], 0.0), reads=[mgk], writes=[mgk])
                    self.op("pool", lambda e, rows=rows, cols=cols, mb_=mb_: e.memset(MCC[rows, mb_, :, cols], 0.0), reads=[mck], writes=[mck])
            for g2 in range(2):
                rows = slice(g2 * 64, (g2 + 1) * 64)
                cols = slice((2 * jp + g2) * 16, (2 * jp + g2 + 1) * 16)
                for ri in range(2):
                    o = MGb[rows, mb_, :, ri, cols]
                    src = Hs[rows, 2 + so + ri, :, :]
                    self.op("act", lambda e, o=o, src=src: e.activation(out=o, in_=src, func=AF.Copy), reads=[("Hs", 2 + so + ri), mgk], writes=[mgk])
                o2 = MCC[rows, mb_, 0, cols]
                s2 = pt[rows, 0:16]
                self.op("act", lambda e, o2=o2, s2=s2: e.activation(out=o2, in_=s2, func=AF.Copy), reads=[ptk, mck], writes=[mck])
                o3 = MCC[rows, mb_, 1, cols]
                s3 = pt[rows, 16:32]
                self.op("act", lambda e, o3=o3, s3=s3: e.activation(out=o3, in_=s3, func=AF.Copy, scale=-1.0), reads=[ptk, mck], writes=[mck])

        def stageB(q):
            fc, jp = q // 4, q % 4
            mb_ = q % 2
            mgk, mck = ("MG", mb_), ("MCC", mb_)
            for hb in range(2):
                b = self.next_bank()
                for t4 in range(4):
                    tau = hb * 4 + t4
                    o = self.ps[:, b, t4 * 128:(t4 + 1) * 128]
                    for ri in range(2):
                        lhsT = MGb[:, mb_, tau, ri, :]
                        rhs = MCC[:, mb_, ri, :]
                        self.op("pe", lambda e, o=o, lhsT=lhsT, rhs=rhs, ri=ri: e.matmul(o, lhsT, rhs, start=(ri == 0), stop=(ri == 1)),
                                reads=[mgk, mck], writes=[("ps", b)])
                pb = self.psb(b)
                bo = BDf[:, hb * 512:(hb + 1) * 512]
                if jp == 0:
                    self.op("dve", lambda e, bo=bo, pb=pb: e.tensor_copy(out=bo, in_=pb), reads=[("ps", b)], writes=[("BDf", hb)])
                else:
                    self.op("dve", lambda e, bo=bo, pb=pb: e.tensor_tensor(out=bo, in0=pb, in1=bo, op=ALU.add), reads=[("ps", b), ("BDf", hb)], writes=[("BDf", hb)])
            for k4 in range(4):
                b = self.next_bank()
                for sl_, (i, ri) in enumerate([(2 * k4, 0), (2 * k4, 1), (2 * k4 + 1, 0), (2 * k4 + 1, 1)]):
                    o = self.ps[:, b, sl_ * 128:(sl_ + 1) * 128]
                    src = MGb[:, mb_, 7 - i, ri, :]
                    self.op("pe", lambda e, o=o, src=src: e.matmul(o, src, identb[:, :], start=True, stop=True),
                            reads=[mgk, "identb"], writes=[("ps", b)])
                pb = self.psb(b)
                mo = MB[:, mb_, 2 * k4:2 * k4 + 2, :, :]
                pin = pb.rearrange("p (a r c) -> p a r c", a=2, r=2)
                self.op("act", lambda e, mo=mo, pin=pin: e.activation(out=mo, in_=pin, func=AF.Copy), reads=[("ps", b)], writes=[("MB", mb_)])
            if jp == 3:
                dsk = self.small("l2_d_skip")[:, fc:fc + 1]
                self.op("dve", lambda e, dsk=dsk: e.scalar_tensor_tensor(out=BDf[:, 0:128], in0=self.ident[:, :], scalar=dsk, in1=BDf[:, 0:128], op0=ALU.mult, op1=ALU.add),
                        reads=["ident", "sm", ("BDf", 0)], writes=[("BDf", 0)])
                bdo = BDall[:, fc, :]
                self.op("act", lambda e, bdo=bdo: e.activation(out=bdo, in_=BDf[:, :], func=AF.Copy), reads=[("BDf", 0), ("BDf", 1)], writes=[("BD", fc)])

        def stageC(q):
            mb_ = q % 2
            b = self.next_bank()
            for ri in range(2):
                o = self.ps[:, b, ri * 256:(ri + 1) * 256]
                for i in range(8):
                    lhsT = MB[:, mb_, i, ri, :]
                    rhs = ub3[:, i, :]
                    self.op("pe", lambda e, o=o, lhsT=lhsT, rhs=rhs, i=i: e.matmul(o, lhsT, rhs, start=(i == 0), stop=(i == 7)),
                            reads=[("MB", mb_), "ub"], writes=[("ps", b)])
            for ri in range(2):
                pin = self.ps[:, b, ri * 256:(ri + 1) * 256]
                xo = XS[:, 1:257, ri, q]
                self.op("act", lambda e, xo=xo, pin=pin: e.activation(out=xo, in_=pin, func=AF.Copy), reads=[("ps", b), EPF], writes=[("XSq", q)])

        stageA(0)
        stageA(1)
        stageB(0)
        for q in range(32):
            if q % 4 == 0:
                uproj(q // 4)
            if q + 2 < 32:
                stageA(q + 2)
            if q + 1 < 32:
                stageB(q + 1)
            stageC(q)


        xsq = [("XSq", q) for q in range(32)]
        P1t, P2t = self.P1t, self.P2t
        S5ST = int(os.environ.get("KS5_STAGE", "9"))
        ada_bank = None
        if self.next_ada is not None:
            ada_bank = self.emit_ada_mm(self.next_ada[1])
        for c in range(1, 257 if S5ST >= 2 else 1):
            prev = XS[:, c - 1, :, :]
            pv = XS[:, c - 1, 1, :]
            prev_sw = bass.AP(pv.tensor, pv.offset, [list(pv.ap[0]), [-32, 2], [1, 32]])
            cur = XS[:, c, :, :]
            rk = [("XS", c - 1), "AA", "AI", EPF] + (xsq if c == 1 else [])
            self.op("dve", lambda e, prev=prev: e.tensor_tensor(out=P1t[:, :, :], in0=AA[:, :, :], in1=prev, op=ALU.mult), reads=rk, writes=["P1t"])
            self.op("dve", lambda e, prev_sw=prev_sw: e.tensor_tensor(out=P2t[:, :, :], in0=AI[:, :, :], in1=prev_sw, op=ALU.mult), reads=rk, writes=["P2t"])
            wk = [("XS", c)]
            rk2 = xsq if c == 1 else []
            self.op("dve", lambda e, cur=cur: e.tensor_tensor(out=cur, in0=cur, in1=P1t[:, :, :], op=ALU.add), reads=["P1t", EPF] + rk2 + wk, writes=wk)
            self.op("dve", lambda e, cur=cur: e.tensor_tensor(out=cur, in0=cur, in1=P2t[:, :, :], op=ALU.add), reads=["P2t", EPF] + wk, writes=wk)
        if ada_bank is not None:
            self.emit_ada_fin(self.next_ada[0], ada_bank, self.modB, "modB")
            self.pre_ada = self.next_ada[0]
        self.epoch("bigb")
        for part in range(4):
            cs_ = slice(part * 64, (part + 1) * 64)
            so = Sbf[:, :, :, cs_]
            si = XS[:, cs_, :, :].rearrange("p c r q -> p r q c")
            self.op("pool", lambda e, so=so, si=si: e.tensor_copy(out=so, in_=si), reads=[("XS", c) for c in range(part * 64, part * 64 + 64)] + [EPF, EPB], writes=[("Sbf", part)])
        sbfk = [("Sbf", p_) for p_ in range(4)]
        self.epoch("bigf", kill=[("XS", c) for c in range(257)])
        for kc in range(4, NCH):
            self.dma(self.xt[:, kc, :], self.xspill[:, kc, :], reads=[EPF, ("xspill", kc)], writes=[("x", kc, tt) for tt in range(NTT)])
        self.op("pool", lambda e: e.memset(MC[:, :, :, :, :], 0.0), writes=["MC", ("MG", 0), ("MG", 1), ("MB", 0), ("MB", 1)])

        for fc in range(8 if S5ST >= 3 else 0):
            for jp in range(4):
                q = fc * 4 + jp
                pt, ptk, (cre, cim, bre, bim) = pair_tile(q)
                ex = pw1keys
                so = (jp % 2) * 2
                self.tt(hs(0), bc8(cre), Apw(q, 1, 0), ALU.mult, extra=ex)
                self.tt(hs(1), bc8(cim), Apw(q, 1, 1), ALU.mult, extra=ex)
                self.tt(hs(2 + so), hs(0), hs(1), ALU.subtract)
                self.tt(hs(0), bc8(cre), Apw(q, 1, 1), ALU.mult, extra=ex)
                self.tt(hs(1), bc8(cim), Apw(q, 1, 0), ALU.mult, extra=ex)
                self.stt(hs(3 + so), hs(0), -1.0, hs(1), ALU.mult, ALU.subtract)
                for g2 in range(2):
                    rows = slice(g2 * 64, (g2 + 1) * 64)
                    cols = slice((2 * jp + g2) * 16, (2 * jp + g2 + 1) * 16)
                    for ri in range(2):
                        o = MC[rows, jp, :, ri, cols]
                        src = Hs[rows, 2 + so + ri, :, :]
                        self.op("act", lambda e, o=o, src=src: e.activation(out=o, in_=src, func=AF.Copy), reads=[("Hs", 2 + so + ri), "MC"], writes=["MC"])
            key, wt = self.wget()
            for tt in range(NTT):
                ts = slice(tt * TT, (tt + 1) * TT)
                b = self.next_bank()
                for kc in range(NCH):
                    o = self.psb(b)
                    lhsT = wt[:, kc * 128:(kc + 1) * 128]
                    rhs = h[:, kc, ts]
                    self.op("pe", lambda e, o=o, lhsT=lhsT, rhs=rhs, kc=kc: e.matmul(o, lhsT, rhs, start=(kc == 0), stop=(kc == NCH - 1)),
                            reads=[key, ("h", kc, tt)], writes=[("ps", b)])
                pb = self.psb(b)
                uo = ub3[:, :, tt * 64:(tt + 1) * 64]
                pin_u = pb.rearrange("p (c i) -> p i c", i=8)
                self.op("act", lambda e, uo=uo, pin_u=pin_u: e.activation(out=uo, in_=pin_u, func=AF.Copy), reads=[("ps", b)], writes=["ub"])
            for j in range(8):
                b = self.next_bank()
                o = self.ps[:, b, 0:256]
                nmm = (j + 1) + 8
                k_ = 0
                for i in range(j + 1):
                    lhsT = BDall[:, fc, (j - i) * 128:(j - i + 1) * 128]
                    rhs = ub3[:, i, :]
                    self.op("pe", lambda e, o=o, lhsT=lhsT, rhs=rhs, k_=k_, nmm=nmm: e.matmul(o, lhsT, rhs, start=(k_ == 0), stop=(k_ == nmm - 1)),
                            reads=[("BD", fc), "ub"], writes=[("ps", b)])
                    k_ += 1
                for jp in range(4):
                    q = fc * 4 + jp
                    for ri in range(2):
                        lhsT = MC[:, jp, j, ri, :]
                        rhs = Sbf[:, ri, q, :]
                        self.op("pe", lambda e, o=o, lhsT=lhsT, rhs=rhs, k_=k_, nmm=nmm: e.matmul(o, lhsT, rhs, start=(k_ == 0), stop=(k_ == nmm - 1)),
                                reads=["MC", EPB] + sbfk, writes=[("ps", b)])
                        k_ += 1
                go = gbv[:, fc, j::8]
                self.op("act", lambda e, go=go, o=o: e.activation(out=go, in_=o, func=AF.Gelu_apprx_tanh), reads=[("ps", b), EPF], writes=[("gb", fc)])

        self.epoch("bigb", kill=sbfk)
        bgl = self.small("l2_b_glu")
        for dc in range(NCH if S5ST >= 4 else 0):
            keyg, wg = self.wget()
            keyz, wz = self.wget()
            for tt in range(NTT):
                ts = slice(tt * TT, (tt + 1) * TT)
                b = self.next_bank()
                for kc in range(NCH):
                    o = self.psb(b)
                    lhsT = wg[:, kc * 128:(kc + 1) * 128]
                    rhs = gbv[:, kc, ts]
                    self.op("pe", lambda e, o=o, lhsT=lhsT, rhs=rhs, kc=kc: e.matmul(o, lhsT, rhs, start=(kc == 0), stop=(kc == NCH - 1)),
                            reads=[keyg, ("gb", kc), EPF], writes=[("ps", b)])
                bz = self.next_bank()
                for kc in range(NCH):
                    o = self.psb(bz)
                    lhsT = wz[:, kc * 128:(kc + 1) * 128]
                    rhs = h[:, kc, ts]
                    self.op("pe", lambda e, o=o, lhsT=lhsT, rhs=rhs, kc=kc: e.matmul(o, lhsT, rhs, start=(kc == 0), stop=(kc == NCH - 1)),
                            reads=[keyz, ("h", kc, tt)], writes=[("ps", bz)])
                sg = self.sg
                szz = self.szz
                pb = self.psb(b)
                pz = self.psb(bz)
                bcol = bgl[:, dc:dc + 1]
                self.op("act", lambda e, pb=pb, bcol=bcol: e.activation(out=sg[:, :], in_=pb, func=AF.Sigmoid, bias=bcol), reads=[("ps", b), "sm"], writes=["sg"])
                self.op("act", lambda e, pz=pz: e.activation(out=szz[:, :], in_=pz, func=AF.Silu), reads=[("ps", bz)], writes=["szz"])
                gin = gbv[:, dc, ts]
                self.op("dve", lambda e, gin=gin: e.tensor_tensor(out=sg[:, :], in0=sg[:, :], in1=gin, op=ALU.mult), reads=["sg", ("gb", dc), EPF], writes=["sg"])
                yo = y2b[:, dc, ts]
                self.op("dve", lambda e, yo=yo: e.tensor_tensor(out=yo, in0=sg[:, :], in1=szz[:, :], op=ALU.mult), reads=["sg", "szz", EPB], writes=[("y2", dc, tt)])
        self.epoch("bigf", kill=[("gb", fc) for fc in range(8)])
        for kc in range(4):
            self.dma(self.xt[:, kc, :], self.xspill[:, kc, :], reads=[EPF, ("xspill", kc)], writes=[("x", kc, tt) for tt in range(NTT)])
        self.emit_wout(w_out, lambda dc: dc, y2b, "y2", 8, mod[:, 16:24], dc_order=S5_DC)

    def carve_reset(self):
        self.gen_off = 0
        self.bb_off = 0

    def carve(self, shape, dt, pool="gen"):
        n = 1
        for d_ in shape[1:]:
            n *= d_
        if pool == "gen":
            esz = 4 if dt == F32 else 2
            off = (self.gen_off + 31) // 32 * 32
            self.gen_off = off + n * esz
            assert self.gen_off <= self.GEN_BYTES, ("GEN overflow", self.gen_off)
            base = self.GEN if dt == F32 else self.GENb
            ap = base[:, off // esz: off // esz + n]
        else:
            assert dt == BF16
            off = (self.bb_off + 31) // 32 * 32
            self.bb_off = off + n * 2
            assert self.bb_off <= 16448 * 2, ("bigb overflow", self.bb_off)
            ap = self.bigb[:, off // 2: off // 2 + n]
        if len(shape) > 2:
            names = " ".join(f"d{i}" for i in range(len(shape) - 1))
            kw = {f"d{i}": shape[i + 1] for i in range(len(shape) - 2)}
            ap = ap.rearrange(f"p ({names}) -> p {names}", **kw)
        return ap

    def alloc_conv(self):
        self.carve_reset()
        self.y = self.carve([128, 8, S], BF16, "bigb")
        self.usb = self.carve([128, TT], F32)
        self.vbuf = self.carve([128, 2 + S], F32)
        self.cacc = self.carve([128, TT], F32)
        self.gtmp = self.carve([128, TT], F32)
        self.sz = self.carve([128, TT], F32)
        vb = self.vbuf
        self.op("pool", lambda e: e.memset(vb[:, 0:2], 0.0), writes=[("v", -1)])

    def alloc_attn(self):
        self.carve_reset()
        self.qT = self.carve([128, S], BF16, "bigb")
        self.kA = self.carve([128, 2, S], BF16, "bigb")
        self.vA = self.carve([128, 2, 16 * 128], BF16, "bigb")
        self.onesA = self.carve([128, 2, 128], BF16, "bigb")
        self.yhp = self.carve([128, 2, S], BF16, "bigb")
        self.pT = self.carve([128, 4, 512], BF16)
        self.ND = self.carve([128, 2, S], F32)
        self.szb = self.carve([128, S], F32)
        self.tab = self.carve([128, 2, 512], F32)
        self.ssb = self.carve([128, 2, 512], F32)
        self.E = self.carve([128, 4, 512], F32)
        e0 = self.E
        self.soh = bass.AP(e0.tensor, e0.offset, [list(e0.ap[0]), [384, 2], [1, 384]])
        self.rrep = bass.AP(e0.tensor, e0.offset + 768, [list(e0.ap[0]), [384, 2], [1, 384]])
        kA, vA, onesA = self.kA, self.vA, self.onesA
        self.op("pool", lambda e: e.memset(kA[:, :, :], 0.0), writes=["kT"])
        self.op("pool", lambda e: e.memset(vA[:, :, :], 0.0), writes=[("vtm", tb) for tb in range(4)])
        self.op("pool", lambda e: e.memset(onesA[:, :, :], 0.0), writes=["onesb"])
        self.op("pool", lambda e: e.memset(onesA[:, 0, 0:64], 1.0), writes=["onesb"])
        self.op("pool", lambda e: e.memset(onesA[:, 1, 64:128], 1.0), writes=["onesb"])

    def alloc_s5(self):
        self.carve_reset()
        self.bigf = self.xt_raw
        off0 = (self.gen_off + 31) // 32 * 32
        self.MC = self.carve([128, 4, 8, 2, 128], BF16)
        self.MB = self.GENb[:, off0 // 2: off0 // 2 + 4096].rearrange("p (a b c d) -> p a b c d", a=2, b=8, c=2)
        self.MGb = self.GENb[:, (off0 + 8192) // 2: (off0 + 8192) // 2 + 4096].rearrange("p (u t r c) -> p u t r c", u=2, t=8, r=2)
        self.ub16 = self.carve([128, S], BF16)
        self.BDall = self.carve([128, 8, 1024], BF16)
        sgz = self.carve([128, 2 * TT], F32)
        self.sg = sgz[:, 0:TT]
        self.szz = sgz[:, TT:2 * TT]
        self.BDf = sgz
        self.gq = self.sg[:, 0:256]
        self.gt_ = self.szz[:, 0:256]
        self.MCC = self.carve([128, 2, 2, 128], BF16)
        self.identb = self.carve([128, 128], BF16)
        self.lam1 = self.carve([128, 32, 3], F32)
        self.Hs = self.carve([128, 6, 8, 16], F32)
        hs0 = self.Hs
        self.scr1 = bass.AP(hs0.tensor, hs0.offset, [list(hs0.ap[0]), [32, 16], [1, 32]])
        self.pw1 = self.carve([128, 9, 2, 32], F32)
        self.q1 = self.carve([128, 2, 32], F32)
        self.AA = self.carve([128, 2, 32], F32)
        self.AI = self.carve([128, 2, 32], F32)
        self.P1t = self.carve([128, 2, 32], F32)
        self.P2t = self.carve([128, 2, 32], F32)
        self.PT = self.carve([128, 2, 64], F32)
        self.bb1 = self.carve([128, 3, 16], F32)
        self.halfpi = self.carve([128, 1], F32)
        self.ident = self.carve([128, 128], F32)
        hpi = self.halfpi
        self.op("pool", lambda e: e.memset(hpi[:, :], float(np.pi / 2)), writes=["halfpi"])
        self.dma(self.ident[:, :], self.ident_in, writes=["ident"])

    def barrier(self):
        d = self.dummy
        self.sch.add("pool", lambda e: e.memset(d[:, 0:1], 0.0), reads=(), writes=[Sched.BAR], is_barrier=True)

    def build(self):
        nc = self.nc
        L = self.layers
        xin = nc.dram_tensor("xT", [D, S], F32, kind="ExternalInput").ap()
        small_in = nc.dram_tensor("small", [128, SMALL_N], F32, kind="ExternalInput").ap()
        wd = {}

        def wten(name, ntiles):
            wd[name] = nc.dram_tensor(name, [ntiles, 128, 1024], F32, kind="ExternalInput").ap()

        for l in L:
            wten(f"l{l}_ada_w", 24)
            if l in (0, 3):
                wten(f"l{l}_w_in", 64)
                wten(f"l{l}_w_out", 16)
            if l == 2:
                wten("l2_w_in", 16)
                wten("l2_w_glu", 8)
                wten("l2_w_out", 8)
                self.s5l1_in = nc.dram_tensor("s5_l1", [32, 128, 67], F32, kind="ExternalInput").ap()
                self.s5l2_in = nc.dram_tensor("s5_l2", [8, 128, 257], F32, kind="ExternalInput").ap()
                self.ident_in = nc.dram_tensor("s5_ident", [128, 128], F32, kind="ExternalInput").ap()
                self.gmask_in = nc.dram_tensor("s5_gmask", [128, 8], F32, kind="ExternalInput").ap()
                self.xspill = nc.dram_tensor("xspill", [128, NCH, S], F32).ap()
            if l == 1:
                wten("l1_w_in", 80)
                wten("l1_w_outk", 8)
                self.rb_in = nc.dram_tensor("rel_bias", [32, 48], F32, kind="ExternalInput").ap()
                self.oh_in = nc.dram_tensor("attn_oh", [32, 3 * 384], F32, kind="ExternalInput").ap()
                self.negm_in = nc.dram_tensor("attn_negm", [128, 384], F32, kind="ExternalInput").ap()
                self.rscr_h = nc.dram_tensor("rscr", [48, 128, 384], F32)
                self.rscr = self.rscr_h.ap()
        xout = nc.dram_tensor("outT", [D, S], F32, kind="ExternalOutput").ap()

        with contextlib.ExitStack() as st:
            def sb(name, shape, dt):
                return st.enter_context(nc.sbuf_tensor(name, shape, dt))

            self.xt_raw = sb("xt_raw", [128, 16448], F32)
            self.xt = self.xt_raw[:, 0:NCH * S].rearrange("p (c t) -> p c t", c=NCH)
            self.h = sb("h", [128, NCH, S], BF16)
            self.bigb = sb("bigb", [128, 16448], BF16)
            self.GEN_BYTES = 51456
            self.GEN = sb("gen", [128, self.GEN_BYTES // 4], F32)
            self.GENb = self.GEN.bitcast(BF16)
            self.NSTG = 2
            self.NWB = 6
            self.stg = sb("stg", [128, self.NSTG, 1024], F32)
            self.wb = sb("wb", [128, self.NWB, 1024], BF16)
            self.sm = sb("sm", [128, SMALL_N], F32)
            self.sc = sb("sc", [128, 8], F32)
            self.modA = sb("mod", [128, 24], F32)
            self.modB = sb("modB", [128, 24], F32)
            self.mod = self.modA
            self.modk = "mod"
            self.pre_ada = None
            self.ones = sb("ones", [128, 128], F32)
            self.epsc = sb("epsc", [128, 1], F32)
            self.dummy = sb("dummy_ep", [128, 1], F32)
            self.NNT = 2
            self.sq = sb("sq", [128, self.NNT, TT], BF16)
            self.onesb16 = sb("onesb16", [128, 128], BF16)
            self.rstd = sb("rstd", [128, TT], F32)
            self.ps = st.enter_context(nc.psum_tensor("ps", [128, 8, TT], F32))

            engines = ["pe", "act", "dve", "pool", "sp"]
            sems = {e: st.enter_context(nc.semaphore(f"sem_{e}")) for e in engines if e != "sp"}
            dsems = [st.enter_context(nc.semaphore(f"dsem{i}")) for i in range(Sched.NDS)]

            xr = xin.rearrange("(c p) t -> p c t", p=128)
            self.dma(self.sm[:, :], small_in, writes=["sm"])
            for tt in range(NTT):
                ts = slice(tt * TT, (tt + 1) * TT)
                self.dma(self.xt[:, :, ts], xr[:, :, ts], writes=[("x", kc, tt) for kc in range(NCH)])
            ones = self.ones
            self.op("pool", lambda e: e.memset(ones[:, :], 1.0), writes=["ones"])
            epsc = self.epsc
            self.op("pool", lambda e: e.memset(epsc[:, :], EPS), writes=["epsc"])
            o16 = self.onesb16
            self.op("pool", lambda e: e.memset(o16[:, :], 1.0), writes=["onesb16"])
            sc = self.sc
            cin = self.small("c")
            self.op("act", lambda e: e.activation(out=sc[:, :], in_=cin, func=AF.Silu), reads=["sm"], writes=["sc"])

            self.tables_done = False
            for li, l in enumerate(L):
                if li > 0:
                    self.barrier()
                self.next_ada = (L[li + 1], wd[f"l{L[li + 1]}_ada_w"]) if li + 1 < len(L) else None
                if l in (0, 3):
                    self.alloc_conv()
                    if li + 1 < len(L) and L[li + 1] == 1:
                        self.emit_attn_tables()
                    self.emit_conv_layer(l, wd[f"l{l}_ada_w"], wd[f"l{l}_w_in"], wd[f"l{l}_w_out"])
                if l == 1:
                    self.alloc_attn()
                    self.emit_attn_layer(wd["l1_ada_w"], wd["l1_w_in"], wd["l1_w_outk"])
                if l == 2:
                    self.alloc_s5()
                    self.emit_s5_layer(wd["l2_ada_w"], wd["l2_w_in"], wd["l2_w_glu"], wd["l2_w_out"], self.s5l1_in, self.s5l2_in)
            if self.do_final:
                self.emit_norm(self.small("final_g"), None, self.xt, "x")
            xo = xout.rearrange("(c p) t -> p c t", p=128)
            for kc in range(NCH):
                self.dma(xo[:, kc, :], self.xt[:, kc, :], reads=[("x", kc, tt) for tt in range(NTT)])

            run_engine = self.sch.emit(nc, engines, sems, dsems)
            with nc.Block() as block:
                @block.sync
                def _(e):
                    run_engine("sp", e)

                @block.tensor
                def _(e):
                    run_engine("pe", e)

                @block.scalar
                def _(e):
                    run_engine("act", e)

                @block.vector
                def _(e):
                    run_engine("dve", e)

                @block.gpsimd
                def _(e):
                    run_engine("pool", e)
        return nc


def prep_weights(inp, layers):
    w = {}
    for l in layers:
        w[f"l{l}_ada_w"] = tile_w(np.asarray(inp[f"l{l}_ada_w"], np.float32))
        if l in (0, 3):
            w[f"l{l}_w_in"] = tile_w(np.asarray(inp[f"l{l}_w_in"], np.float32))
            w[f"l{l}_w_out"] = tile_w(np.asarray(inp[f"l{l}_w_out"], np.float32))
        if l == 2:
            w["l2_w_in"] = tile_w(np.asarray(inp["l2_w_in"], np.float32))
            w["l2_w_glu"] = tile_w(np.asarray(inp["l2_w_glu"], np.float32))
            w["l2_w_out"] = tile_w(np.asarray(inp["l2_w_out"], np.float32))
            w.update(s5_host_inputs(inp))
        if l == 1:
            w["l1_w_in"] = tile_w(np.asarray(inp["l1_w_in"], np.float32))
            w["l1_w_outk"] = np.ascontiguousarray(np.asarray(inp["l1_w_out"], np.float32).reshape(8, 128, 1024))
            w["rel_bias"] = np.ascontiguousarray(np.asarray(inp["rel_bias"], np.float32))
            oh, negm = attn_static()
            w["attn_oh"] = oh
            w["attn_negm"] = negm
    return w


def run_layers(inp, x, layers, do_final):
    bld = Builder(layers, do_final)
    nc = bld.build()
    w = prep_weights(inp, layers)
    in_maps = []
    for b in range(NCORES):
        m = {"xT": np.ascontiguousarray(x[b].T), "small": pack_small(inp, b)}
        m.update(w)
        in_maps.append(m)
    res = run_bass_kernel_spmd(nc, in_maps, core_ids=list(range(NCORES)))
    out = np.stack([np.ascontiguousarray(r["outT"].T) for r in res.results], axis=0)
    return out.astype(np.float32)


def kernel(**inputs):
    inp = {k: np.asarray(v) for k, v in inputs.items()}
    x = np.asarray(inp["x"], np.float32)
    return run_layers(inp, x, [0, 1, 2, 3], True)
```

```python
import contextlib
import os
import numpy as np
import concourse.bass as bass
import concourse.mybir as mybir
from concourse.bass_utils import run_bass_kernel_spmd

F32 = mybir.dt.float32
BF16 = mybir.dt.bfloat16
AF = mybir.ActivationFunctionType
ALU = mybir.AluOpType

D = 1024
S = 2048
NCH = 8
TT = 512
NTT = S // TT
EPS = 1e-6
DEBUG = bool(int(os.environ.get("KDEBUG", "0")))
NCORES = int(os.environ.get("KCORES", "8"))


class _Op:
    __slots__ = ("eng", "fn", "deps", "dma", "needs_inc", "count", "dsem", "dval")

    def __init__(self, eng, fn, deps, dma):
        self.eng = eng
        self.fn = fn
        self.deps = deps
        self.dma = dma
        self.needs_inc = False
        self.count = 0
        self.dsem = -1
        self.dval = 0


class Sched:
    NDS = 24
    BAR = "__bar__"

    def __init__(self):
        self.ops = []
        self.last_w = {}
        self.readers = {}

    def add(self, eng, fn, reads=(), writes=(), dma=False, is_barrier=False):
        idx = len(self.ops)
        if not is_barrier:
            reads = list(reads) + [Sched.BAR]
        deps = set()
        for k in reads:
            j = self.last_w.get(k)
            if j is not None:
                deps.add(j)
        for k in writes:
            j = self.last_w.get(k)
            if j is not None:
                deps.add(j)
            for r in self.readers.get(k, ()):
                deps.add(r)
        self.ops.append(_Op(eng, fn, deps, dma))
        for k in reads:
            lst = self.readers.setdefault(k, [])
            if not dma:
                lst[:] = [r for r in lst if self.ops[r].dma or self.ops[r].eng != eng]
            lst.append(idx)
        for k in writes:
            self.last_w[k] = idx
            self.readers[k] = []
        return idx

    def emit(self, nc, engines, sems, dsems, final_waits=True):
        ops = self.ops
        for op in ops:
            for j in op.deps:
                pj = ops[j]
                if pj.dma:
                    continue
                if pj.eng != op.eng or op.dma or op.eng != "pe":
                    pj.needs_inc = True
        cnt = {e: 0 for e in engines}
        ndma = 0
        for op in ops:
            if op.dma:
                op.dsem = ndma % self.NDS
                op.dval = 16 * (ndma // self.NDS + 1)
                ndma += 1
            elif op.needs_inc:
                cnt[op.eng] += 1
                op.count = cnt[op.eng]
        per_eng = {e: [] for e in engines}
        for i, op in enumerate(ops):
            per_eng[op.eng].append(i)
        last_dma = [None] * self.NDS

        def run_engine(ename, e):
            waited = {}

            def wait(sem_key, sem, val):
                if waited.get(sem_key, 0) >= val:
                    return
                waited[sem_key] = val
                e.wait_ge(sem, val)

            for i in per_eng[ename]:
                op = ops[i]
                for j in sorted(op.deps):
                    pj = ops[j]
                    if pj.dma:
                        wait(("d", pj.dsem), dsems[pj.dsem], pj.dval)
                    elif pj.eng != op.eng or op.dma or op.eng != "pe":
                        wait(("c", pj.eng), sems[pj.eng], pj.count)
                if op.dma:
                    if op.dval > 16:
                        wait(("d", op.dsem), dsems[op.dsem], op.dval - 16)
                    ins = op.fn(e)
                    ins.then_inc(dsems[op.dsem], 16)
                else:
                    ins = op.fn(e)
                    if op.needs_inc:
                        ins.then_inc(sems[op.eng], 1)
            if ename == "sp" and final_waits:
                fin = {}
                for op in ops:
                    if op.dma:
                        fin[op.dsem] = max(fin.get(op.dsem, 0), op.dval)
                for s_, v_ in fin.items():
                    wait(("d", s_), dsems[s_], v_)

        return run_engine


def tile_w(W):
    K, N = W.shape
    A = K // 1024
    NT = N // 128
    t = W.reshape(A, 8, 128, NT, 128).transpose(0, 3, 2, 1, 4)
    return np.ascontiguousarray(t).reshape(A * NT, 128, 1024)


def colmajor(v, nch):
    return np.ascontiguousarray(np.asarray(v).reshape(nch, 128).T)


SMALL_COLS = {}

DILS = ((128, 1), (512, 4), (2048, 16))
NEGV = -30000.0


def _t5_bucket(dist):
    import math
    max_exact = 16
    d = np.maximum(dist, 0)
    ratio = np.log(np.maximum(d, 1) / max_exact) / math.log(2048 / max_exact)
    large = np.minimum(max_exact + (ratio * (32 - max_exact)).astype(np.int64), 31)
    return np.where(d < max_exact, d, large).astype(np.int32)


def attn_static():
    oh = np.zeros((32, 3, 384), np.float32)
    negm = np.full((128, 384), NEGV, np.float32)
    w = np.arange(127, 256)
    negm[:, 127:256] = 0.0
    for g, (win, dil) in enumerate(DILS):
        bk = _t5_bucket((w - 127) * dil)
        oh[bk, g, w] = 1.0
    return oh.reshape(32, 3 * 384), negm


def _small_layout():
    off = 0
    lay = {}

    def put(name, n):
        nonlocal off
        lay[name] = (off, n)
        off += n

    put("c", 8)
    put("final_g", 8)
    for l in range(4):
        put(f"l{l}_ada_b", 24)
    for l in (0, 3):
        put(f"l{l}_conv_w", 48)
        put(f"l{l}_conv_b", 16)
    put("l2_d_skip", 8)
    put("l2_b_glu", 8)
    return lay, off


SMALL_LAY, SMALL_N = _small_layout()


def pack_small(inp, b):
    sm = np.zeros((128, SMALL_N), np.float32)

    def put(name, arr):
        o, n = SMALL_LAY[name]
        assert arr.shape == (128, n), (name, arr.shape, n)
        sm[:, o:o + n] = arr

    put("c", colmajor(inp["c"][b], 8))
    put("final_g", colmajor(inp["final_g"], 8))
    for l in range(4):
        put(f"l{l}_ada_b", colmajor(inp[f"l{l}_ada_b"], 24))
    for l in (0, 3):
        cw = np.asarray(inp[f"l{l}_conv_w"])
        put(f"l{l}_conv_w", np.ascontiguousarray(cw.T.reshape(16, 128, 3).transpose(1, 0, 2)).reshape(128, 48))
        put(f"l{l}_conv_b", colmajor(inp[f"l{l}_conv_b"], 16))
    put("l2_d_skip", colmajor(inp["l2_d_skip"], 8))
    put("l2_b_glu", colmajor(inp["l2_b_glu"], 8))
    return sm


def s5_host_inputs(inp):
    f = lambda k: np.asarray(inp[k], np.float32)
    lre, lim, ldt = f("l2_lambda_re"), f("l2_lambda_im"), f("l2_log_dt")
    bre, bim, cre, cim = f("l2_b_re"), f("l2_b_im"), f("l2_c_re"), f("l2_c_im")
    l1 = np.zeros((32, 128, 67), np.float32)
    for q in range(32):
        for g2 in range(2):
            g = 2 * q + g2
            rows = slice(g2 * 64, g2 * 64 + 64)
            l1[q, rows, 0:16] = cre[g].T
            l1[q, rows, 16:32] = cim[g].T
            l1[q, rows, 32:48] = bre[g]
            l1[q, rows, 48:64] = bim[g]
            l1[q, rows, 64] = lre[g]
            l1[q, rows, 65] = lim[g]
            l1[q, rows, 66] = ldt[g]
    l2 = np.zeros((8, 128, 257), np.float32)
    for fc in range(8):
        for gl in range(8):
            g = fc * 8 + gl
            rows = slice(gl * 16, gl * 16 + 16)
            l2[fc, rows, 0:64] = bre[g].T
            l2[fc, rows, 64:128] = bim[g].T
            l2[fc, rows, 128:192] = lre[g][None, :]
            l2[fc, rows, 192:256] = lim[g][None, :]
            l2[fc, rows, 256] = ldt[g]
    ident = np.eye(128, dtype=np.float32)
    gmask = np.zeros((128, 8), np.float32)
    for gl in range(8):
        gmask[gl * 16:(gl + 1) * 16, gl] = 1.0
    return {"s5_l1": l1, "s5_l2": l2, "s5_ident": ident, "s5_gmask": gmask}


def _bc(ap, pos, n):
    dims = [list(d) for d in ap.ap]
    dims.insert(pos, [0, n])
    return bass.AP(ap.tensor, ap.offset, dims)

class Builder:
    def __init__(self, layers, do_final):
        self.layers = layers
        self.do_final = do_final
        self.nc = bass.Bass("TRN2", target_bir_lowering=False)
        self.sch = Sched()
        self.bank_ctr = 0
        self.stg_ctr = 0
        self.wb_ctr = 0

    def op(self, eng, fn, reads=(), writes=()):
        return self.sch.add(eng, fn, reads, writes)

    def dma(self, out, in_, reads=(), writes=()):
        return self.sch.add("sp", lambda e: e.dma_start(out=out, in_=in_), reads, writes, dma=True)

    def next_bank(self):
        b = self.bank_ctr % 8
        self.bank_ctr += 1
        return b

    def psb(self, b):
        return self.ps[:, b, :]

    def load_wtile(self, dram_ap_tile, cast_eng="pool", want_bf16=True):
        s = self.stg_ctr % self.NSTG
        self.stg_ctr += 1
        self.dma(self.stg[:, s, :], dram_ap_tile, writes=[("stg", s)])
        if not want_bf16:
            return ("stg", s), self.stg[:, s, :]
        w = self.wb_ctr % self.NWB
        self.wb_ctr += 1
        src = self.stg[:, s, :]
        dst = self.wb[:, w, :]
        if cast_eng == "act":
            self.op("act", lambda e: e.activation(out=dst, in_=src, func=AF.Copy), reads=[("stg", s)], writes=[("wb", w)])
        else:
            self.op(cast_eng, lambda e: e.tensor_copy(out=dst, in_=src), reads=[("stg", s)], writes=[("wb", w)])
        return ("wb", w), self.wb[:, w, :]

    def wstream(self, reqs, la):
        self._ws_reqs = list(reqs)
        self._ws_issued = 0
        self._ws_taken = 0
        self._ws_q = []
        self._ws_la = la

    def wget(self):
        la = self._ws_la
        for k_ in range(self._ws_taken, min(len(self._ws_reqs), self._ws_taken + 1 + la)):
            if not self._ws_reqs[k_][2]:
                la = min(la, self.NSTG - 1)
        want = min(len(self._ws_reqs), self._ws_taken + 1 + la)
        while self._ws_issued < want:
            ap, ceng, bf = self._ws_reqs[self._ws_issued]
            self._ws_q.append(self.load_wtile(ap, cast_eng=ceng, want_bf16=bf))
            self._ws_issued += 1
        self._ws_taken += 1
        return self._ws_q.pop(0)

    def small(self, name, c0=0, n=None):
        o, nn = SMALL_LAY[name]
        if n is None:
            n = nn - c0
        return self.sm[:, o + c0:o + c0 + n]

    def emit_ada_mm(self, ada_w):
        b = self.next_bank()
        for j in range(24):
            key, wt = self.wget()
            for kc in range(8):
                lhsT = wt[:, kc * 128:(kc + 1) * 128]
                rhs = self.sc[:, kc:kc + 1]
                o = self.ps[:, b, j:j + 1]
                self.op("pe", lambda e, o=o, lhsT=lhsT, rhs=rhs, kc=kc: e.matmul(o, lhsT, rhs, start=(kc == 0), stop=(kc == 7)),
                        reads=[key, "sc"], writes=[("ps", b)])
        return b

    def emit_ada_fin(self, l, b, mod, mk):
        ps = self.ps[:, b, 0:24]
        ab = self.small(f"l{l}_ada_b")
        self.op("dve", lambda e: e.tensor_tensor(out=mod[:, 0:24], in0=ps, in1=ab, op=ALU.add),
                reads=[("ps", b), "sm"], writes=[mk])
        self.op("dve", lambda e: e.tensor_scalar_add(mod[:, 8:16], mod[:, 8:16], 1.0), reads=[mk], writes=[mk])

    def emit_ada(self, l, ada_w):
        if self.pre_ada == l:
            self.mod, self.modk = self.modB, "modB"
            return
        self.mod, self.modk = self.modA, "mod"
        b = self.emit_ada_mm(ada_w)
        self.emit_ada_fin(l, b, self.modA, "mod")

    def emit_norm(self, scale_ap, bias_ap, out_tile, out_key, to_x=False):
        xt = self.xt
        for tt in range(NTT):
            ts = slice(tt * TT, (tt + 1) * TT)
            b = self.next_bank()
            for kc in range(NCH):
                sq = self.sq[:, kc % self.NNT, :]
                xin = xt[:, kc, ts]
                self.op("act", lambda e, sq=sq, xin=xin: e.activation(out=sq, in_=xin, func=AF.Square),
                        reads=[("x", kc, tt)], writes=[("sq", kc % self.NNT)])
                o = self.psb(b)
                self.op("pe", lambda e, o=o, sq=sq, kc=kc: e.matmul(o, self.onesb16[:, :], sq, start=(kc == 0), stop=(kc == NCH - 1)),
                        reads=[("sq", kc % self.NNT), "onesb16"], writes=[("ps", b)])
            rstd = self.rstd
            pb = self.psb(b)
            self.op("act", lambda e, pb=pb: e.activation(out=rstd[:, :], in_=pb, func=AF.Sqrt, scale=1.0 / D, bias=self.epsc[:, 0:1]),
                    reads=[("ps", b), "epsc"], writes=["rstd"])
            self.op("dve", lambda e: e.reciprocal(out=rstd[:, :], in_=rstd[:, :]), reads=["rstd"], writes=["rstd"])
            for kc in range(NCH):
                bt = self.next_bank()
                tmp = self.psb(bt)
                xin = xt[:, kc, ts]
                self.op("dve", lambda e, tmp=tmp, xin=xin: e.tensor_tensor(out=tmp, in0=xin, in1=rstd[:, :], op=ALU.mult),
                        reads=[("x", kc, tt), "rstd"], writes=[("ps", bt)])
                o = out_tile[:, kc, ts]
                sc_ = scale_ap[:, kc:kc + 1]
                if bias_ap is not None:
                    bi_ = bias_ap[:, kc:kc + 1]
                    self.op("act", lambda e, o=o, tmp=tmp, sc_=sc_, bi_=bi_: e.activation(out=o, in_=tmp, func=AF.Identity, scale=sc_, bias=bi_),
                            reads=[("ps", bt), self.modk, "sm"], writes=[(out_key, kc, tt)])
                else:
                    self.op("act", lambda e, o=o, tmp=tmp, sc_=sc_: e.activation(out=o, in_=tmp, func=AF.Identity, scale=sc_),
                            reads=[("ps", bt), self.modk, "sm"], writes=[(out_key, kc, tt)])

    def emit_wout(self, w_out_tiles, tile_idx_fn, y_tile, y_key, n_k, gate_ap, dc_order=None):
        xt = self.xt
        for dc in (dc_order if dc_order is not None else range(NCH)):
            key, wt = self.wget()
            for tt in range(NTT):
                ts = slice(tt * TT, (tt + 1) * TT)
                b = self.next_bank()
                for kc in range(n_k):
                    lhsT = wt[:, kc * 128:(kc + 1) * 128]
                    rhs = y_tile[:, kc, ts]
                    o = self.psb(b)
                    self.op("pe", lambda e, o=o, lhsT=lhsT, rhs=rhs, kc=kc: e.matmul(o, lhsT, rhs, start=(kc == 0), stop=(kc == n_k - 1)),
                            reads=[key, (y_key, kc, tt)], writes=[("ps", b)])
                xo = xt[:, dc, ts]
                pb = self.psb(b)
                g_ = gate_ap[:, dc:dc + 1]
                self.op("dve", lambda e, xo=xo, pb=pb, g_=g_: e.scalar_tensor_tensor(out=xo, in0=pb, scalar=g_, in1=xo, op0=ALU.mult, op1=ALU.add),
                        reads=[("ps", b), self.modk, ("x", dc, tt)], writes=[("x", dc, tt)])

    def emit_conv_layer(self, l, ada_w, w_in, w_out):
        reqs = [(ada_w[j], "pool", False) for j in range(24)] if self.pre_ada != l else []
        for half in range(2):
            for el in range(8):
                for q in range(4):
                    reqs.append((w_in[q * 16 + half * 8 + el], "pool" if q % 2 == 0 else "act", True))
            for dc in range(NCH):
                reqs.append((w_out[half * 8 + dc], "pool", True))
        self.wstream(reqs, 2)
        self.emit_ada(l, ada_w)
        mod = self.mod
        self.emit_norm(mod[:, 8:16], mod[:, 0:8], self.h, "h")
        cw = self.small(f"l{l}_conv_w")
        cb = self.small(f"l{l}_conv_b")
        h = self.h
        vb = self.vbuf
        for half in range(2):
            for el in range(8):
                e_ = half * 8 + el
                wk = []
                for q in range(4):
                    wk.append(self.wget())
                for tt in range(NTT):
                    ts = slice(tt * TT, (tt + 1) * TT)
                    banks = []
                    for q in range(4):
                        b = self.next_bank()
                        banks.append(b)
                        key, wt = wk[q]
                        for kc in range(NCH):
                            lhsT = wt[:, kc * 128:(kc + 1) * 128]
                            rhs = h[:, kc, ts]
                            o = self.psb(b)
                            self.op("pe", lambda e, o=o, lhsT=lhsT, rhs=rhs, kc=kc: e.matmul(o, lhsT, rhs, start=(kc == 0), stop=(kc == NCH - 1)),
                                    reads=[key, ("h", kc, tt)], writes=[("ps", b)])
                    bu, bgc, bgb, bz = banks
                    usb = self.usb
                    pu = self.psb(bu)
                    self.op("act", lambda e, pu=pu: e.activation(out=usb[:, :], in_=pu, func=AF.Copy), reads=[("ps", bu)], writes=["usb"])
                    vcur = vb[:, 2 + tt * TT: 2 + (tt + 1) * TT]
                    pgc = self.psb(bgc)
                    self.op("dve", lambda e, vcur=vcur, pgc=pgc: e.tensor_tensor(out=vcur, in0=pgc, in1=usb[:, :], op=ALU.mult),
                            reads=[("ps", bgc), "usb"], writes=[("v", tt)])
                    v0 = vb[:, tt * TT: (tt + 1) * TT]
                    v1 = vb[:, 1 + tt * TT: 1 + (tt + 1) * TT]
                    cacc = self.cacc
                    w0 = cw[:, e_ * 3 + 0:e_ * 3 + 1]
                    w1 = cw[:, e_ * 3 + 1:e_ * 3 + 2]
                    w2 = cw[:, e_ * 3 + 2:e_ * 3 + 3]
                    rv = [("v", tt)] + ([("v", tt - 1)] if tt > 0 else [])
                    self.op("pool", lambda e, v0=v0, w0=w0: e.tensor_scalar(out=cacc[:, :], in0=v0, scalar1=w0, scalar2=0.0, op0=ALU.mult, op1=ALU.add),
                            reads=rv + ["sm"], writes=["cacc"])
                    self.op("dve", lambda e, v1=v1, w1=w1: e.scalar_tensor_tensor(out=cacc[:, :], in0=v1, scalar=w1, in1=cacc[:, :], op0=ALU.mult, op1=ALU.add),
                            reads=rv + ["sm", "cacc"], writes=["cacc"])
                    self.op("dve", lambda e, vcur=vcur, w2=w2: e.scalar_tensor_tensor(out=cacc[:, :], in0=vcur, scalar=w2, in1=cacc[:, :], op0=ALU.mult, op1=ALU.add),
                            reads=rv + ["sm", "cacc"], writes=["cacc"])
                    pgb = self.psb(bgb)
                    cb_ = cb[:, e_:e_ + 1]
                    gt = self.gtmp
                    self.op("dve", lambda e, pgb=pgb, cb_=cb_: e.scalar_tensor_tensor(out=gt[:, :], in0=cacc[:, :], scalar=cb_, in1=pgb, op0=ALU.add, op1=ALU.mult),
                            reads=["cacc", "sm", ("ps", bgb)], writes=["gtmp"])
                    sz = self.sz
                    pz = self.psb(bz)
                    self.op("act", lambda e, pz=pz: e.activation(out=sz[:, :], in_=pz, func=AF.Silu), reads=[("ps", bz)], writes=["sz"])
                    yo = self.y[:, el, ts]
                    self.op("pool", lambda e, yo=yo: e.tensor_tensor(out=yo, in0=gt[:, :], in1=sz[:, :], op=ALU.mult),
                            reads=["gtmp", "sz"], writes=[("y", el, tt)])
            self.emit_wout(w_out, lambda dc, half=half: half * 8 + dc, self.y, "y", 8, mod[:, 16:24])


    def emit_attn_tables(self):
        self.tables_done = True
        if getattr(self, "in_attn", False):
            self.tsoh, self.trrep = self.soh, self.rrep
        else:
            self.tsoh = self.carve([128, 2, 384], F32)
            self.trrep = self.carve([128, 2, 384], F32)
        self.trb = self.carve([128, 48], F32)
        self.toh = self.carve([128, 3 * 384], F32)
        self.tnegm = self.carve([128, 384], F32)
        self.dma(self.trb[0:32, :], self.rb_in, writes=["rb"])
        self.dma(self.toh[0:32, :], self.oh_in, writes=["oh"])
        self.dma(self.tnegm[:, :], self.negm_in, writes=["negm"])
        for gh in range(48):
            g = gh // 16
            soh = self.tsoh[0:32, gh % 2, :]
            ohg = self.toh[0:32, g * 384:(g + 1) * 384]
            rbc = self.trb[0:32, gh:gh + 1]
            self.op("dve", lambda e, soh=soh, ohg=ohg, rbc=rbc: e.tensor_scalar(out=soh, in0=ohg, scalar1=rbc, scalar2=0.0, op0=ALU.mult, op1=ALU.add),
                    reads=["oh", "rb"], writes=[("soh", gh % 2)])
            b = self.next_bank()
            o = self.ps[:, b, 0:384]
            self.op("pe", lambda e, o=o, soh=soh: e.matmul(o, self.ones[0:32, :], soh, start=True, stop=True),
                    reads=[("soh", gh % 2), "ones"], writes=[("ps", b)])
            rr = self.trrep[:, gh % 2, :]
            self.op("dve", lambda e, rr=rr, o=o: e.tensor_tensor(out=rr, in0=o, in1=self.tnegm[:, :], op=ALU.add),
                    reads=[("ps", b), "negm"], writes=[("rrep", gh % 2)])
            self.dma(self.rscr[gh], rr, reads=[("rrep", gh % 2)], writes=[("rscr", gh)])


    def emit_attn_layer(self, ada_w, w_in, w_outk):
        l = 1
        reqs = [(ada_w[j], "pool", False) for j in range(24)]
        for hp in range(8):
            reqs.append((w_in[72 + hp], "act", True))
            for g in range(3):
                for t_ in range(3):
                    reqs.append((w_in[(g * 3 + t_) * 8 + hp], "pool" if t_ != 1 else "act", True))
                if g == 0 and hp > 0:
                    reqs.append((w_outk[hp - 1], "act", True))
        reqs.append((w_outk[7], "act", True))
        self.wstream(reqs, 4)
        self.emit_ada(l, ada_w)
        mod = self.mod
        self.emit_norm(mod[:, 8:16], mod[:, 0:8], self.h, "h")
        h = self.h
        xt = self.xt
        if not self.tables_done:
            self.in_attn = True
            self.emit_attn_tables()
        qT, kA, vA, ND, szb = self.qT, self.kA, self.vA, self.ND, self.szb
        STAGE = int(os.environ.get("KATT_STAGE", "9"))
        if STAGE == 0:
            return
        def emit_wout_hp(hp):
            yhp = self.yhp[:, hp % 2, :]
            key, wt = self.wget()
            for dc in range(NCH):
                for tt in range(NTT):
                    ts = slice(tt * TT, (tt + 1) * TT)
                    b = self.next_bank()
                    o = self.psb(b)
                    lhsT = wt[:, dc * 128:(dc + 1) * 128]
                    rhs = yhp[:, ts]
                    self.op("pe", lambda e, o=o, lhsT=lhsT, rhs=rhs: e.matmul(o, lhsT, rhs, start=True, stop=True),
                            reads=[key, ("yhp", hp % 2, tt)], writes=[("ps", b)])
                    xo = xt[:, dc, ts]
                    g_ = mod[:, 16 + dc:17 + dc]
                    self.op("dve", lambda e, xo=xo, o=o, g_=g_: e.scalar_tensor_tensor(out=xo, in0=o, scalar=g_, in1=xo, op0=ALU.mult, op1=ALU.add),
                            reads=[("ps", b), self.modk, ("x", dc, tt)], writes=[("x", dc, tt)])

        for hp in range(8 if STAGE >= 9 else 1):
            key, wt = self.wget()
            for tt in range(NTT):
                ts = slice(tt * TT, (tt + 1) * TT)
                b = self.next_bank()
                for kc in range(NCH):
                    o = self.psb(b)
                    lhsT = wt[:, kc * 128:(kc + 1) * 128]
                    rhs = h[:, kc, ts]
                    self.op("pe", lambda e, o=o, lhsT=lhsT, rhs=rhs, kc=kc: e.matmul(o, lhsT, rhs, start=(kc == 0), stop=(kc == NCH - 1)),
                            reads=[key, ("h", kc, tt)], writes=[("ps", b)])
                pb = self.psb(b)
                so = szb[:, ts]
                self.op("act", lambda e, so=so, pb=pb: e.activation(out=so, in_=pb, func=AF.Silu), reads=[("ps", b)], writes=[("szb", tt)])
            for g, (win, dil) in enumerate(DILS):
                tbi = (hp * 3 + g) % 2
                tabf = self.tab[:, tbi, :]
                tabk = ("tab", tbi)
                for hd in range(2):
                    gh = g * 16 + hp * 2 + hd
                    for blk, off in ((0, 255), (1, 127)):
                        src = bass.AP(self.rscr_h, gh * 128 * 384 + off, [[383, 128], [1, 128]])
                        dstt = tabf[:, (hd * 2 + blk) * 128:(hd * 2 + blk + 1) * 128]
                        self.dma(dstt, src, reads=[("rscr", gh)], writes=[tabk])
                self.op("act", lambda e, tabf=tabf: e.activation(out=tabf, in_=tabf, func=AF.Exp), reads=[tabk], writes=[tabk])
                for t_, dkey, scl in ((0, "qT", 0.125), (1, "kT", 1.0)):
                    key, wt = self.wget()
                    for tt in range(NTT):
                        ts = slice(tt * TT, (tt + 1) * TT)
                        b = self.next_bank()
                        for kc in range(NCH):
                            o = self.psb(b)
                            lhsT = wt[:, kc * 128:(kc + 1) * 128]
                            rhs = h[:, kc, ts]
                            self.op("pe", lambda e, o=o, lhsT=lhsT, rhs=rhs, kc=kc: e.matmul(o, lhsT, rhs, start=(kc == 0), stop=(kc == NCH - 1)),
                                    reads=[key, ("h", kc, tt)], writes=[("ps", b)])
                        pb = self.psb(b)
                        if t_ == 0:
                            do = qT[:, ts]
                            self.op("act", lambda e, do=do, pb=pb, scl=scl: e.activation(out=do, in_=pb, func=AF.Copy, scale=scl),
                                    reads=[("ps", b)], writes=[dkey])
                        else:
                            for hd in range(2):
                                do = kA[hd * 64:(hd + 1) * 64, hd, ts]
                                pi = self.ps[hd * 64:(hd + 1) * 64, b, :]
                                self.op("act", lambda e, do=do, pi=pi: e.activation(out=do, in_=pi, func=AF.Copy),
                                        reads=[("ps", b)], writes=[dkey])
                nb = (S // dil) // 128
                key, wt = self.wget()

                def tok(ti, dil=dil, nb=nb):
                    r, n = ti // nb, ti % nb
                    st_ = r + dil * 128 * n
                    return slice(st_, st_ + dil * 127 + 1, dil)

                for tb in range(4):
                    b = self.next_bank()
                    for tj in range(4):
                        ti = tb * 4 + tj
                        for kc in range(NCH):
                            o = self.ps[:, b, tj * 128:(tj + 1) * 128]
                            lhsT = h[:, kc, tok(ti)]
                            rhs = wt[:, kc * 128:(kc + 1) * 128]
                            self.op("pe", lambda e, o=o, lhsT=lhsT, rhs=rhs, kc=kc: e.matmul(o, lhsT, rhs, start=(kc == 0), stop=(kc == NCH - 1)),
                                    reads=[key] + [("h", kc, t4) for t4 in range(NTT)], writes=[("ps", b)])
                    pb4 = self.psb(b).rearrange("p (a q) -> p a q", a=4)
                    for hd in range(2):
                        vo = vA[:, hd, tb * 512:(tb + 1) * 512].rearrange("p (a q) -> p a q", a=4)[:, :, hd * 64:(hd + 1) * 64]
                        pi = pb4[:, :, hd * 64:(hd + 1) * 64]
                        self.op("act", lambda e, vo=vo, pi=pi: e.activation(out=vo, in_=pi, func=AF.Copy), reads=[("ps", b)], writes=[("vtm", tb)])
                if g == 0 and hp > 0:
                    emit_wout_hp(hp - 1)
                EW = [("rrep", 0), ("rrep", 1), ("soh", 0), ("soh", 1)]
                NB3 = 4

                def emit_scores(ti):
                    n = ti % nb
                    b = self.next_bank()
                    blks = (0, 1) if n > 0 else (1,)
                    for hd in range(2):
                        for blk in blks:
                            kti = ti if blk == 1 else ti - 1
                            o = self.ps[:, b, (hd * 2 + blk) * 128:(hd * 2 + blk + 1) * 128]
                            lhsT = kA[:, hd, tok(kti)]
                            rhs = qT[:, tok(ti)]
                            self.op("pe", lambda e, o=o, lhsT=lhsT, rhs=rhs: e.matmul(o, lhsT, rhs, start=True, stop=True),
                                    reads=["qT", "kT"], writes=[("ps", b)])
                    return b

                def emit_softmax(ti, b):
                    n = ti % nb
                    bu = ti % NB3
                    E = self.E[:, bu, :]
                    pT = self.pT[:, bu, :]
                    if n > 0:
                        pb = self.psb(b)
                        self.op("act", lambda e, E=E, pb=pb: e.activation(out=E, in_=pb, func=AF.Exp),
                                reads=[("ps", b)], writes=[("E", bu)] + EW)
                        self.op("dve", lambda e, E=E, pT=pT, tabf=tabf: e.tensor_tensor(out=pT[:, 0:256], in0=E[:, 0:256], in1=tabf[:, 0:256], op=ALU.mult),
                                reads=[("E", bu), tabk], writes=[("pT", bu, 0)])
                        self.op("pool", lambda e, E=E, pT=pT, tabf=tabf: e.tensor_tensor(out=pT[:, 256:512], in0=E[:, 256:512], in1=tabf[:, 256:512], op=ALU.mult),
                                reads=[("E", bu), tabk], writes=[("pT", bu, 1)])
                    else:
                        for hd in range(2):
                            cs = slice((hd * 2 + 1) * 128, (hd * 2 + 2) * 128)
                            pb = self.ps[:, b, cs]
                            eo = E[:, cs]
                            self.op("act", lambda e, eo=eo, pb=pb: e.activation(out=eo, in_=pb, func=AF.Exp),
                                    reads=[("ps", b)], writes=[("E", bu)] + EW)
                            po = pT[:, cs]
                            tb_ = tabf[:, cs]
                            self.op("dve" if hd == 0 else "pool", lambda e, po=po, eo=eo, tb_=tb_: e.tensor_tensor(out=po, in0=eo, in1=tb_, op=ALU.mult),
                                    reads=[("E", bu), tabk], writes=[("pT", bu, hd)])

                def emit_pv(ti):
                    n = ti % nb
                    bu = ti % NB3
                    pT = self.pT[:, bu, :]
                    blks = (0, 1) if n > 0 else (1,)
                    b2 = self.next_bank()
                    for which in range(2):
                        o = self.ps[:, b2, which * 128:(which + 1) * 128]
                        combos = [(hd, blk) for hd in range(2) for blk in blks]
                        for ci, (hd, blk) in enumerate(combos):
                            kti = ti if blk == 1 else ti - 1
                            if which == 0:
                                lhsT = vA[:, hd, kti * 128:(kti + 1) * 128]
                                rk = [("vtm", kti // 4)]
                            else:
                                lhsT = self.onesA[:, hd, :]
                                rk = ["onesb"]
                            rhs = pT[:, (hd * 2 + blk) * 128:(hd * 2 + blk + 1) * 128]
                            self.op("pe", lambda e, o=o, lhsT=lhsT, rhs=rhs, ci=ci, ncb=len(combos): e.matmul(o, lhsT, rhs, start=(ci == 0), stop=(ci == ncb - 1)),
                                    reads=rk + [("pT", bu, 0), ("pT", bu, 1)], writes=[("ps", b2)])
                    ndo = ND[:, :, tok(ti)]
                    pin = self.ps[:, b2, 0:256].rearrange("p (a q) -> p a q", a=2)
                    if g == 0:
                        self.op("dve", lambda e, ndo=ndo, pin=pin: e.tensor_copy(out=ndo, in_=pin), reads=[("ps", b2)], writes=["ND"])
                    else:
                        self.op("dve", lambda e, ndo=ndo, pin=pin: e.tensor_tensor(out=ndo, in0=pin, in1=ndo, op=ALU.add),
                                reads=[("ps", b2), "ND"], writes=["ND"])

                sbank = {}
                for t0 in range(3):
                    sbank[t0] = emit_scores(t0)
                emit_softmax(0, sbank[0])
                emit_softmax(1, sbank[1])
                for ti in range(16):
                    if ti + 3 < 16:
                        sbank[ti + 3] = emit_scores(ti + 3)
                    if ti + 2 < 16:
                        emit_softmax(ti + 2, sbank[ti + 2])
                    emit_pv(ti)
            for tt in range(NTT):
                ts = slice(tt * TT, (tt + 1) * TT)
                rc = self.ssb[:, tt % 2, :]
                self.op("dve", lambda e, rc=rc, ts=ts: e.reciprocal(out=rc, in_=ND[:, 1, ts]), reads=["ND"], writes=[("ssb", tt % 2)])
                self.op("pool", lambda e, rc=rc, ts=ts: e.tensor_tensor(out=rc, in0=rc, in1=ND[:, 0, ts], op=ALU.mult), reads=["ND", ("ssb", tt % 2)], writes=[("ssb", tt % 2)])
                yo = self.yhp[:, hp % 2, ts]
                so = szb[:, ts]
                self.op("pool", lambda e, rc=rc, yo=yo, so=so: e.tensor_tensor(out=yo, in0=rc, in1=so, op=ALU.mult),
                        reads=[("ssb", tt % 2), ("szb", tt)], writes=[("yhp", hp % 2, tt)])
            pending_wout = hp
            if hp == 7:
                emit_wout_hp(7)

    def tt(self, o, a, b, op, eng="dve", extra=()):
        self.op(eng, lambda e: e.tensor_tensor(out=o[0], in0=a[0], in1=b[0], op=op), reads=[a[1], b[1]] + list(extra), writes=[o[1]])

    def tsc(self, o, a, s1, s2, op0, op1, eng="dve", extra=()):
        rk = [a[1]] + list(extra)
        s1v, s2v = s1, s2
        if isinstance(s1, tuple):
            rk.append(s1[1]); s1v = s1[0]
        if isinstance(s2, tuple):
            rk.append(s2[1]); s2v = s2[0]
        self.op(eng, lambda e: e.tensor_scalar(out=o[0], in0=a[0], scalar1=s1v, scalar2=s2v, op0=op0, op1=op1), reads=rk, writes=[o[1]])

    def stt(self, o, a, sc, b, op0, op1, eng="dve", extra=()):
        rk = [a[1], b[1]] + list(extra)
        scv = sc
        if isinstance(sc, tuple):
            rk.append(sc[1]); scv = sc[0]
        self.op(eng, lambda e: e.scalar_tensor_tensor(out=o[0], in0=a[0], scalar=scv, in1=b[0], op0=op0, op1=op1), reads=rk, writes=[o[1]])

    def actf(self, o, a, func, scale=1.0, bias=None, extra=()):
        rk = [a[1]] + list(extra)
        if bias is not None:
            rk.append(bias[1])
            bv = bias[0]
            self.op("act", lambda e: e.activation(out=o[0], in_=a[0], func=func, scale=scale, bias=bv), reads=rk, writes=[o[1]])
        else:
            self.op("act", lambda e: e.activation(out=o[0], in_=a[0], func=func, scale=scale), reads=rk, writes=[o[1]])

    def epoch(self, region, kill=()):
        d = self.dummy
        self.op("pool", lambda e: e.memset(d[:, 0:1], 0.0), writes=[("ep", region)] + list(kill))

    def emit_zoh(self, lre, lim, ldt, scr, tag):
        sl = lambda i: (scr[:, i, :], (tag, i))
        dt, x, mag, th, sn, cs, t1, t2, ar, ai, inv, am1, qr, qi = [sl(i) for i in range(14)]
        hp_ = (self.halfpi[:, 0:1], "halfpi")
        self.actf(dt, ldt, AF.Exp)
        self.tt(x, lre, dt, ALU.mult)
        self.actf(mag, x, AF.Exp)
        self.tt(th, lim, dt, ALU.mult)
        self.actf(sn, th, AF.Sin, scale=1.0 / 32.0)
        self.actf(cs, th, AF.Sin, scale=1.0 / 32.0, bias=hp_)
        for _ in range(5):
            self.tt(t1, cs, cs, ALU.mult)
            self.tt(t2, sn, sn, ALU.mult)
            self.stt(sn, cs, 2.0, sn, ALU.mult, ALU.mult)
            self.tt(cs, t1, t2, ALU.subtract)
        self.tt(ar, mag, cs, ALU.mult)
        self.tt(ai, mag, sn, ALU.mult)
        self.tt(t1, lre, lre, ALU.mult)
        self.tt(t2, lim, lim, ALU.mult)
        self.tt(t1, t1, t2, ALU.add)
        self.op("dve", lambda e: e.reciprocal(out=inv[0], in_=t1[0]), reads=[t1[1]], writes=[inv[1]])
        self.tsc(am1, ar, -1.0, None, ALU.add, ALU.bypass)
        self.tt(t1, am1, lre, ALU.mult)
        self.tt(t2, ai, lim, ALU.mult)
        self.tt(t1, t1, t2, ALU.add)
        self.tt(qr, t1, inv, ALU.mult)
        self.tt(t1, ai, lre, ALU.mult)
        self.tt(t2, am1, lim, ALU.mult)
        self.tt(t1, t1, t2, ALU.subtract)
        self.tt(qi, t1, inv, ALU.mult)
        return ar, ai, qr, qi, t1, t2

    def emit_powers(self, pw, tag, ar, ai, t1, t2, nmax):
        P = lambda m, ri: (pw[:, m, ri, :], (tag, m, ri))
        p0r, p0i = P(0, 0), P(0, 1)
        self.op("pool", lambda e: e.memset(p0r[0], 1.0), writes=[p0r[1]])
        self.op("pool", lambda e: e.memset(p0i[0], 0.0), writes=[p0i[1]])
        for m in range(1, nmax + 1):
            self.tt(t1, P(m - 1, 0), ar, ALU.mult)
            self.tt(t2, P(m - 1, 1), ai, ALU.mult)
            self.tt(P(m, 0), t1, t2, ALU.subtract)
            self.tt(t1, P(m - 1, 0), ai, ALU.mult)
            self.tt(t2, P(m - 1, 1), ar, ALU.mult)
            self.tt(P(m, 1), t1, t2, ALU.add)
        return P

    def emit_s5_layer(self, ada_w, w_in, w_glu, w_out, l1_in, l2_in):
        l = 2
        reqs = [(ada_w[j], "pool", False) for j in range(24)]
        reqs += [(w_in[fc], "pool", True) for fc in range(8)]
        if self.next_ada is not None:
            reqs += [(self.next_ada[1][j], "pool", False) for j in range(24)]
        if int(os.environ.get("KS5_STAGE", "9")) >= 3:
            reqs += [(w_in[fc], "pool", True) for fc in range(8)]
        if int(os.environ.get("KS5_STAGE", "9")) >= 4:
            for dc in range(NCH):
                reqs += [(w_glu[dc], "pool", True), (w_in[8 + dc], "act", True)]
        S5_DC = [4, 5, 6, 7, 0, 1, 2, 3]
        reqs += [(w_out[dc], "pool", True) for dc in S5_DC]
        self.wstream(reqs, 3)
        self.emit_ada(l, ada_w)
        mod = self.mod
        self.emit_norm(mod[:, 8:16], mod[:, 0:8], self.h, "h")
        h = self.h
        xkeys = [("x", kc, tt) for kc in range(NCH) for tt in range(NTT)]
        for kc in range(NCH):
            self.dma(self.xspill[:, kc, :], self.xt[:, kc, :], reads=[("x", kc, tt) for tt in range(NTT)], writes=[("xspill", kc)])
        self.epoch("bigf", kill=xkeys)
        EPF = ("ep", "bigf")
        XS = self.bigf[:, :].rearrange("p (c r q) -> p c r q", r=2, q=32)
        gbv = self.bigf.bitcast(BF16)[:, 0:NCH * S].rearrange("p (c t) -> p c t", c=NCH)
        Sbf = self.bigb[:, 0:256 * 64].rearrange("p (r q c) -> p r q c", r=2, q=32)
        y2b = self.bigb[:, 0:NCH * S].rearrange("p (c t) -> p c t", c=NCH)
        EPB = ("ep", "bigb")

        lam1 = self.lam1
        self.dma(lam1[:, :, :], l1_in[:, :, 64:67].rearrange("q p c -> p q c"), writes=["lam1"])
        L1 = lambda i: (lam1[:, :, i], "lam1")
        ar1, ai1, qr1, qi1, t1a, t2a = self.emit_zoh(L1(0), L1(1), L1(2), self.scr1, "scr1")
        q1 = self.q1
        self.op("dve", lambda e: e.tensor_copy(out=q1[:, 0, :], in_=qr1[0]), reads=[qr1[1]], writes=["q1"])
        self.op("dve", lambda e: e.tensor_copy(out=q1[:, 1, :], in_=qi1[0]), reads=[qi1[1]], writes=["q1"])
        P1 = self.emit_powers(self.pw1, "pw1", ar1, ai1, t1a, t2a, 8)
        pw1keys = [("pw1", m, ri) for m in range(9) for ri in range(2)]
        scr1keys = [("scr1", i) for i in range(16)]
        AA, AI = self.AA, self.AI
        for r_ in range(2):
            self.op("dve", lambda e, r_=r_: e.tensor_copy(out=AA[:, r_, :], in_=self.pw1[:, 8, 0, :]), reads=[("pw1", 8, 0)], writes=["AA"])
            sgn = -1.0 if r_ == 0 else 1.0
            self.op("dve", lambda e, r_=r_, sgn=sgn: e.tensor_scalar(out=AI[:, r_, :], in0=self.pw1[:, 8, 1, :], scalar1=sgn, scalar2=None, op0=ALU.mult), reads=[("pw1", 8, 1)], writes=["AI"])

        MC = self.MC
        MGb, MCC, BDf, BDall, MB = self.MGb, self.MCC, self.BDf, self.BDall, self.MB
        identb = self.identb
        self.op("act", lambda e: e.activation(out=identb[:, :], in_=self.ident[:, :], func=AF.Copy), reads=["ident"], writes=["identb"])
        Hs = self.Hs
        PT = self.PT
        self.op("pool", lambda e: e.memset(XS[:, 0, :, :], 0.0), reads=[EPF], writes=[("XS", 0)])
        self.op("pool", lambda e: e.memset(MGb[:, :, :, :, :], 0.0), writes=[("MG", 0), ("MG", 1)])
        self.op("pool", lambda e: e.memset(MCC[:, :, :, :], 0.0), writes=[("MCC", 0), ("MCC", 1)])
        hs = lambda i: (Hs[:, i, :, :], ("Hs", i))
        hs_dead = [("Hs", i) for i in range(6)]

        def pair_tile(q):
            pt = PT[:, q % 2, :]
            ptk = ("PT", q % 2)
            self.dma(pt, l1_in[q][:, 0:64], writes=[ptk])
            return pt, ptk, [(pt[:, a:a + 16], ptk) for a in (0, 16, 32, 48)]

        bc8 = lambda a: (_bc(a[0], 1, 8), a[1])
        Apw = lambda q, m0, ri: (_bc(self.pw1[:, m0:m0 + 8, ri, q], 2, 16), ("pw1", 8, 0))
        self.op("dve", lambda e: e.memset(Hs[:, 0, 0, 0:1], 0.0), writes=scr1keys + [("Hs", i) for i in range(6)])

        ub = self.ub16
        ub3 = self.ub16.rearrange("p (i c) -> p i c", i=8)

        def uproj(fc):
            key, wt = self.wget()
            for tt in range(NTT):
                ts = slice(tt * TT, (tt + 1) * TT)
                b = self.next_bank()
                for kc in range(NCH):
                    o = self.psb(b)
                    lhsT = wt[:, kc * 128:(kc + 1) * 128]
                    rhs = h[:, kc, ts]
                    self.op("pe", lambda e, o=o, lhsT=lhsT, rhs=rhs, kc=kc: e.matmul(o, lhsT, rhs, start=(kc == 0), stop=(kc == NCH - 1)),
                            reads=[key, ("h", kc, tt)], writes=[("ps", b)])
                pb = self.psb(b)
                uo = ub3[:, :, tt * 64:(tt + 1) * 64]
                pin_u = pb.rearrange("p (c i) -> p i c", i=8)
                self.op("act", lambda e, uo=uo, pin_u=pin_u: e.activation(out=uo, in_=pin_u, func=AF.Copy), reads=[("ps", b)], writes=["ub"])

        def stageA(q):
            fc, jp = q // 4, q % 4
            pt, ptk, (cre, cim, bre, bim) = pair_tile(q)
            ex = pw1keys
            qrs = (q1[:, 0, q:q + 1], "q1")
            qis = (q1[:, 1, q:q + 1], "q1")
            bbr = (self.bb1[:, 0, :], ("bb1", 0))
            bbi = (self.bb1[:, 1, :], ("bb1", 1))
            tb = (self.bb1[:, 2, :], ("bb1", 2))
            self.tsc(tb, bim, qis, None, ALU.mult, ALU.bypass)
            self.stt(bbr, bre, qrs, tb, ALU.mult, ALU.subtract)
            self.tsc(tb, bre, qis, None, ALU.mult, ALU.bypass)
            self.stt(bbi, bim, qrs, tb, ALU.mult, ALU.add)
            so = (q % 2) * 2
            self.tt(hs(0), bc8(bbr), Apw(q, 0, 0), ALU.mult, extra=ex)
            self.tt(hs(1), bc8(bbi), Apw(q, 0, 1), ALU.mult, extra=ex)
            self.tt(hs(2 + so), hs(0), hs(1), ALU.subtract)
            self.tt(hs(0), bc8(bbr), Apw(q, 0, 1), ALU.mult, extra=ex)
            self.tt(hs(1), bc8(bbi), Apw(q, 0, 0), ALU.mult, extra=ex)
            self.tt(hs(3 + so), hs(0), hs(1), ALU.add)
            mb_ = q % 2
            mgk, mck = ("MG", mb_), ("MCC", mb_)
            if q > 1:
                pj = (jp - 2) % 4
                for g2 in range(2):
                    rows = slice(g2 * 64, (g2 + 1) * 64)
                    cols = slice((2 * pj + g2) * 16, (2 * pj + g2 + 1) * 16)
                    self.op("pool", lambda e, rows=rows, cols=cols, mb_=mb_: e.memset(MGb[rows, mb_, :, :, cols], 0.0), reads=[mgk], writes=[mgk])
                    self.op("pool", lambda e, rows=rows, cols=cols, mb_=mb_: e.memset(MCC[rows, mb_, :, cols], 0.0), reads=[mck], writes=[mck])
            for g2 in range(2):
                rows = slice(g2 * 64, (g2 + 1) * 64)
                cols = slice((2 * jp + g2) * 16, (2 * jp + g2 + 1) * 16)
                for ri in range(2):
                    o = MGb[rows, mb_, :, ri, cols]
                    src = Hs[rows, 2 + so + ri, :, :]
                    self.op("act", lambda e, o=o, src=src: e.activation(out=o, in_=src, func=AF.Copy), reads=[("Hs", 2 + so + ri), mgk], writes=[mgk])
                o2 = MCC[rows, mb_, 0, cols]
                s2 = pt[rows, 0:16]
                self.op("act", lambda e, o2=o2, s2=s2: e.activation(out=o2, in_=s2, func=AF.Copy), reads=[ptk, mck], writes=[mck])
                o3 = MCC[rows, mb_, 1, cols]
                s3 = pt[rows, 16:32]
                self.op("act", lambda e, o3=o3, s3=s3: e.activation(out=o3, in_=s3, func=AF.Copy, scale=-1.0), reads=[ptk, mck], writes=[mck])

        def stageB(q):
            fc, jp = q // 4, q % 4
            mb_ = q % 2
            mgk, mck = ("MG", mb_), ("MCC", mb_)
            for hb in range(2):
                b = self.next_bank()
                for t4 in range(4):
                    tau = hb * 4 + t4
                    o = self.ps[:, b, t4 * 128:(t4 + 1) * 128]
                    for ri in range(2):
                        lhsT = MGb[:, mb_, tau, ri, :]
                        rhs = MCC[:, mb_, ri, :]
                        self.op("pe", lambda e, o=o, lhsT=lhsT, rhs=rhs, ri=ri: e.matmul(o, lhsT, rhs, start=(ri == 0), stop=(ri == 1)),
                                reads=[mgk, mck], writes=[("ps", b)])
                pb = self.psb(b)
                bo = BDf[:, hb * 512:(hb + 1) * 512]
                if jp == 0:
                    self.op("dve", lambda e, bo=bo, pb=pb: e.tensor_copy(out=bo, in_=pb), reads=[("ps", b)], writes=[("BDf", hb)])
                else:
                    self.op("dve", lambda e, bo=bo, pb=pb: e.tensor_tensor(out=bo, in0=pb, in1=bo, op=ALU.add), reads=[("ps", b), ("BDf", hb)], writes=[("BDf", hb)])
            for k4 in range(4):
                b = self.next_bank()
                for sl_, (i, ri) in enumerate([(2 * k4, 0), (2 * k4, 1), (2 * k4 + 1, 0), (2 * k4 + 1, 1)]):
                    o = self.ps[:, b, sl_ * 128:(sl_ + 1) * 128]
                    src = MGb[:, mb_, 7 - i, ri, :]
                    self.op("pe", lambda e, o=o, src=src: e.matmul(o, src, identb[:, :], start=True, stop=True),
                            reads=[mgk, "identb"], writes=[("ps", b)])
                pb = self.psb(b)
                mo = MB[:, mb_, 2 * k4:2 * k4 + 2, :, :]
                pin = pb.rearrange("p (a r c) -> p a r c", a=2, r=2)
                self.op("act", lambda e, mo=mo, pin=pin: e.activation(out=mo, in_=pin, func=AF.Copy), reads=[("ps", b)], writes=[("MB", mb_)])
            if jp == 3:
                dsk = self.small("l2_d_skip")[:, fc:fc + 1]
                self.op("dve", lambda e, dsk=dsk: e.scalar_tensor_tensor(out=BDf[:, 0:128], in0=self.ident[:, :], scalar=dsk, in1=BDf[:, 0:128], op0=ALU.mult, op1=ALU.add),
                        reads=["ident", "sm", ("BDf", 0)], writes=[("BDf", 0)])
                bdo = BDall[:, fc, :]
                self.op("act", lambda e, bdo=bdo: e.activation(out=bdo, in_=BDf[:, :], func=AF.Copy), reads=[("BDf", 0), ("BDf", 1)], writes=[("BD", fc)])

        def stageC(q):
            mb_ = q % 2
            b = self.next_bank()
            for ri in range(2):
                o = self.ps[:, b, ri * 256:(ri + 1) * 256]
                for i in range(8):
                    lhsT = MB[:, mb_, i, ri, :]
                    rhs = ub3[:, i, :]
                    self.op("pe", lambda e, o=o, lhsT=lhsT, rhs=rhs, i=i: e.matmul(o, lhsT, rhs, start=(i == 0), stop=(i == 7)),
                            reads=[("MB", mb_), "ub"], writes=[("ps", b)])
            for ri in range(2):
                pin = self.ps[:, b, ri * 256:(ri + 1) * 256]
                xo = XS[:, 1:257, ri, q]
                self.op("act", lambda e, xo=xo, pin=pin: e.activation(out=xo, in_=pin, func=AF.Copy), reads=[("ps", b), EPF], writes=[("XSq", q)])

        stageA(0)
        stageA(1)
        stageB(0)
        for q in range(32):
            if q % 4 == 0:
                uproj(q // 4)
            if q + 2 < 32:
                stageA(q + 2)
            if q + 1 < 32:
                stageB(q + 1)
            stageC(q)


        xsq = [("XSq", q) for q in range(32)]
        P1t, P2t = self.P1t, self.P2t
        S5ST = int(os.environ.get("KS5_STAGE", "9"))
        ada_bank = None
        if self.next_ada is not None:
            ada_bank = self.emit_ada_mm(self.next_ada[1])
        for c in range(1, 257 if S5ST >= 2 else 1):
            prev = XS[:, c - 1, :, :]
            pv = XS[:, c - 1, 1, :]
            prev_sw = bass.AP(pv.tensor, pv.offset, [list(pv.ap[0]), [-32, 2], [1, 32]])
            cur = XS[:, c, :, :]
            rk = [("XS", c - 1), "AA", "AI", EPF] + (xsq if c == 1 else [])
            self.op("dve", lambda e, prev=prev: e.tensor_tensor(out=P1t[:, :, :], in0=AA[:, :, :], in1=prev, op=ALU.mult), reads=rk, writes=["P1t"])
            self.op("dve", lambda e, prev_sw=prev_sw: e.tensor_tensor(out=P2t[:, :, :], in0=AI[:, :, :], in1=prev_sw, op=ALU.mult), reads=rk, writes=["P2t"])
            wk = [("XS", c)]
            rk2 = xsq if c == 1 else []
            self.op("dve", lambda e, cur=cur: e.tensor_tensor(out=cur, in0=cur, in1=P1t[:, :, :], op=ALU.add), reads=["P1t", EPF] + rk2 + wk, writes=wk)
            self.op("dve", lambda e, cur=cur: e.tensor_tensor(out=cur, in0=cur, in1=P2t[:, :, :], op=ALU.add), reads=["P2t", EPF] + wk, writes=wk)
        if ada_bank is not None:
            self.emit_ada_fin(self.next_ada[0], ada_bank, self.modB, "modB")
            self.pre_ada = self.next_ada[0]
        self.epoch("bigb")
        for part in range(4):
            cs_ = slice(part * 64, (part + 1) * 64)
            so = Sbf[:, :, :, cs_]
            si = XS[:, cs_, :, :].rearrange("p c r q -> p r q c")
            self.op("pool", lambda e, so=so, si=si: e.tensor_copy(out=so, in_=si), reads=[("XS", c) for c in range(part * 64, part * 64 + 64)] + [EPF, EPB], writes=[("Sbf", part)])
        sbfk = [("Sbf", p_) for p_ in range(4)]
        self.epoch("bigf", kill=[("XS", c) for c in range(257)])
        for kc in range(4, NCH):
            self.dma(self.xt[:, kc, :], self.xspill[:, kc, :], reads=[EPF, ("xspill", kc)], writes=[("x", kc, tt) for tt in range(NTT)])
        self.op("pool", lambda e: e.memset(MC[:, :, :, :, :], 0.0), writes=["MC", ("MG", 0), ("MG", 1), ("MB", 0), ("MB", 1)])

        for fc in range(8 if S5ST >= 3 else 0):
            for jp in range(4):
                q = fc * 4 + jp
                pt, ptk, (cre, cim, bre, bim) = pair_tile(q)
                ex = pw1keys
                so = (jp % 2) * 2
                self.tt(hs(0), bc8(cre), Apw(q, 1, 0), ALU.mult, extra=ex)
                self.tt(hs(1), bc8(cim), Apw(q, 1, 1), ALU.mult, extra=ex)
                self.tt(hs(2 + so), hs(0), hs(1), ALU.subtract)
                self.tt(hs(0), bc8(cre), Apw(q, 1, 1), ALU.mult, extra=ex)
                self.tt(hs(1), bc8(cim), Apw(q, 1, 0), ALU.mult, extra=ex)
                self.stt(hs(3 + so), hs(0), -1.0, hs(1), ALU.mult, ALU.subtract)
                for g2 in range(2):
                    rows = slice(g2 * 64, (g2 + 1) * 64)
                    cols = slice((2 * jp + g2) * 16, (2 * jp + g2 + 1) * 16)
                    for ri in range(2):
                        o = MC[rows, jp, :, ri, cols]
                        src = Hs[rows, 2 + so + ri, :, :]
                        self.op("act", lambda e, o=o, src=src: e.activation(out=o, in_=src, func=AF.Copy), reads=[("Hs", 2 + so + ri), "MC"], writes=["MC"])
            key, wt = self.wget()
            for tt in range(NTT):
                ts = slice(tt * TT, (tt + 1) * TT)
                b = self.next_bank()
                for kc in range(NCH):
                    o = self.psb(b)
                    lhsT = wt[:, kc * 128:(kc + 1) * 128]
                    rhs = h[:, kc, ts]
                    self.op("pe", lambda e, o=o, lhsT=lhsT, rhs=rhs, kc=kc: e.matmul(o, lhsT, rhs, start=(kc == 0), stop=(kc == NCH - 1)),
                            reads=[key, ("h", kc, tt)], writes=[("ps", b)])
                pb = self.psb(b)
                uo = ub3[:, :, tt * 64:(tt + 1) * 64]
                pin_u = pb.rearrange("p (c i) -> p i c", i=8)
                self.op("act", lambda e, uo=uo, pin_u=pin_u: e.activation(out=uo, in_=pin_u, func=AF.Copy), reads=[("ps", b)], writes=["ub"])
            for j in range(8):
                b = self.next_bank()
                o = self.ps[:, b, 0:256]
                nmm = (j + 1) + 8
                k_ = 0
                for i in range(j + 1):
                    lhsT = BDall[:, fc, (j - i) * 128:(j - i + 1) * 128]
                    rhs = ub3[:, i, :]
                    self.op("pe", lambda e, o=o, lhsT=lhsT, rhs=rhs, k_=k_, nmm=nmm: e.matmul(o, lhsT, rhs, start=(k_ == 0), stop=(k_ == nmm - 1)),
                            reads=[("BD", fc), "ub"], writes=[("ps", b)])
                    k_ += 1
                for jp in range(4):
                    q = fc * 4 + jp
                    for ri in range(2):
                        lhsT = MC[:, jp, j, ri, :]
                        rhs = Sbf[:, ri, q, :]
                        self.op("pe", lambda e, o=o, lhsT=lhsT, rhs=rhs, k_=k_, nmm=nmm: e.matmul(o, lhsT, rhs, start=(k_ == 0), stop=(k_ == nmm - 1)),
                                reads=["MC", EPB] + sbfk, writes=[("ps", b)])
                        k_ += 1
                go = gbv[:, fc, j::8]
                self.op("act", lambda e, go=go, o=o: e.activation(out=go, in_=o, func=AF.Gelu_apprx_tanh), reads=[("ps", b), EPF], writes=[("gb", fc)])

        self.epoch("bigb", kill=sbfk)
        bgl = self.small("l2_b_glu")
        for dc in range(NCH if S5ST >= 4 else 0):
            keyg, wg = self.wget()
            keyz, wz = self.wget()
            for tt in range(NTT):
                ts = slice(tt * TT, (tt + 1) * TT)
                b = self.next_bank()
                for kc in range(NCH):
                    o = self.psb(b)
                    lhsT = wg[:, kc * 128:(kc + 1) * 128]
                    rhs = gbv[:, kc, ts]
                    self.op("pe", lambda e, o=o, lhsT=lhsT, rhs=rhs, kc=kc: e.matmul(o, lhsT, rhs, start=(kc == 0), stop=(kc == NCH - 1)),
                            reads=[keyg, ("gb", kc), EPF], writes=[("ps", b)])
                bz = self.next_bank()
                for kc in range(NCH):
                    o = self.psb(bz)
                    lhsT = wz[:, kc * 128:(kc + 1) * 128]
                    rhs = h[:, kc, ts]
                    self.op("pe", lambda e, o=o, lhsT=lhsT, rhs=rhs, kc=kc: e.matmul(o, lhsT, rhs, start=(kc == 0), stop=(kc == NCH - 1)),
                            reads=[keyz, ("h", kc, tt)], writes=[("ps", bz)])
                sg = self.sg
                szz = self.szz
                pb = self.psb(b)
                pz = self.psb(bz)
                bcol = bgl[:, dc:dc + 1]
                self.op("act", lambda e, pb=pb, bcol=bcol: e.activation(out=sg[:, :], in_=pb, func=AF.Sigmoid, bias=bcol), reads=[("ps", b), "sm"], writes=["sg"])
                self.op("act", lambda e, pz=pz: e.activation(out=szz[:, :], in_=pz, func=AF.Silu), reads=[("ps", bz)], writes=["szz"])
                gin = gbv[:, dc, ts]
                self.op("dve", lambda e, gin=gin: e.tensor_tensor(out=sg[:, :], in0=sg[:, :], in1=gin, op=ALU.mult), reads=["sg", ("gb", dc), EPF], writes=["sg"])
                yo = y2b[:, dc, ts]
                self.op("dve", lambda e, yo=yo: e.tensor_tensor(out=yo, in0=sg[:, :], in1=szz[:, :], op=ALU.mult), reads=["sg", "szz", EPB], writes=[("y2", dc, tt)])
        self.epoch("bigf", kill=[("gb", fc) for fc in range(8)])
        for kc in range(4):
            self.dma(self.xt[:, kc, :], self.xspill[:, kc, :], reads=[EPF, ("xspill", kc)], writes=[("x", kc, tt) for tt in range(NTT)])
        self.emit_wout(w_out, lambda dc: dc, y2b, "y2", 8, mod[:, 16:24], dc_order=S5_DC)

    def carve_reset(self):
        self.gen_off = 0
        self.bb_off = 0

    def carve(self, shape, dt, pool="gen"):
        n = 1
        for d_ in shape[1:]:
            n *= d_
        if pool == "gen":
            esz = 4 if dt == F32 else 2
            off = (self.gen_off + 31) // 32 * 32
            self.gen_off = off + n * esz
            assert self.gen_off <= self.GEN_BYTES, ("GEN overflow", self.gen_off)
            base = self.GEN if dt == F32 else self.GENb
            ap = base[:, off // esz: off // esz + n]
        else:
            assert dt == BF16
            off = (self.bb_off + 31) // 32 * 32
            self.bb_off = off + n * 2
            assert self.bb_off <= 16448 * 2, ("bigb overflow", self.bb_off)
            ap = self.bigb[:, off // 2: off // 2 + n]
        if len(shape) > 2:
            names = " ".join(f"d{i}" for i in range(len(shape) - 1))
            kw = {f"d{i}": shape[i + 1] for i in range(len(shape) - 2)}
            ap = ap.rearrange(f"p ({names}) -> p {names}", **kw)
        return ap

    def alloc_conv(self):
        self.carve_reset()
        self.y = self.carve([128, 8, S], BF16, "bigb")
        self.usb = self.carve([128, TT], F32)
        self.vbuf = self.carve([128, 2 + S], F32)
        self.cacc = self.carve([128, TT], F32)
        self.gtmp = self.carve([128, TT], F32)
        self.sz = self.carve([128, TT], F32)
        vb = self.vbuf
        self.op("pool", lambda e: e.memset(vb[:, 0:2], 0.0), writes=[("v", -1)])

    def alloc_attn(self):
        self.carve_reset()
        self.qT = self.carve([128, S], BF16, "bigb")
        self.kA = self.carve([128, 2, S], BF16, "bigb")
        self.vA = self.carve([128, 2, 16 * 128], BF16, "bigb")
        self.onesA = self.carve([128, 2, 128], BF16, "bigb")
        self.yhp = self.carve([128, 2, S], BF16, "bigb")
        self.pT = self.carve([128, 4, 512], BF16)
        self.ND = self.carve([128, 2, S], F32)
        self.szb = self.carve([128, S], F32)
        self.tab = self.carve([128, 2, 512], F32)
        self.ssb = self.carve([128, 2, 512], F32)
        self.E = self.carve([128, 4, 512], F32)
        e0 = self.E
        self.soh = bass.AP(e0.tensor, e0.offset, [list(e0.ap[0]), [384, 2], [1, 384]])
        self.rrep = bass.AP(e0.tensor, e0.offset + 768, [list(e0.ap[0]), [384, 2], [1, 384]])
        kA, vA, onesA = self.kA, self.vA, self.onesA
        self.op("pool", lambda e: e.memset(kA[:, :, :], 0.0), writes=["kT"])
        self.op("pool", lambda e: e.memset(vA[:, :, :], 0.0), writes=[("vtm", tb) for tb in range(4)])
        self.op("pool", lambda e: e.memset(onesA[:, :, :], 0.0), writes=["onesb"])
        self.op("pool", lambda e: e.memset(onesA[:, 0, 0:64], 1.0), writes=["onesb"])
        self.op("pool", lambda e: e.memset(onesA[:, 1, 64:128], 1.0), writes=["onesb"])

    def alloc_s5(self):
        self.carve_reset()
        self.bigf = self.xt_raw
        off0 = (self.gen_off + 31) // 32 * 32
        self.MC = self.carve([128, 4, 8, 2, 128], BF16)
        self.MB = self.GENb[:, off0 // 2: off0 // 2 + 4096].rearrange("p (a b c d) -> p a b c d", a=2, b=8, c=2)
        self.MGb = self.GENb[:, (off0 + 8192) // 2: (off0 + 8192) // 2 + 4096].rearrange("p (u t r c) -> p u t r c", u=2, t=8, r=2)
        self.ub16 = self.carve([128, S], BF16)
        self.BDall = self.carve([128, 8, 1024], BF16)
        sgz = self.carve([128, 2 * TT], F32)
        self.sg = sgz[:, 0:TT]
        self.szz = sgz[:, TT:2 * TT]
        self.BDf = sgz
        self.gq = self.sg[:, 0:256]
        self.gt_ = self.szz[:, 0:256]
        self.MCC = self.carve([128, 2, 2, 128], BF16)
        self.identb = self.carve([128, 128], BF16)
        self.lam1 = self.carve([128, 32, 3], F32)
        self.Hs = self.carve([128, 6, 8, 16], F32)
        hs0 = self.Hs
        self.scr1 = bass.AP(hs0.tensor, hs0.offset, [list(hs0.ap[0]), [32, 16], [1, 32]])
        self.pw1 = self.carve([128, 9, 2, 32], F32)
        self.q1 = self.carve([128, 2, 32], F32)
        self.AA = self.carve([128, 2, 32], F32)
        self.AI = self.carve([128, 2, 32], F32)
        self.P1t = self.carve([128, 2, 32], F32)
        self.P2t = self.carve([128, 2, 32], F32)
        self.PT = self.carve([128, 2, 64], F32)
        self.bb1 = self.carve([128, 3, 16], F32)
        self.halfpi = self.carve([128, 1], F32)
        self.ident = self.carve([128, 128], F32)
        hpi = self.halfpi
        self.op("pool", lambda e: e.memset(hpi[:, :], float(np.pi / 2)), writes=["halfpi"])
        self.dma(self.ident[:, :], self.ident_in, writes=["ident"])

    def barrier(self):
        d = self.dummy
        self.sch.add("pool", lambda e: e.memset(d[:, 0:1], 0.0), reads=(), writes=[Sched.BAR], is_barrier=True)

    def build(self):
        nc = self.nc
        L = self.layers
        xin = nc.dram_tensor("xT", [D, S], F32, kind="ExternalInput").ap()
        small_in = nc.dram_tensor("small", [128, SMALL_N], F32, kind="ExternalInput").ap()
        wd = {}

        def wten(name, ntiles):
            wd[name] = nc.dram_tensor(name, [ntiles, 128, 1024], F32, kind="ExternalInput").ap()

        for l in L:
            wten(f"l{l}_ada_w", 24)
            if l in (0, 3):
                wten(f"l{l}_w_in", 64)
                wten(f"l{l}_w_out", 16)
            if l == 2:
                wten("l2_w_in", 16)
                wten("l2_w_glu", 8)
                wten("l2_w_out", 8)
                self.s5l1_in = nc.dram_tensor("s5_l1", [32, 128, 67], F32, kind="ExternalInput").ap()
                self.s5l2_in = nc.dram_tensor("s5_l2", [8, 128, 257], F32, kind="ExternalInput").ap()
                self.ident_in = nc.dram_tensor("s5_ident", [128, 128], F32, kind="ExternalInput").ap()
                self.gmask_in = nc.dram_tensor("s5_gmask", [128, 8], F32, kind="ExternalInput").ap()
                self.xspill = nc.dram_tensor("xspill", [128, NCH, S], F32).ap()
            if l == 1:
                wten("l1_w_in", 80)
                wten("l1_w_outk", 8)
                self.rb_in = nc.dram_tensor("rel_bias", [32, 48], F32, kind="ExternalInput").ap()
                self.oh_in = nc.dram_tensor("attn_oh", [32, 3 * 384], F32, kind="ExternalInput").ap()
                self.negm_in = nc.dram_tensor("attn_negm", [128, 384], F32, kind="ExternalInput").ap()
                self.rscr_h = nc.dram_tensor("rscr", [48, 128, 384], F32)
                self.rscr = self.rscr_h.ap()
        xout = nc.dram_tensor("outT", [D, S], F32, kind="ExternalOutput").ap()

        with contextlib.ExitStack() as st:
            def sb(name, shape, dt):
                return st.enter_context(nc.sbuf_tensor(name, shape, dt))

            self.xt_raw = sb("xt_raw", [128, 16448], F32)
            self.xt = self.xt_raw[:, 0:NCH * S].rearrange("p (c t) -> p c t", c=NCH)
            self.h = sb("h", [128, NCH, S], BF16)
            self.bigb = sb("bigb", [128, 16448], BF16)
            self.GEN_BYTES = 51456
            self.GEN = sb("gen", [128, self.GEN_BYTES // 4], F32)
            self.GENb = self.GEN.bitcast(BF16)
            self.NSTG = 2
            self.NWB = 6
            self.stg = sb("stg", [128, self.NSTG, 1024], F32)
            self.wb = sb("wb", [128, self.NWB, 1024], BF16)
            self.sm = sb("sm", [128, SMALL_N], F32)
            self.sc = sb("sc", [128, 8], F32)
            self.modA = sb("mod", [128, 24], F32)
            self.modB = sb("modB", [128, 24], F32)
            self.mod = self.modA
            self.modk = "mod"
            self.pre_ada = None
            self.ones = sb("ones", [128, 128], F32)
            self.epsc = sb("epsc", [128, 1], F32)
            self.dummy = sb("dummy_ep", [128, 1], F32)
            self.NNT = 2
            self.sq = sb("sq", [128, self.NNT, TT], BF16)
            self.onesb16 = sb("onesb16", [128, 128], BF16)
            self.rstd = sb("rstd", [128, TT], F32)
            self.ps = st.enter_context(nc.psum_tensor("ps", [128, 8, TT], F32))

            engines = ["pe", "act", "dve", "pool", "sp"]
            sems = {e: st.enter_context(nc.semaphore(f"sem_{e}")) for e in engines if e != "sp"}
            dsems = [st.enter_context(nc.semaphore(f"dsem{i}")) for i in range(Sched.NDS)]

            xr = xin.rearrange("(c p) t -> p c t", p=128)
            self.dma(self.sm[:, :], small_in, writes=["sm"])
            for tt in range(NTT):
                ts = slice(tt * TT, (tt + 1) * TT)
                self.dma(self.xt[:, :, ts], xr[:, :, ts], writes=[("x", kc, tt) for kc in range(NCH)])
            ones = self.ones
            self.op("pool", lambda e: e.memset(ones[:, :], 1.0), writes=["ones"])
            epsc = self.epsc
            self.op("pool", lambda e: e.memset(epsc[:, :], EPS), writes=["epsc"])
            o16 = self.onesb16
            self.op("pool", lambda e: e.memset(o16[:, :], 1.0), writes=["onesb16"])
            sc = self.sc
            cin = self.small("c")
            self.op("act", lambda e: e.activation(out=sc[:, :], in_=cin, func=AF.Silu), reads=["sm"], writes=["sc"])

            self.tables_done = False
            for li, l in enumerate(L):
                if li > 0:
                    self.barrier()
                self.next_ada = (L[li + 1], wd[f"l{L[li + 1]}_ada_w"]) if li + 1 < len(L) else None
                if l in (0, 3):
                    self.alloc_conv()
                    if li + 1 < len(L) and L[li + 1] == 1:
                        self.emit_attn_tables()
                    self.emit_conv_layer(l, wd[f"l{l}_ada_w"], wd[f"l{l}_w_in"], wd[f"l{l}_w_out"])
                if l == 1:
                    self.alloc_attn()
                    self.emit_attn_layer(wd["l1_ada_w"], wd["l1_w_in"], wd["l1_w_outk"])
                if l == 2:
                    self.alloc_s5()
                    self.emit_s5_layer(wd["l2_ada_w"], wd["l2_w_in"], wd["l2_w_glu"], wd["l2_w_out"], self.s5l1_in, self.s5l2_in)
            if self.do_final:
                self.emit_norm(self.small("final_g"), None, self.xt, "x")
            xo = xout.rearrange("(c p) t -> p c t", p=128)
            for kc in range(NCH):
                self.dma(xo[:, kc, :], self.xt[:, kc, :], reads=[("x", kc, tt) for tt in range(NTT)])

            run_engine = self.sch.emit(nc, engines, sems, dsems)
            with nc.Block() as block:
                @block.sync
                def _(e):
                    run_engine("sp", e)

                @block.tensor
                def _(e):
                    run_engine("pe", e)

                @block.scalar
                def _(e):
                    run_engine("act", e)

                @block.vector
                def _(e):
                    run_engine("dve", e)

                @block.gpsimd
                def _(e):
                    run_engine("pool", e)
        return nc


def prep_weights(inp, layers):
    w = {}
    for l in layers:
        w[f"l{l}_ada_w"] = tile_w(np.asarray(inp[f"l{l}_ada_w"], np.float32))
        if l in (0, 3):
            w[f"l{l}_w_in"] = tile_w(np.asarray(inp[f"l{l}_w_in"], np.float32))
            w[f"l{l}_w_out"] = tile_w(np.asarray(inp[f"l{l}_w_out"], np.float32))
        if l == 2:
            w["l2_w_in"] = tile_w(np.asarray(inp["l2_w_in"], np.float32))
            w["l2_w_glu"] = tile_w(np.asarray(inp["l2_w_glu"], np.float32))
            w["l2_w_out"] = tile_w(np.asarray(inp["l2_w_out"], np.float32))
            w.update(s5_host_inputs(inp))
        if l == 1:
            w["l1_w_in"] = tile_w(np.asarray(inp["l1_w_in"], np.float32))
            w["l1_w_outk"] = np.ascontiguousarray(np.asarray(inp["l1_w_out"], np.float32).reshape(8, 128, 1024))
            w["rel_bias"] = np.ascontiguousarray(np.asarray(inp["rel_bias"], np.float32))
            oh, negm = attn_static()
            w["attn_oh"] = oh
            w["attn_negm"] = negm
    return w


def run_layers(inp, x, layers, do_final):
    bld = Builder(layers, do_final)
    nc = bld.build()
    w = prep_weights(inp, layers)
    in_maps = []
    for b in range(NCORES):
        m = {"xT": np.ascontiguousarray(x[b].T), "small": pack_small(inp, b)}
        m.update(w)
        in_maps.append(m)
    res = run_bass_kernel_spmd(nc, in_maps, core_ids=list(range(NCORES)))
    out = np.stack([np.ascontiguousarray(r["outT"].T) for r in res.results], axis=0)
    return out.astype(np.float32)


def kernel(**inputs):
    inp = {k: np.asarray(v) for k, v in inputs.items()}
    x = np.asarray(inp["x"], np.float32)
    return run_layers(inp, x, [0, 1, 2, 3], True)
```

```python
import contextlib
import os
import numpy as np
import concourse.bass as bass
import concourse.mybir as mybir
from concourse.bass_utils import run_bass_kernel_spmd

F32 = mybir.dt.float32
BF16 = mybir.dt.bfloat16
AF = mybir.ActivationFunctionType
ALU = mybir.AluOpType

D = 1024
S = 2048
NCH = 8
TT = 512
NTT = S // TT
EPS = 1e-6
DEBUG = bool(int(os.environ.get("KDEBUG", "0")))
NCORES = int(os.environ.get("KCORES", "8"))


class _Op:
    __slots__ = ("eng", "fn", "deps", "dma", "needs_inc", "count", "dsem", "dval")

    def __init__(self, eng, fn, deps, dma):
        self.eng = eng
        self.fn = fn
        self.deps = deps
        self.dma = dma
        self.needs_inc = False
        self.count = 0
        self.dsem = -1
        self.dval = 0


class Sched:
    NDS = 24
    BAR = "__bar__"

    def __init__(self):
        self.ops = []
        self.last_w = {}
        self.readers = {}

    def add(self, eng, fn, reads=(), writes=(), dma=False, is_barrier=False):
        idx = len(self.ops)
        if not is_barrier:
            reads = list(reads) + [Sched.BAR]
        deps = set()
        for k in reads:
            j = self.last_w.get(k)
            if j is not None:
                deps.add(j)
        for k in writes:
            j = self.last_w.get(k)
            if j is not None:
                deps.add(j)
            for r in self.readers.get(k, ()):
                deps.add(r)
        self.ops.append(_Op(eng, fn, deps, dma))
        for k in reads:
            lst = self.readers.setdefault(k, [])
            if not dma:
                lst[:] = [r for r in lst if self.ops[r].dma or self.ops[r].eng != eng]
            lst.append(idx)
        for k in writes:
            self.last_w[k] = idx
            self.readers[k] = []
        return idx

    def emit(self, nc, engines, sems, dsems, final_waits=True):
        ops = self.ops
        for op in ops:
            for j in op.deps:
                pj = ops[j]
                if pj.dma:
                    continue
                if pj.eng != op.eng or op.dma or op.eng != "pe":
                    pj.needs_inc = True
        cnt = {e: 0 for e in engines}
        ndma = 0
        for op in ops:
            if op.dma:
                op.dsem = ndma % self.NDS
                op.dval = 16 * (ndma // self.NDS + 1)
                ndma += 1
            elif op.needs_inc:
                cnt[op.eng] += 1
                op.count = cnt[op.eng]
        per_eng = {e: [] for e in engines}
        for i, op in enumerate(ops):
            per_eng[op.eng].append(i)
        last_dma = [None] * self.NDS

        def run_engine(ename, e):
            waited = {}

            def wait(sem_key, sem, val):
                if waited.get(sem_key, 0) >= val:
                    return
                waited[sem_key] = val
                e.wait_ge(sem, val)

            for i in per_eng[ename]:
                op = ops[i]
                for j in sorted(op.deps):
                    pj = ops[j]
                    if pj.dma:
                        wait(("d", pj.dsem), dsems[pj.dsem], pj.dval)
                    elif pj.eng != op.eng or op.dma or op.eng != "pe":
                        wait(("c", pj.eng), sems[pj.eng], pj.count)
                if op.dma:
                    if op.dval > 16:
                        wait(("d", op.dsem), dsems[op.dsem], op.dval - 16)
                    ins = op.fn(e)
                    ins.then_inc(dsems[op.dsem], 16)
                else:
                    ins = op.fn(e)
                    if op.needs_inc:
                        ins.then_inc(sems[op.eng], 1)
            if ename == "sp" and final_waits:
                fin = {}
                for op in ops:
                    if op.dma:
                        fin[op.dsem] = max(fin.get(op.dsem, 0), op.dval)
                for s_, v_ in fin.items():
                    wait(("d", s_), dsems[s_], v_)

        return run_engine


def tile_w(W):
    K, N = W.shape
    A = K // 1024
    NT = N // 128
    t = W.reshape(A, 8, 128, NT, 128).transpose(0, 3, 2, 1, 4)
    return np.ascontiguousarray(t).reshape(A * NT, 128, 1024)


def colmajor(v, nch):
    return np.ascontiguousarray(np.asarray(v).reshape(nch, 128).T)


SMALL_COLS = {}

DILS = ((128, 1), (512, 4), (2048, 16))
NEGV = -30000.0


def _t5_bucket(dist):
    import math
    max_exact = 16
    d = np.maximum(dist, 0)
    ratio = np.log(np.maximum(d, 1) / max_exact) / math.log(2048 / max_exact)
    large = np.minimum(max_exact + (ratio * (32 - max_exact)).astype(np.int64), 31)
    return np.where(d < max_exact, d, large).astype(np.int32)


def attn_static():
    oh = np.zeros((32, 3, 384), np.float32)
    negm = np.full((128, 384), NEGV, np.float32)
    w = np.arange(127, 256)
    negm[:, 127:256] = 0.0
    for g, (win, dil) in enumerate(DILS):
        bk = _t5_bucket((w - 127) * dil)
        oh[bk, g, w] = 1.0
    return oh.reshape(32, 3 * 384), negm


def _small_layout():
    off = 0
    lay = {}

    def put(name, n):
        nonlocal off
        lay[name] = (off, n)
        off += n

    put("c", 8)
    put("final_g", 8)
    for l in range(4):
        put(f"l{l}_ada_b", 24)
    for l in (0, 3):
        put(f"l{l}_conv_w", 48)
        put(f"l{l}_conv_b", 16)
    put("l2_d_skip", 8)
    put("l2_b_glu", 8)
    return lay, off


SMALL_LAY, SMALL_N = _small_layout()


def pack_small(inp, b):
    sm = np.zeros((128, SMALL_N), np.float32)

    def put(name, arr):
        o, n = SMALL_LAY[name]
        assert arr.shape == (128, n), (name, arr.shape, n)
        sm[:, o:o + n] = arr

    put("c", colmajor(inp["c"][b], 8))
    put("final_g", colmajor(inp["final_g"], 8))
    for l in range(4):
        put(f"l{l}_ada_b", colmajor(inp[f"l{l}_ada_b"], 24))
    for l in (0, 3):
        cw = np.asarray(inp[f"l{l}_conv_w"])
        put(f"l{l}_conv_w", np.ascontiguousarray(cw.T.reshape(16, 128, 3).transpose(1, 0, 2)).reshape(128, 48))
        put(f"l{l}_conv_b", colmajor(inp[f"l{l}_conv_b"], 16))
    put("l2_d_skip", colmajor(inp["l2_d_skip"], 8))
    put("l2_b_glu", colmajor(inp["l2_b_glu"], 8))
    return sm


def s5_host_inputs(inp):
    f = lambda k: np.asarray(inp[k], np.float32)
    lre, lim, ldt = f("l2_lambda_re"), f("l2_lambda_im"), f("l2_log_dt")
    bre, bim, cre, cim = f("l2_b_re"), f("l2_b_im"), f("l2_c_re"), f("l2_c_im")
    l1 = np.zeros((32, 128, 67), np.float32)
    for q in range(32):
        for g2 in range(2):
            g = 2 * q + g2
            rows = slice(g2 * 64, g2 * 64 + 64)
            l1[q, rows, 0:16] = cre[g].T
            l1[q, rows, 16:32] = cim[g].T
            l1[q, rows, 32:48] = bre[g]
            l1[q, rows, 48:64] = bim[g]
            l1[q, rows, 64] = lre[g]
            l1[q, rows, 65] = lim[g]
            l1[q, rows, 66] = ldt[g]
    l2 = np.zeros((8, 128, 257), np.float32)
    for fc in range(8):
        for gl in range(8):
            g = fc * 8 + gl
            rows = slice(gl * 16, gl * 16 + 16)
            l2[fc, rows, 0:64] = bre[g].T
            l2[fc, rows, 64:128] = bim[g].T
            l2[fc, rows, 128:192] = lre[g][None, :]
            l2[fc, rows, 192:256] = lim[g][None, :]
            l2[fc, rows, 256] = ldt[g]
    ident = np.eye(128, dtype=np.float32)
    gmask = np.zeros((128, 8), np.float32)
    for gl in range(8):
        gmask[gl * 16:(gl + 1) * 16, gl] = 1.0
    return {"s5_l1": l1, "s5_l2": l2, "s5_ident": ident, "s5_gmask": gmask}


def _bc(ap, pos, n):
    dims = [list(d) for d in ap.ap]
    dims.insert(pos, [0, n])
    return bass.AP(ap.tensor, ap.offset, dims)

class Builder:
    def __init__(self, layers, do_final):
        self.layers = layers
        self.do_final = do_final
        self.nc = bass.Bass("TRN2", target_bir_lowering=False)
        self.sch = Sched()
        self.bank_ctr = 0
        self.stg_ctr = 0
        self.wb_ctr = 0

    def op(self, eng, fn, reads=(), writes=()):
        return self.sch.add(eng, fn, reads, writes)

    def dma(self, out, in_, reads=(), writes=()):
        return self.sch.add("sp", lambda e: e.dma_start(out=out, in_=in_), reads, writes, dma=True)

    def next_bank(self):
        b = self.bank_ctr % 8
        self.bank_ctr += 1
        return b

    def psb(self, b):
        return self.ps[:, b, :]

    def load_wtile(self, dram_ap_tile, cast_eng="pool", want_bf16=True):
        s = self.stg_ctr % self.NSTG
        self.stg_ctr += 1
        self.dma(self.stg[:, s, :], dram_ap_tile, writes=[("stg", s)])
        if not want_bf16:
            return ("stg", s), self.stg[:, s, :]
        w = self.wb_ctr % self.NWB
        self.wb_ctr += 1
        src = self.stg[:, s, :]
        dst = self.wb[:, w, :]
        if cast_eng == "act":
            self.op("act", lambda e: e.activation(out=dst, in_=src, func=AF.Copy), reads=[("stg", s)], writes=[("wb", w)])
        else:
            self.op(cast_eng, lambda e: e.tensor_copy(out=dst, in_=src), reads=[("stg", s)], writes=[("wb", w)])
        return ("wb", w), self.wb[:, w, :]

    def wstream(self, reqs, la):
        self._ws_reqs = list(reqs)
        self._ws_issued = 0
        self._ws_taken = 0
        self._ws_q = []
        self._ws_la = la

    def wget(self):
        la = self._ws_la
        for k_ in range(self._ws_taken, min(len(self._ws_reqs), self._ws_taken + 1 + la)):
            if not self._ws_reqs[k_][2]:
                la = min(la, self.NSTG - 1)
        want = min(len(self._ws_reqs), self._ws_taken + 1 + la)
        while self._ws_issued < want:
            ap, ceng, bf = self._ws_reqs[self._ws_issued]
            self._ws_q.append(self.load_wtile(ap, cast_eng=ceng, want_bf16=bf))
            self._ws_issued += 1
        self._ws_taken += 1
        return self._ws_q.pop(0)

    def small(self, name, c0=0, n=None):
        o, nn = SMALL_LAY[name]
        if n is None:
            n = nn - c0
        return self.sm[:, o + c0:o + c0 + n]

    def emit_ada_mm(self, ada_w):
        b = self.next_bank()
        for j in range(24):
            key, wt = self.wget()
            for kc in range(8):
                lhsT = wt[:, kc * 128:(kc + 1) * 128]
                rhs = self.sc[:, kc:kc + 1]
                o = self.ps[:, b, j:j + 1]
                self.op("pe", lambda e, o=o, lhsT=lhsT, rhs=rhs, kc=kc: e.matmul(o, lhsT, rhs, start=(kc == 0), stop=(kc == 7)),
                        reads=[key, "sc"], writes=[("ps", b)])
        return b

    def emit_ada_fin(self, l, b, mod, mk):
        ps = self.ps[:, b, 0:24]
        ab = self.small(f"l{l}_ada_b")
        self.op("dve", lambda e: e.tensor_tensor(out=mod[:, 0:24], in0=ps, in1=ab, op=ALU.add),
                reads=[("ps", b), "sm"], writes=[mk])
        self.op("dve", lambda e: e.tensor_scalar_add(mod[:, 8:16], mod[:, 8:16], 1.0), reads=[mk], writes=[mk])

    def emit_ada(self, l, ada_w):
        if self.pre_ada == l:
            self.mod, self.modk = self.modB, "modB"
            return
        self.mod, self.modk = self.modA, "mod"
        b = self.emit_ada_mm(ada_w)
        self.emit_ada_fin(l, b, self.modA, "mod")

    def emit_norm(self, scale_ap, bias_ap, out_tile, out_key, to_x=False):
        xt = self.xt
        for tt in range(NTT):
            ts = slice(tt * TT, (tt + 1) * TT)
            b = self.next_bank()
            for kc in range(NCH):
                sq = self.sq[:, kc % self.NNT, :]
                xin = xt[:, kc, ts]
                self.op("act", lambda e, sq=sq, xin=xin: e.activation(out=sq, in_=xin, func=AF.Square),
                        reads=[("x", kc, tt)], writes=[("sq", kc % self.NNT)])
                o = self.psb(b)
                self.op("pe", lambda e, o=o, sq=sq, kc=kc: e.matmul(o, self.onesb16[:, :], sq, start=(kc == 0), stop=(kc == NCH - 1)),
                        reads=[("sq", kc % self.NNT), "onesb16"], writes=[("ps", b)])
            rstd = self.rstd
            pb = self.psb(b)
            self.op("act", lambda e, pb=pb: e.activation(out=rstd[:, :], in_=pb, func=AF.Sqrt, scale=1.0 / D, bias=self.epsc[:, 0:1]),
                    reads=[("ps", b), "epsc"], writes=["rstd"])
            self.op("dve", lambda e: e.reciprocal(out=rstd[:, :], in_=rstd[:, :]), reads=["rstd"], writes=["rstd"])
            for kc in range(NCH):
                bt = self.next_bank()
                tmp = self.psb(bt)
                xin = xt[:, kc, ts]
                self.op("dve", lambda e, tmp=tmp, xin=xin: e.tensor_tensor(out=tmp, in0=xin, in1=rstd[:, :], op=ALU.mult),
                        reads=[("x", kc, tt), "rstd"], writes=[("ps", bt)])
                o = out_tile[:, kc, ts]
                sc_ = scale_ap[:, kc:kc + 1]
                if bias_ap is not None:
                    bi_ = bias_ap[:, kc:kc + 1]
                    self.op("act", lambda e, o=o, tmp=tmp, sc_=sc_, bi_=bi_: e.activation(out=o, in_=tmp, func=AF.Identity, scale=sc_, bias=bi_),
                            reads=[("ps", bt), self.modk, "sm"], writes=[(out_key, kc, tt)])
                else:
                    self.op("act", lambda e, o=o, tmp=tmp, sc_=sc_: e.activation(out=o, in_=tmp, func=AF.Identity, scale=sc_),
                            reads=[("ps", bt), self.modk, "sm"], writes=[(out_key, kc, tt)])

    def emit_wout(self, w_out_tiles, tile_idx_fn, y_tile, y_key, n_k, gate_ap, dc_order=None):
        xt = self.xt
        for dc in (dc_order if dc_order is not None else range(NCH)):
            key, wt = self.wget()
            for tt in range(NTT):
                ts = slice(tt * TT, (tt + 1) * TT)
                b = self.next_bank()
                for kc in range(n_k):
                    lhsT = wt[:, kc * 128:(kc + 1) * 128]
                    rhs = y_tile[:, kc, ts]
                    o = self.psb(b)
                    self.op("pe", lambda e, o=o, lhsT=lhsT, rhs=rhs, kc=kc: e.matmul(o, lhsT, rhs, start=(kc == 0), stop=(kc == n_k - 1)),
                            reads=[key, (y_key, kc, tt)], writes=[("ps", b)])
                xo = xt[:, dc, ts]
                pb = self.psb(b)
                g_ = gate_ap[:, dc:dc + 1]
                self.op("dve", lambda e, xo=xo, pb=pb, g_=g_: e.scalar_tensor_tensor(out=xo, in0=pb, scalar=g_, in1=xo, op0=ALU.mult, op1=ALU.add),
                        reads=[("ps", b), self.modk, ("x", dc, tt)], writes=[("x", dc, tt)])

    def emit_conv_layer(self, l, ada_w, w_in, w_out):
        reqs = [(ada_w[j], "pool", False) for j in range(24)] if self.pre_ada != l else []
        for half in range(2):
            for el in range(8):
                for q in range(4):
                    reqs.append((w_in[q * 16 + half * 8 + el], "pool" if q % 2 == 0 else "act", True))
            for dc in range(NCH):
                reqs.append((w_out[half * 8 + dc], "act", True))
        self.wstream(reqs, 2)
        self.emit_ada(l, ada_w)
        mod = self.mod
        self.emit_norm(mod[:, 8:16], mod[:, 0:8], self.h, "h")
        cw = self.small(f"l{l}_conv_w")
        cb = self.small(f"l{l}_conv_b")
        h = self.h
        vb = self.vbuf
        for half in range(2):
            for el in range(8):
                e_ = half * 8 + el
                wk = []
                for q in range(4):
                    wk.append(self.wget())
                for tt in range(NTT):
                    ts = slice(tt * TT, (tt + 1) * TT)
                    banks = []
                    for q in range(4):
                        b = self.next_bank()
                        banks.append(b)
                        key, wt = wk[q]
                        for kc in range(NCH):
                            lhsT = wt[:, kc * 128:(kc + 1) * 128]
                            rhs = h[:, kc, ts]
                            o = self.psb(b)
                            self.op("pe", lambda e, o=o, lhsT=lhsT, rhs=rhs, kc=kc: e.matmul(o, lhsT, rhs, start=(kc == 0), stop=(kc == NCH - 1)),
                                    reads=[key, ("h", kc, tt)], writes=[("ps", b)])
                    bu, bgc, bgb, bz = banks
                    usb = self.usb
                    pu = self.psb(bu)
                    self.op("act", lambda e, pu=pu: e.activation(out=usb[:, :], in_=pu, func=AF.Copy), reads=[("ps", bu)], writes=["usb"])
                    vcur = vb[:, 2 + tt * TT: 2 + (tt + 1) * TT]
                    pgc = self.psb(bgc)
                    self.op("dve", lambda e, vcur=vcur, pgc=pgc: e.tensor_tensor(out=vcur, in0=pgc, in1=usb[:, :], op=ALU.mult),
                            reads=[("ps", bgc), "usb"], writes=[("v", tt)])
                    v0 = vb[:, tt * TT: (tt + 1) * TT]
                    v1 = vb[:, 1 + tt * TT: 1 + (tt + 1) * TT]
                    cacc = self.cacc
                    w0 = cw[:, e_ * 3 + 0:e_ * 3 + 1]
                    w1 = cw[:, e_ * 3 + 1:e_ * 3 + 2]
                    w2 = cw[:, e_ * 3 + 2:e_ * 3 + 3]
                    rv = [("v", tt)] + ([("v", tt - 1)] if tt > 0 else [])
                    self.op("pool", lambda e, v0=v0, w0=w0: e.tensor_scalar(out=cacc[:, :], in0=v0, scalar1=w0, scalar2=0.0, op0=ALU.mult, op1=ALU.add),
                            reads=rv + ["sm"], writes=["cacc"])
                    self.op("dve", lambda e, v1=v1, w1=w1: e.scalar_tensor_tensor(out=cacc[:, :], in0=v1, scalar=w1, in1=cacc[:, :], op0=ALU.mult, op1=ALU.add),
                            reads=rv + ["sm", "cacc"], writes=["cacc"])
                    self.op("dve", lambda e, vcur=vcur, w2=w2: e.scalar_tensor_tensor(out=cacc[:, :], in0=vcur, scalar=w2, in1=cacc[:, :], op0=ALU.mult, op1=ALU.add),
                            reads=rv + ["sm", "cacc"], writes=["cacc"])
                    pgb = self.psb(bgb)
                    cb_ = cb[:, e_:e_ + 1]
                    gt = self.gtmp
                    self.op("dve", lambda e, pgb=pgb, cb_=cb_: e.scalar_tensor_tensor(out=gt[:, :], in0=cacc[:, :], scalar=cb_, in1=pgb, op0=ALU.add, op1=ALU.mult),
                            reads=["cacc", "sm", ("ps", bgb)], writes=["gtmp"])
                    sz = self.sz
                    pz = self.psb(bz)
                    self.op("act", lambda e, pz=pz: e.activation(out=sz[:, :], in_=pz, func=AF.Silu), reads=[("ps", bz)], writes=["sz"])
                    yo = self.y[:, el, ts]
                    self.op("pool", lambda e, yo=yo: e.tensor_tensor(out=yo, in0=gt[:, :], in1=sz[:, :], op=ALU.mult),
                            reads=["gtmp", "sz"], writes=[("y", el, tt)])
            self.emit_wout(w_out, lambda dc, half=half: half * 8 + dc, self.y, "y", 8, mod[:, 16:24])


    def emit_attn_tables(self):
        self.tables_done = True
        if getattr(self, "in_attn", False):
            self.tsoh, self.trrep = self.soh, self.rrep
        else:
            self.tsoh = self.carve([128, 2, 384], F32)
            self.trrep = self.carve([128, 2, 384], F32)
        self.trb = self.carve([128, 48], F32)
        self.toh = self.carve([128, 3 * 384], F32)
        self.tnegm = self.carve([128, 384], F32)
        self.dma(self.trb[0:32, :], self.rb_in, writes=["rb"])
        self.dma(self.toh[0:32, :], self.oh_in, writes=["oh"])
        self.dma(self.tnegm[:, :], self.negm_in, writes=["negm"])
        for gh in range(48):
            g = gh // 16
            soh = self.tsoh[0:32, gh % 2, :]
            ohg = self.toh[0:32, g * 384:(g + 1) * 384]
            rbc = self.trb[0:32, gh:gh + 1]
            self.op("dve", lambda e, soh=soh, ohg=ohg, rbc=rbc: e.tensor_scalar(out=soh, in0=ohg, scalar1=rbc, scalar2=0.0, op0=ALU.mult, op1=ALU.add),
                    reads=["oh", "rb"], writes=[("soh", gh % 2)])
            b = self.next_bank()
            o = self.ps[:, b, 0:384]
            self.op("pe", lambda e, o=o, soh=soh: e.matmul(o, self.ones[0:32, :], soh, start=True, stop=True),
                    reads=[("soh", gh % 2), "ones"], writes=[("ps", b)])
            rr = self.trrep[:, gh % 2, :]
            self.op("dve", lambda e, rr=rr, o=o: e.tensor_tensor(out=rr, in0=o, in1=self.tnegm[:, :], op=ALU.add),
                    reads=[("ps", b), "negm"], writes=[("rrep", gh % 2)])
            self.dma(self.rscr[gh], rr, reads=[("rrep", gh % 2)], writes=[("rscr", gh)])


    def emit_attn_layer(self, ada_w, w_in, w_outk):
        l = 1
        reqs = [(ada_w[j], "pool", False) for j in range(24)]
        for hp in range(8):
            reqs.append((w_in[72 + hp], "act", True))
            for g in range(3):
                for t_ in range(3):
                    reqs.append((w_in[(g * 3 + t_) * 8 + hp], "pool" if t_ != 1 else "act", True))
                if g == 0 and hp > 0:
                    reqs.append((w_outk[hp - 1], "act", True))
        reqs.append((w_outk[7], "act", True))
        self.wstream(reqs, 4)
        self.emit_ada(l, ada_w)
        mod = self.mod
        self.emit_norm(mod[:, 8:16], mod[:, 0:8], self.h, "h")
        h = self.h
        xt = self.xt
        if not self.tables_done:
            self.in_attn = True
            self.emit_attn_tables()
        qT, kA, vA, ND, szb = self.qT, self.kA, self.vA, self.ND, self.szb
        STAGE = int(os.environ.get("KATT_STAGE", "9"))
        if STAGE == 0:
            return
        def emit_wout_hp(hp):
            yhp = self.yhp[:, hp % 2, :]
            key, wt = self.wget()
            for dc in range(NCH):
                for tt in range(NTT):
                    ts = slice(tt * TT, (tt + 1) * TT)
                    b = self.next_bank()
                    o = self.psb(b)
                    lhsT = wt[:, dc * 128:(dc + 1) * 128]
                    rhs = yhp[:, ts]
                    self.op("pe", lambda e, o=o, lhsT=lhsT, rhs=rhs: e.matmul(o, lhsT, rhs, start=True, stop=True),
                            reads=[key, ("yhp", hp % 2, tt)], writes=[("ps", b)])
                    xo = xt[:, dc, ts]
                    g_ = mod[:, 16 + dc:17 + dc]
                    self.op("dve", lambda e, xo=xo, o=o, g_=g_: e.scalar_tensor_tensor(out=xo, in0=o, scalar=g_, in1=xo, op0=ALU.mult, op1=ALU.add),
                            reads=[("ps", b), self.modk, ("x", dc, tt)], writes=[("x", dc, tt)])

        for hp in range(8 if STAGE >= 9 else 1):
            key, wt = self.wget()
            for tt in range(NTT):
                ts = slice(tt * TT, (tt + 1) * TT)
                b = self.next_bank()
                for kc in range(NCH):
                    o = self.psb(b)
                    lhsT = wt[:, kc * 128:(kc + 1) * 128]
                    rhs = h[:, kc, ts]
                    self.op("pe", lambda e, o=o, lhsT=lhsT, rhs=rhs, kc=kc: e.matmul(o, lhsT, rhs, start=(kc == 0), stop=(kc == NCH - 1)),
                            reads=[key, ("h", kc, tt)], writes=[("ps", b)])
                pb = self.psb(b)
                so = szb[:, ts]
                self.op("act", lambda e, so=so, pb=pb: e.activation(out=so, in_=pb, func=AF.Silu), reads=[("ps", b)], writes=[("szb", tt)])
            for g, (win, dil) in enumerate(DILS):
                tbi = (hp * 3 + g) % 2
                tabf = self.tab[:, tbi, :]
                tabk = ("tab", tbi)
                for hd in range(2):
                    gh = g * 16 + hp * 2 + hd
                    for blk, off in ((0, 255), (1, 127)):
                        src = bass.AP(self.rscr_h, gh * 128 * 384 + off, [[383, 128], [1, 128]])
                        dstt = tabf[:, (hd * 2 + blk) * 128:(hd * 2 + blk + 1) * 128]
                        self.dma(dstt, src, reads=[("rscr", gh)], writes=[tabk])
                self.op("act", lambda e, tabf=tabf: e.activation(out=tabf, in_=tabf, func=AF.Exp), reads=[tabk], writes=[tabk])
                for t_, dkey, scl in ((0, "qT", 0.125), (1, "kT", 1.0)):
                    key, wt = self.wget()
                    for tt in range(NTT):
                        ts = slice(tt * TT, (tt + 1) * TT)
                        b = self.next_bank()
                        for kc in range(NCH):
                            o = self.psb(b)
                            lhsT = wt[:, kc * 128:(kc + 1) * 128]
                            rhs = h[:, kc, ts]
                            self.op("pe", lambda e, o=o, lhsT=lhsT, rhs=rhs, kc=kc: e.matmul(o, lhsT, rhs, start=(kc == 0), stop=(kc == NCH - 1)),
                                    reads=[key, ("h", kc, tt)], writes=[("ps", b)])
                        pb = self.psb(b)
                        if t_ == 0:
                            do = qT[:, ts]
                            self.op("act", lambda e, do=do, pb=pb, scl=scl: e.activation(out=do, in_=pb, func=AF.Copy, scale=scl),
                                    reads=[("ps", b)], writes=[dkey])
                        else:
                            for hd in range(2):
                                do = kA[hd * 64:(hd + 1) * 64, hd, ts]
                                pi = self.ps[hd * 64:(hd + 1) * 64, b, :]
                                self.op("act", lambda e, do=do, pi=pi: e.activation(out=do, in_=pi, func=AF.Copy),
                                        reads=[("ps", b)], writes=[dkey])
                nb = (S // dil) // 128
                key, wt = self.wget()

                def tok(ti, dil=dil, nb=nb):
                    r, n = ti // nb, ti % nb
                    st_ = r + dil * 128 * n
                    return slice(st_, st_ + dil * 127 + 1, dil)

                for tb in range(4):
                    b = self.next_bank()
                    for tj in range(4):
                        ti = tb * 4 + tj
                        for kc in range(NCH):
                            o = self.ps[:, b, tj * 128:(tj + 1) * 128]
                            lhsT = h[:, kc, tok(ti)]
                            rhs = wt[:, kc * 128:(kc + 1) * 128]
                            self.op("pe", lambda e, o=o, lhsT=lhsT, rhs=rhs, kc=kc: e.matmul(o, lhsT, rhs, start=(kc == 0), stop=(kc == NCH - 1)),
                                    reads=[key] + [("h", kc, t4) for t4 in range(NTT)], writes=[("ps", b)])
                    pb4 = self.psb(b).rearrange("p (a q) -> p a q", a=4)
                    for hd in range(2):
                        vo = vA[:, hd, tb * 512:(tb + 1) * 512].rearrange("p (a q) -> p a q", a=4)[:, :, hd * 64:(hd + 1) * 64]
                        pi = pb4[:, :, hd * 64:(hd + 1) * 64]
                        self.op("act", lambda e, vo=vo, pi=pi: e.activation(out=vo, in_=pi, func=AF.Copy), reads=[("ps", b)], writes=[("vtm", tb)])
                if g == 0 and hp > 0:
                    emit_wout_hp(hp - 1)
                EW = [("rrep", 0), ("rrep", 1), ("soh", 0), ("soh", 1)]
                NB3 = 4

                def emit_scores(ti):
                    n = ti % nb
                    b = self.next_bank()
                    blks = (0, 1) if n > 0 else (1,)
                    for hd in range(2):
                        for blk in blks:
                            kti = ti if blk == 1 else ti - 1
                            o = self.ps[:, b, (hd * 2 + blk) * 128:(hd * 2 + blk + 1) * 128]
                            lhsT = kA[:, hd, tok(kti)]
                            rhs = qT[:, tok(ti)]
                            self.op("pe", lambda e, o=o, lhsT=lhsT, rhs=rhs: e.matmul(o, lhsT, rhs, start=True, stop=True),
                                    reads=["qT", "kT"], writes=[("ps", b)])
                    return b

                def emit_softmax(ti, b):
                    n = ti % nb
                    bu = ti % NB3
                    E = self.E[:, bu, :]
                    pT = self.pT[:, bu, :]
                    if n > 0:
                        pb = self.psb(b)
                        self.op("act", lambda e, E=E, pb=pb: e.activation(out=E, in_=pb, func=AF.Exp),
                                reads=[("ps", b)], writes=[("E", bu)] + EW)
                        self.op("dve", lambda e, E=E, pT=pT, tabf=tabf: e.tensor_tensor(out=pT[:, 0:256], in0=E[:, 0:256], in1=tabf[:, 0:256], op=ALU.mult),
                                reads=[("E", bu), tabk], writes=[("pT", bu, 0)])
                        self.op("pool", lambda e, E=E, pT=pT, tabf=tabf: e.tensor_tensor(out=pT[:, 256:512], in0=E[:, 256:512], in1=tabf[:, 256:512], op=ALU.mult),
                                reads=[("E", bu), tabk], writes=[("pT", bu, 1)])
                    else:
                        for hd in range(2):
                            cs = slice((hd * 2 + 1) * 128, (hd * 2 + 2) * 128)
                            pb = self.ps[:, b, cs]
                            eo = E[:, cs]
                            self.op("act", lambda e, eo=eo, pb=pb: e.activation(out=eo, in_=pb, func=AF.Exp),
                                    reads=[("ps", b)], writes=[("E", bu)] + EW)
                            po = pT[:, cs]
                            tb_ = tabf[:, cs]
                            self.op("dve" if hd == 0 else "pool", lambda e, po=po, eo=eo, tb_=tb_: e.tensor_tensor(out=po, in0=eo, in1=tb_, op=ALU.mult),
                                    reads=[("E", bu), tabk], writes=[("pT", bu, hd)])

                def emit_pv(ti):
                    n = ti % nb
                    bu = ti % NB3
                    pT = self.pT[:, bu, :]
                    blks = (0, 1) if n > 0 else (1,)
                    b2 = self.next_bank()
                    for which in range(2):
                        o = self.ps[:, b2, which * 128:(which + 1) * 128]
                        combos = [(hd, blk) for hd in range(2) for blk in blks]
                        for ci, (hd, blk) in enumerate(combos):
                            kti = ti if blk == 1 else ti - 1
                            if which == 0:
                                lhsT = vA[:, hd, kti * 128:(kti + 1) * 128]
                                rk = [("vtm", kti // 4)]
                            else:
                                lhsT = self.onesA[:, hd, :]
                                rk = ["onesb"]
                            rhs = pT[:, (hd * 2 + blk) * 128:(hd * 2 + blk + 1) * 128]
                            self.op("pe", lambda e, o=o, lhsT=lhsT, rhs=rhs, ci=ci, ncb=len(combos): e.matmul(o, lhsT, rhs, start=(ci == 0), stop=(ci == ncb - 1)),
                                    reads=rk + [("pT", bu, 0), ("pT", bu, 1)], writes=[("ps", b2)])
                    ndo = ND[:, :, tok(ti)]
                    pin = self.ps[:, b2, 0:256].rearrange("p (a q) -> p a q", a=2)
                    if g == 0:
                        self.op("dve", lambda e, ndo=ndo, pin=pin: e.tensor_copy(out=ndo, in_=pin), reads=[("ps", b2)], writes=["ND"])
                    else:
                        self.op("dve", lambda e, ndo=ndo, pin=pin: e.tensor_tensor(out=ndo, in0=pin, in1=ndo, op=ALU.add),
                                reads=[("ps", b2), "ND"], writes=["ND"])

                sbank = {}
                for t0 in range(3):
                    sbank[t0] = emit_scores(t0)
                emit_softmax(0, sbank[0])
                emit_softmax(1, sbank[1])
                for ti in range(16):
                    if ti + 3 < 16:
                        sbank[ti + 3] = emit_scores(ti + 3)
                    if ti + 2 < 16:
                        emit_softmax(ti + 2, sbank[ti + 2])
                    emit_pv(ti)
            for tt in range(NTT):
                ts = slice(tt * TT, (tt + 1) * TT)
                rc = self.ssb[:, tt % 2, :]
                self.op("dve", lambda e, rc=rc, ts=ts: e.reciprocal(out=rc, in_=ND[:, 1, ts]), reads=["ND"], writes=[("ssb", tt % 2)])
                self.op("pool", lambda e, rc=rc, ts=ts: e.tensor_tensor(out=rc, in0=rc, in1=ND[:, 0, ts], op=ALU.mult), reads=["ND", ("ssb", tt % 2)], writes=[("ssb", tt % 2)])
                yo = self.yhp[:, hp % 2, ts]
                so = szb[:, ts]
                self.op("pool", lambda e, rc=rc, yo=yo, so=so: e.tensor_tensor(out=yo, in0=rc, in1=so, op=ALU.mult),
                        reads=[("ssb", tt % 2), ("szb", tt)], writes=[("yhp", hp % 2, tt)])
            pending_wout = hp
            if hp == 7:
                emit_wout_hp(7)

    def tt(self, o, a, b, op, eng="dve", extra=()):
        self.op(eng, lambda e: e.tensor_tensor(out=o[0], in0=a[0], in1=b[0], op=op), reads=[a[1], b[1]] + list(extra), writes=[o[1]])

    def tsc(self, o, a, s1, s2, op0, op1, eng="dve", extra=()):
        rk = [a[1]] + list(extra)
        s1v, s2v = s1, s2
        if isinstance(s1, tuple):
            rk.append(s1[1]); s1v = s1[0]
        if isinstance(s2, tuple):
            rk.append(s2[1]); s2v = s2[0]
        self.op(eng, lambda e: e.tensor_scalar(out=o[0], in0=a[0], scalar1=s1v, scalar2=s2v, op0=op0, op1=op1), reads=rk, writes=[o[1]])

    def stt(self, o, a, sc, b, op0, op1, eng="dve", extra=()):
        rk = [a[1], b[1]] + list(extra)
        scv = sc
        if isinstance(sc, tuple):
            rk.append(sc[1]); scv = sc[0]
        self.op(eng, lambda e: e.scalar_tensor_tensor(out=o[0], in0=a[0], scalar=scv, in1=b[0], op0=op0, op1=op1), reads=rk, writes=[o[1]])

    def actf(self, o, a, func, scale=1.0, bias=None, extra=()):
        rk = [a[1]] + list(extra)
        if bias is not None:
            rk.append(bias[1])
            bv = bias[0]
            self.op("act", lambda e: e.activation(out=o[0], in_=a[0], func=func, scale=scale, bias=bv), reads=rk, writes=[o[1]])
        else:
            self.op("act", lambda e: e.activation(out=o[0], in_=a[0], func=func, scale=scale), reads=rk, writes=[o[1]])

    def epoch(self, region, kill=()):
        d = self.dummy
        self.op("pool", lambda e: e.memset(d[:, 0:1], 0.0), writes=[("ep", region)] + list(kill))

    def emit_zoh(self, lre, lim, ldt, scr, tag):
        sl = lambda i: (scr[:, i, :], (tag, i))
        dt, x, mag, th, sn, cs, t1, t2, ar, ai, inv, am1, qr, qi = [sl(i) for i in range(14)]
        hp_ = (self.halfpi[:, 0:1], "halfpi")
        self.actf(dt, ldt, AF.Exp)
        self.tt(x, lre, dt, ALU.mult)
        self.actf(mag, x, AF.Exp)
        self.tt(th, lim, dt, ALU.mult)
        self.actf(sn, th, AF.Sin, scale=1.0 / 32.0)
        self.actf(cs, th, AF.Sin, scale=1.0 / 32.0, bias=hp_)
        for _ in range(5):
            self.tt(t1, cs, cs, ALU.mult)
            self.tt(t2, sn, sn, ALU.mult)
            self.stt(sn, cs, 2.0, sn, ALU.mult, ALU.mult)
            self.tt(cs, t1, t2, ALU.subtract)
        self.tt(ar, mag, cs, ALU.mult)
        self.tt(ai, mag, sn, ALU.mult)
        self.tt(t1, lre, lre, ALU.mult)
        self.tt(t2, lim, lim, ALU.mult)
        self.tt(t1, t1, t2, ALU.add)
        self.op("dve", lambda e: e.reciprocal(out=inv[0], in_=t1[0]), reads=[t1[1]], writes=[inv[1]])
        self.tsc(am1, ar, -1.0, None, ALU.add, ALU.bypass)
        self.tt(t1, am1, lre, ALU.mult)
        self.tt(t2, ai, lim, ALU.mult)
        self.tt(t1, t1, t2, ALU.add)
        self.tt(qr, t1, inv, ALU.mult)
        self.tt(t1, ai, lre, ALU.mult)
        self.tt(t2, am1, lim, ALU.mult)
        self.tt(t1, t1, t2, ALU.subtract)
        self.tt(qi, t1, inv, ALU.mult)
        return ar, ai, qr, qi, t1, t2

    def emit_powers(self, pw, tag, ar, ai, t1, t2, nmax):
        P = lambda m, ri: (pw[:, m, ri, :], (tag, m, ri))
        p0r, p0i = P(0, 0), P(0, 1)
        self.op("pool", lambda e: e.memset(p0r[0], 1.0), writes=[p0r[1]])
        self.op("pool", lambda e: e.memset(p0i[0], 0.0), writes=[p0i[1]])
        for m in range(1, nmax + 1):
            self.tt(t1, P(m - 1, 0), ar, ALU.mult)
            self.tt(t2, P(m - 1, 1), ai, ALU.mult)
            self.tt(P(m, 0), t1, t2, ALU.subtract)
            self.tt(t1, P(m - 1, 0), ai, ALU.mult)
            self.tt(t2, P(m - 1, 1), ar, ALU.mult)
            self.tt(P(m, 1), t1, t2, ALU.add)
        return P

    def emit_s5_layer(self, ada_w, w_in, w_glu, w_out, l1_in, l2_in):
        l = 2
        reqs = [(ada_w[j], "pool", False) for j in range(24)]
        reqs += [(w_in[fc], "act", True) for fc in range(8)]
        if self.next_ada is not None:
            reqs += [(self.next_ada[1][j], "pool", False) for j in range(24)]
        if int(os.environ.get("KS5_STAGE", "9")) >= 3:
            reqs += [(w_in[fc], "act", True) for fc in range(8)]
        if int(os.environ.get("KS5_STAGE", "9")) >= 4:
            for dc in range(NCH):
                reqs += [(w_glu[dc], "pool", True), (w_in[8 + dc], "act", True)]
        S5_DC = [4, 5, 6, 7, 0, 1, 2, 3]
        reqs += [(w_out[dc], "pool", True) for dc in S5_DC]
        self.wstream(reqs, 3)
        self.emit_ada(l, ada_w)
        mod = self.mod
        self.emit_norm(mod[:, 8:16], mod[:, 0:8], self.h, "h")
        h = self.h
        xkeys = [("x", kc, tt) for kc in range(NCH) for tt in range(NTT)]
        for kc in range(NCH):
            self.dma(self.xspill[:, kc, :], self.xt[:, kc, :], reads=[("x", kc, tt) for tt in range(NTT)], writes=[("xspill", kc)])
        self.epoch("bigf", kill=xkeys)
        EPF = ("ep", "bigf")
        XS = self.bigf[:, :].rearrange("p (c r q) -> p c r q", r=2, q=32)
        gbv = self.bigf.bitcast(BF16)[:, 0:NCH * S].rearrange("p (c t) -> p c t", c=NCH)
        Sbf = self.bigb[:, 0:256 * 64].rearrange("p (r q c) -> p r q c", r=2, q=32)
        y2b = self.bigb[:, 0:NCH * S].rearrange("p (c t) -> p c t", c=NCH)
        EPB = ("ep", "bigb")

        lam1 = self.lam1
        self.dma(lam1[:, :, :], l1_in[:, :, 64:67].rearrange("q p c -> p q c"), writes=["lam1"])
        L1 = lambda i: (lam1[:, :, i], "lam1")
        ar1, ai1, qr1, qi1, t1a, t2a = self.emit_zoh(L1(0), L1(1), L1(2), self.scr1, "scr1")
        q1 = self.q1
        self.op("dve", lambda e: e.tensor_copy(out=q1[:, 0, :], in_=qr1[0]), reads=[qr1[1]], writes=["q1"])
        self.op("dve", lambda e: e.tensor_copy(out=q1[:, 1, :], in_=qi1[0]), reads=[qi1[1]], writes=["q1"])
        P1 = self.emit_powers(self.pw1, "pw1", ar1, ai1, t1a, t2a, 8)
        pw1keys = [("pw1", m, ri) for m in range(9) for ri in range(2)]
        scr1keys = [("scr1", i) for i in range(16)]
        AA, AI = self.AA, self.AI
        for r_ in range(2):
            self.op("dve", lambda e, r_=r_: e.tensor_copy(out=AA[:, r_, :], in_=self.pw1[:, 8, 0, :]), reads=[("pw1", 8, 0)], writes=["AA"])
            sgn = -1.0 if r_ == 0 else 1.0
            self.op("dve", lambda e, r_=r_, sgn=sgn: e.tensor_scalar(out=AI[:, r_, :], in0=self.pw1[:, 8, 1, :], scalar1=sgn, scalar2=None, op0=ALU.mult), reads=[("pw1", 8, 1)], writes=["AI"])

        MC = self.MC
        MGb, MCC, BDf, BDall, MB = self.MGb, self.MCC, self.BDf, self.BDall, self.MB
        identb = self.identb
        self.op("act", lambda e: e.activation(out=identb[:, :], in_=self.ident[:, :], func=AF.Copy), reads=["ident"], writes=["identb"])
        Hs = self.Hs
        PT = self.PT
        self.op("pool", lambda e: e.memset(XS[:, 0, :, :], 0.0), reads=[EPF], writes=[("XS", 0)])
        self.op("pool", lambda e: e.memset(MGb[:, :, :, :, :], 0.0), writes=[("MG", 0), ("MG", 1)])
        self.op("pool", lambda e: e.memset(MCC[:, :, :, :], 0.0), writes=[("MCC", 0), ("MCC", 1)])
        hs = lambda i: (Hs[:, i, :, :], ("Hs", i))
        hs_dead = [("Hs", i) for i in range(6)]

        def pair_tile(q):
            pt = PT[:, q % 2, :]
            ptk = ("PT", q % 2)
            self.dma(pt, l1_in[q][:, 0:64], writes=[ptk])
            return pt, ptk, [(pt[:, a:a + 16], ptk) for a in (0, 16, 32, 48)]

        bc8 = lambda a: (_bc(a[0], 1, 8), a[1])
        Apw = lambda q, m0, ri: (_bc(self.pw1[:, m0:m0 + 8, ri, q], 2, 16), ("pw1", 8, 0))
        self.op("dve", lambda e: e.memset(Hs[:, 0, 0, 0:1], 0.0), writes=scr1keys + [("Hs", i) for i in range(6)])

        ub = self.ub16
        ub3 = self.ub16.rearrange("p (i c) -> p i c", i=8)

        def uproj(fc):
            key, wt = self.wget()
            for tt in range(NTT):
                ts = slice(tt * TT, (tt + 1) * TT)
                b = self.next_bank()
                for kc in range(NCH):
                    o = self.psb(b)
                    lhsT = wt[:, kc * 128:(kc + 1) * 128]
                    rhs = h[:, kc, ts]
                    self.op("pe", lambda e, o=o, lhsT=lhsT, rhs=rhs, kc=kc: e.matmul(o, lhsT, rhs, start=(kc == 0), stop=(kc == NCH - 1)),
                            reads=[key, ("h", kc, tt)], writes=[("ps", b)])
                pb = self.psb(b)
                uo = ub3[:, :, tt * 64:(tt + 1) * 64]
                pin_u = pb.rearrange("p (c i) -> p i c", i=8)
                self.op("act", lambda e, uo=uo, pin_u=pin_u: e.activation(out=uo, in_=pin_u, func=AF.Copy), reads=[("ps", b)], writes=["ub"])

        def stageA(q):
            fc, jp = q // 4, q % 4
            pt, ptk, (cre, cim, bre, bim) = pair_tile(q)
            ex = pw1keys
            qrs = (q1[:, 0, q:q + 1], "q1")
            qis = (q1[:, 1, q:q + 1], "q1")
            bbr = (self.bb1[:, 0, :], ("bb1", 0))
            bbi = (self.bb1[:, 1, :], ("bb1", 1))
            tb = (self.bb1[:, 2, :], ("bb1", 2))
            self.tsc(tb, bim, qis, None, ALU.mult, ALU.bypass)
            self.stt(bbr, bre, qrs, tb, ALU.mult, ALU.subtract)
            self.tsc(tb, bre, qis, None, ALU.mult, ALU.bypass)
            self.stt(bbi, bim, qrs, tb, ALU.mult, ALU.add)
            so = (q % 2) * 2
            self.tt(hs(0), bc8(bbr), Apw(q, 0, 0), ALU.mult, extra=ex)
            self.tt(hs(1), bc8(bbi), Apw(q, 0, 1), ALU.mult, extra=ex)
            self.tt(hs(2 + so), hs(0), hs(1), ALU.subtract)
            self.tt(hs(0), bc8(bbr), Apw(q, 0, 1), ALU.mult, extra=ex)
            self.tt(hs(1), bc8(bbi), Apw(q, 0, 0), ALU.mult, extra=ex)
            self.tt(hs(3 + so), hs(0), hs(1), ALU.add)
            mb_ = q % 2
            mgk, mck = ("MG", mb_), ("MCC", mb_)
            if q > 1:
                pj = (jp - 2) % 4
                for g2 in range(2):
                    rows = slice(g2 * 64, (g2 + 1) * 64)
                    cols = slice((2 * pj + g2) * 16, (2 * pj + g2 + 1) * 16)
                    self.op("pool", lambda e, rows=rows, cols=cols, mb_=mb_: e.memset(MGb[rows, mb_, :, :, cols], 0.0), reads=[mgk], writes=[mgk])
                    self.op("pool", lambda e, rows=rows, cols=cols, mb_=mb_: e.memset(MCC[rows, mb_, :, cols], 0.0), reads=[mck], writes=[mck])
            for g2 in range(2):
                rows = slice(g2 * 64, (g2 + 1) * 64)
                cols = slice((2 * jp + g2) * 16, (2 * jp + g2 + 1) * 16)
                for ri in range(2):
                    o = MGb[rows, mb_, :, ri, cols]
                    src = Hs[rows, 2 + so + ri, :, :]
                    self.op("act", lambda e, o=o, src=src: e.activation(out=o, in_=src, func=AF.Copy), reads=[("Hs", 2 + so + ri), mgk], writes=[mgk])
                o2 = MCC[rows, mb_, 0, cols]
                s2 = pt[rows, 0:16]
                self.op("act", lambda e, o2=o2, s2=s2: e.activation(out=o2, in_=s2, func=AF.Copy), reads=[ptk, mck], writes=[mck])
                o3 = MCC[rows, mb_, 1, cols]
                s3 = pt[rows, 16:32]
                self.op("act", lambda e, o3=o3, s3=s3: e.activation(out=o3, in_=s3, func=AF.Copy, scale=-1.0), reads=[ptk, mck], writes=[mck])

        def stageB(q):
            fc, jp = q // 4, q % 4
            mb_ = q % 2
            mgk, mck = ("MG", mb_), ("MCC", mb_)
            for hb in range(2):
                b = self.next_bank()
                for t4 in range(4):
                    tau = hb * 4 + t4
                    o = self.ps[:, b, t4 * 128:(t4 + 1) * 128]
                    for ri in range(2):
                        lhsT = MGb[:, mb_, tau, ri, :]
                        rhs = MCC[:, mb_, ri, :]
                        self.op("pe", lambda e, o=o, lhsT=lhsT, rhs=rhs, ri=ri: e.matmul(o, lhsT, rhs, start=(ri == 0), stop=(ri == 1)),
                                reads=[mgk, mck], writes=[("ps", b)])
                pb = self.psb(b)
                bo = BDf[:, hb * 512:(hb + 1) * 512]
                if jp == 0:
                    self.op("dve", lambda e, bo=bo, pb=pb: e.tensor_copy(out=bo, in_=pb), reads=[("ps", b)], writes=[("BDf", hb)])
                else:
                    self.op("dve", lambda e, bo=bo, pb=pb: e.tensor_tensor(out=bo, in0=pb, in1=bo, op=ALU.add), reads=[("ps", b), ("BDf", hb)], writes=[("BDf", hb)])
            for k4 in range(4):
                b = self.next_bank()
                for sl_, (i, ri) in enumerate([(2 * k4, 0), (2 * k4, 1), (2 * k4 + 1, 0), (2 * k4 + 1, 1)]):
                    o = self.ps[:, b, sl_ * 128:(sl_ + 1) * 128]
                    src = MGb[:, mb_, 7 - i, ri, :]
                    self.op("pe", lambda e, o=o, src=src: e.matmul(o, src, identb[:, :], start=True, stop=True),
                            reads=[mgk, "identb"], writes=[("ps", b)])
                pb = self.psb(b)
                mo = MB[:, mb_, 2 * k4:2 * k4 + 2, :, :]
                pin = pb.rearrange("p (a r c) -> p a r c", a=2, r=2)
                self.op("act", lambda e, mo=mo, pin=pin: e.activation(out=mo, in_=pin, func=AF.Copy), reads=[("ps", b)], writes=[("MB", mb_)])
            if jp == 3:
                dsk = self.small("l2_d_skip")[:, fc:fc + 1]
                self.op("dve", lambda e, dsk=dsk: e.scalar_tensor_tensor(out=BDf[:, 0:128], in0=self.ident[:, :], scalar=dsk, in1=BDf[:, 0:128], op0=ALU.mult, op1=ALU.add),
                        reads=["ident", "sm", ("BDf", 0)], writes=[("BDf", 0)])
                bdo = BDall[:, fc, :]
                self.op("act", lambda e, bdo=bdo: e.activation(out=bdo, in_=BDf[:, :], func=AF.Copy), reads=[("BDf", 0), ("BDf", 1)], writes=[("BD", fc)])

        def stageC(q):
            mb_ = q % 2
            b = self.next_bank()
            for ri in range(2):
                o = self.ps[:, b, ri * 256:(ri + 1) * 256]
                for i in range(8):
                    lhsT = MB[:, mb_, i, ri, :]
                    rhs = ub3[:, i, :]
                    self.op("pe", lambda e, o=o, lhsT=lhsT, rhs=rhs, i=i: e.matmul(o, lhsT, rhs, start=(i == 0), stop=(i == 7)),
                            reads=[("MB", mb_), "ub"], writes=[("ps", b)])
            for ri in range(2):
                pin = self.ps[:, b, ri * 256:(ri + 1) * 256]
                xo = XS[:, 1:257, ri, q]
                self.op("act", lambda e, xo=xo, pin=pin: e.activation(out=xo, in_=pin, func=AF.Copy), reads=[("ps", b), EPF], writes=[("XSq", q)])

        stageA(0)
        stageA(1)
        stageB(0)
        for q in range(32):
            if q % 4 == 0:
                uproj(q // 4)
            if q + 2 < 32:
                stageA(q + 2)
            if q + 1 < 32:
                stageB(q + 1)
            stageC(q)


        xsq = [("XSq", q) for q in range(32)]
        P1t, P2t = self.P1t, self.P2t
        S5ST = int(os.environ.get("KS5_STAGE", "9"))
        ada_bank = None
        if self.next_ada is not None:
            ada_bank = self.emit_ada_mm(self.next_ada[1])
        for c in range(1, 257 if S5ST >= 2 else 1):
            prev = XS[:, c - 1, :, :]
            pv = XS[:, c - 1, 1, :]
            prev_sw = bass.AP(pv.tensor, pv.offset, [list(pv.ap[0]), [-32, 2], [1, 32]])
            cur = XS[:, c, :, :]
            rk = [("XS", c - 1), "AA", "AI", EPF] + (xsq if c == 1 else [])
            self.op("dve", lambda e, prev=prev: e.tensor_tensor(out=P1t[:, :, :], in0=AA[:, :, :], in1=prev, op=ALU.mult), reads=rk, writes=["P1t"])
            self.op("dve", lambda e, prev_sw=prev_sw: e.tensor_tensor(out=P2t[:, :, :], in0=AI[:, :, :], in1=prev_sw, op=ALU.mult), reads=rk, writes=["P2t"])
            wk = [("XS", c)]
            rk2 = xsq if c == 1 else []
            self.op("dve", lambda e, cur=cur: e.tensor_tensor(out=cur, in0=cur, in1=P1t[:, :, :], op=ALU.add), reads=["P1t", EPF] + rk2 + wk, writes=wk)
            self.op("dve", lambda e, cur=cur: e.tensor_tensor(out=cur, in0=cur, in1=P2t[:, :, :], op=ALU.add), reads=["P2t", EPF] + wk, writes=wk)
        if ada_bank is not None:
            self.emit_ada_fin(self.next_ada[0], ada_bank, self.modB, "modB")
            self.pre_ada = self.next_ada[0]
        self.epoch("bigb")
        for part in range(4):
            cs_ = slice(part * 64, (part + 1) * 64)
            so = Sbf[:, :, :, cs_]
            si = XS[:, cs_, :, :].rearrange("p c r q -> p r q c")
            self.op("pool", lambda e, so=so, si=si: e.tensor_copy(out=so, in_=si), reads=[("XS", c) for c in range(part * 64, part * 64 + 64)] + [EPF, EPB], writes=[("Sbf", part)])
        sbfk = [("Sbf", p_) for p_ in range(4)]
        self.epoch("bigf", kill=[("XS", c) for c in range(257)])
        for kc in range(4, NCH):
            self.dma(self.xt[:, kc, :], self.xspill[:, kc, :], reads=[EPF, ("xspill", kc)], writes=[("x", kc, tt) for tt in range(NTT)])
        self.op("pool", lambda e: e.memset(MC[:, :, :, :, :], 0.0), writes=["MC", ("MG", 0), ("MG", 1), ("MB", 0), ("MB", 1)])

        for fc in range(8 if S5ST >= 3 else 0):
            for jp in range(4):
                q = fc * 4 + jp
                pt, ptk, (cre, cim, bre, bim) = pair_tile(q)
                ex = pw1keys
                so = (jp % 2) * 2
                self.tt(hs(0), bc8(cre), Apw(q, 1, 0), ALU.mult, extra=ex)
                self.tt(hs(1), bc8(cim), Apw(q, 1, 1), ALU.mult, extra=ex)
                self.tt(hs(2 + so), hs(0), hs(1), ALU.subtract)
                self.tt(hs(0), bc8(cre), Apw(q, 1, 1), ALU.mult, extra=ex)
                self.tt(hs(1), bc8(cim), Apw(q, 1, 0), ALU.mult, extra=ex)
                self.stt(hs(3 + so), hs(0), -1.0, hs(1), ALU.mult, ALU.subtract)
                for g2 in range(2):
                    rows = slice(g2 * 64, (g2 + 1) * 64)
                    cols = slice((2 * jp + g2) * 16, (2 * jp + g2 + 1) * 16)
                    for ri in range(2):
                        o = MC[rows, jp, :, ri, cols]
                        src = Hs[rows, 2 + so + ri, :, :]
                        self.op("act", lambda e, o=o, src=src: e.activation(out=o, in_=src, func=AF.Copy), reads=[("Hs", 2 + so + ri), "MC"], writes=["MC"])
            key, wt = self.wget()
            for tt in range(NTT):
                ts = slice(tt * TT, (tt + 1) * TT)
                b = self.next_bank()
                for kc in range(NCH):
                    o = self.psb(b)
                    lhsT = wt[:, kc * 128:(kc + 1) * 128]
                    rhs = h[:, kc, ts]
                    self.op("pe", lambda e, o=o, lhsT=lhsT, rhs=rhs, kc=kc: e.matmul(o, lhsT, rhs, start=(kc == 0), stop=(kc == NCH - 1)),
                            reads=[key, ("h", kc, tt)], writes=[("ps", b)])
                pb = self.psb(b)
                uo = ub3[:, :, tt * 64:(tt + 1) * 64]
                pin_u = pb.rearrange("p (c i) -> p i c", i=8)
                self.op("act", lambda e, uo=uo, pin_u=pin_u: e.activation(out=uo, in_=pin_u, func=AF.Copy), reads=[("ps", b)], writes=["ub"])
            for j in range(8):
                b = self.next_bank()
                o = self.ps[:, b, 0:256]
                nmm = (j + 1) + 8
                k_ = 0
                for i in range(j + 1):
                    lhsT = BDall[:, fc, (j - i) * 128:(j - i + 1) * 128]
                    rhs = ub3[:, i, :]
                    self.op("pe", lambda e, o=o, lhsT=lhsT, rhs=rhs, k_=k_, nmm=nmm: e.matmul(o, lhsT, rhs, start=(k_ == 0), stop=(k_ == nmm - 1)),
                            reads=[("BD", fc), "ub"], writes=[("ps", b)])
                    k_ += 1
                for jp in range(4):
                    q = fc * 4 + jp
                    for ri in range(2):
                        lhsT = MC[:, jp, j, ri, :]
                        rhs = Sbf[:, ri, q, :]
                        self.op("pe", lambda e, o=o, lhsT=lhsT, rhs=rhs, k_=k_, nmm=nmm: e.matmul(o, lhsT, rhs, start=(k_ == 0), stop=(k_ == nmm - 1)),
                                reads=["MC", EPB] + sbfk, writes=[("ps", b)])
                        k_ += 1
                go = gbv[:, fc, j::8]
                self.op("act", lambda e, go=go, o=o: e.activation(out=go, in_=o, func=AF.Gelu_apprx_tanh), reads=[("ps", b), EPF], writes=[("gb", fc)])

        self.epoch("bigb", kill=sbfk)
        bgl = self.small("l2_b_glu")
        for dc in range(NCH if S5ST >= 4 else 0):
            keyg, wg = self.wget()
            keyz, wz = self.wget()
            for tt in range(NTT):
                ts = slice(tt * TT, (tt + 1) * TT)
                b = self.next_bank()
                for kc in range(NCH):
                    o = self.psb(b)
                    lhsT = wg[:, kc * 128:(kc + 1) * 128]
                    rhs = gbv[:, kc, ts]
                    self.op("pe", lambda e, o=o, lhsT=lhsT, rhs=rhs, kc=kc: e.matmul(o, lhsT, rhs, start=(kc == 0), stop=(kc == NCH - 1)),
                            reads=[keyg, ("gb", kc), EPF], writes=[("ps", b)])
                bz = self.next_bank()
                for kc in range(NCH):
                    o = self.psb(bz)
                    lhsT = wz[:, kc * 128:(kc + 1) * 128]
                    rhs = h[:, kc, ts]
                    self.op("pe", lambda e, o=o, lhsT=lhsT, rhs=rhs, kc=kc: e.matmul(o, lhsT, rhs, start=(kc == 0), stop=(kc == NCH - 1)),
                            reads=[keyz, ("h", kc, tt)], writes=[("ps", bz)])
                sg = self.sg
                szz = self.szz
                pb = self.psb(b)
                pz = self.psb(bz)
                bcol = bgl[:, dc:dc + 1]
                self.op("act", lambda e, pb=pb, bcol=bcol: e.activation(out=sg[:, :], in_=pb, func=AF.Sigmoid, bias=bcol), reads=[("ps", b), "sm"], writes=["sg"])
                self.op("act", lambda e, pz=pz: e.activation(out=szz[:, :], in_=pz, func=AF.Silu), reads=[("ps", bz)], writes=["szz"])
                gin = gbv[:, dc, ts]
                self.op("dve", lambda e, gin=gin: e.tensor_tensor(out=sg[:, :], in0=sg[:, :], in1=gin, op=ALU.mult), reads=["sg", ("gb", dc), EPF], writes=["sg"])
                yo = y2b[:, dc, ts]
                self.op("dve", lambda e, yo=yo: e.tensor_tensor(out=yo, in0=sg[:, :], in1=szz[:, :], op=ALU.mult), reads=["sg", "szz", EPB], writes=[("y2", dc, tt)])
        self.epoch("bigf", kill=[("gb", fc) for fc in range(8)])
        for kc in range(4):
            self.dma(self.xt[:, kc, :], self.xspill[:, kc, :], reads=[EPF, ("xspill", kc)], writes=[("x", kc, tt) for tt in range(NTT)])
        self.emit_wout(w_out, lambda dc: dc, y2b, "y2", 8, mod[:, 16:24], dc_order=S5_DC)

    def carve_reset(self):
        self.gen_off = 0
        self.bb_off = 0

    def carve(self, shape, dt, pool="gen"):
        n = 1
        for d_ in shape[1:]:
            n *= d_
        if pool == "gen":
            esz = 4 if dt == F32 else 2
            off = (self.gen_off + 31) // 32 * 32
            self.gen_off = off + n * esz
            assert self.gen_off <= self.GEN_BYTES, ("GEN overflow", self.gen_off)
            base = self.GEN if dt == F32 else self.GENb
            ap = base[:, off // esz: off // esz + n]
        else:
            assert dt == BF16
            off = (self.bb_off + 31) // 32 * 32
            self.bb_off = off + n * 2
            assert self.bb_off <= 16448 * 2, ("bigb overflow", self.bb_off)
            ap = self.bigb[:, off // 2: off // 2 + n]
        if len(shape) > 2:
            names = " ".join(f"d{i}" for i in range(len(shape) - 1))
            kw = {f"d{i}": shape[i + 1] for i in range(len(shape) - 2)}
            ap = ap.rearrange(f"p ({names}) -> p {names}", **kw)
        return ap

    def alloc_conv(self):
        self.carve_reset()
        self.y = self.carve([128, 8, S], BF16, "bigb")
        self.usb = self.carve([128, TT], F32)
        self.vbuf = self.carve([128, 2 + S], F32)
        self.cacc = self.carve([128, TT], F32)
        self.gtmp = self.carve([128, TT], F32)
        self.sz = self.carve([128, TT], F32)
        vb = self.vbuf
        self.op("pool", lambda e: e.memset(vb[:, 0:2], 0.0), writes=[("v", -1)])

    def alloc_attn(self):
        self.carve_reset()
        self.qT = self.carve([128, S], BF16, "bigb")
        self.kA = self.carve([128, 2, S], BF16, "bigb")
        self.vA = self.carve([128, 2, 16 * 128], BF16, "bigb")
        self.onesA = self.carve([128, 2, 128], BF16, "bigb")
        self.yhp = self.carve([128, 2, S], BF16, "bigb")
        self.pT = self.carve([128, 4, 512], BF16)
        self.ND = self.carve([128, 2, S], F32)
        self.szb = self.carve([128, S], F32)
        self.tab = self.carve([128, 2, 512], F32)
        self.ssb = self.carve([128, 2, 512], F32)
        self.E = self.carve([128, 4, 512], F32)
        e0 = self.E
        self.soh = bass.AP(e0.tensor, e0.offset, [list(e0.ap[0]), [384, 2], [1, 384]])
        self.rrep = bass.AP(e0.tensor, e0.offset + 768, [list(e0.ap[0]), [384, 2], [1, 384]])
        kA, vA, onesA = self.kA, self.vA, self.onesA
        self.op("pool", lambda e: e.memset(kA[:, :, :], 0.0), writes=["kT"])
        self.op("pool", lambda e: e.memset(vA[:, :, :], 0.0), writes=[("vtm", tb) for tb in range(4)])
        self.op("pool", lambda e: e.memset(onesA[:, :, :], 0.0), writes=["onesb"])
        self.op("pool", lambda e: e.memset(onesA[:, 0, 0:64], 1.0), writes=["onesb"])
        self.op("pool", lambda e: e.memset(onesA[:, 1, 64:128], 1.0), writes=["onesb"])

    def alloc_s5(self):
        self.carve_reset()
        self.bigf = self.xt_raw
        off0 = (self.gen_off + 31) // 32 * 32
        self.MC = self.carve([128, 4, 8, 2, 128], BF16)
        self.MB = self.GENb[:, off0 // 2: off0 // 2 + 4096].rearrange("p (a b c d) -> p a b c d", a=2, b=8, c=2)
        self.MGb = self.GENb[:, (off0 + 8192) // 2: (off0 + 8192) // 2 + 4096].rearrange("p (u t r c) -> p u t r c", u=2, t=8, r=2)
        self.ub16 = self.carve([128, S], BF16)
        self.BDall = self.carve([128, 8, 1024], BF16)
        sgz = self.carve([128, 2 * TT], F32)
        self.sg = sgz[:, 0:TT]
        self.szz = sgz[:, TT:2 * TT]
        self.BDf = sgz
        self.gq = self.sg[:, 0:256]
        self.gt_ = self.szz[:, 0:256]
        self.MCC = self.carve([128, 2, 2, 128], BF16)
        self.identb = self.carve([128, 128], BF16)
        self.lam1 = self.carve([128, 32, 3], F32)
        self.Hs = self.carve([128, 6, 8, 16], F32)
        hs0 = self.Hs
        self.scr1 = bass.AP(hs0.tensor, hs0.offset, [list(hs0.ap[0]), [32, 16], [1, 32]])
        self.pw1 = self.carve([128, 9, 2, 32], F32)
        self.q1 = self.carve([128, 2, 32], F32)
        self.AA = self.carve([128, 2, 32], F32)
        self.AI = self.carve([128, 2, 32], F32)
        self.P1t = self.carve([128, 2, 32], F32)
        self.P2t = self.carve([128, 2, 32], F32)
        self.PT = self.carve([128, 2, 64], F32)
        self.bb1 = self.carve([128, 3, 16], F32)
        self.halfpi = self.carve([128, 1], F32)
        self.ident = self.carve([128, 128], F32)
        hpi = self.halfpi
        self.op("pool", lambda e: e.memset(hpi[:, :], float(np.pi / 2)), writes=["halfpi"])
        self.dma(self.ident[:, :], self.ident_in, writes=["ident"])

    def barrier(self):
        d = self.dummy
        self.sch.add("pool", lambda e: e.memset(d[:, 0:1], 0.0), reads=(), writes=[Sched.BAR], is_barrier=True)

    def build(self):
        nc = self.nc
        L = self.layers
        xin = nc.dram_tensor("xT", [D, S], F32, kind="ExternalInput").ap()
        small_in = nc.dram_tensor("small", [128, SMALL_N], F32, kind="ExternalInput").ap()
        wd = {}

        def wten(name, ntiles):
            wd[name] = nc.dram_tensor(name, [ntiles, 128, 1024], F32, kind="ExternalInput").ap()

        for l in L:
            wten(f"l{l}_ada_w", 24)
            if l in (0, 3):
                wten(f"l{l}_w_in", 64)
                wten(f"l{l}_w_out", 16)
            if l == 2:
                wten("l2_w_in", 16)
                wten("l2_w_glu", 8)
                wten("l2_w_out", 8)
                self.s5l1_in = nc.dram_tensor("s5_l1", [32, 128, 67], F32, kind="ExternalInput").ap()
                self.s5l2_in = nc.dram_tensor("s5_l2", [8, 128, 257], F32, kind="ExternalInput").ap()
                self.ident_in = nc.dram_tensor("s5_ident", [128, 128], F32, kind="ExternalInput").ap()
                self.gmask_in = nc.dram_tensor("s5_gmask", [128, 8], F32, kind="ExternalInput").ap()
                self.xspill = nc.dram_tensor("xspill", [128, NCH, S], F32).ap()
            if l == 1:
                wten("l1_w_in", 80)
                wten("l1_w_outk", 8)
                self.rb_in = nc.dram_tensor("rel_bias", [32, 48], F32, kind="ExternalInput").ap()
                self.oh_in = nc.dram_tensor("attn_oh", [32, 3 * 384], F32, kind="ExternalInput").ap()
                self.negm_in = nc.dram_tensor("attn_negm", [128, 384], F32, kind="ExternalInput").ap()
                self.rscr_h = nc.dram_tensor("rscr", [48, 128, 384], F32)
                self.rscr = self.rscr_h.ap()
        xout = nc.dram_tensor("outT", [D, S], F32, kind="ExternalOutput").ap()

        with contextlib.ExitStack() as st:
            def sb(name, shape, dt):
                return st.enter_context(nc.sbuf_tensor(name, shape, dt))

            self.xt_raw = sb("xt_raw", [128, 16448], F32)
            self.xt = self.xt_raw[:, 0:NCH * S].rearrange("p (c t) -> p c t", c=NCH)
            self.h = sb("h", [128, NCH, S], BF16)
            self.bigb = sb("bigb", [128, 16448], BF16)
            self.GEN_BYTES = 51456
            self.GEN = sb("gen", [128, self.GEN_BYTES // 4], F32)
            self.GENb = self.GEN.bitcast(BF16)
            self.NSTG = 2
            self.NWB = 6
            self.stg = sb("stg", [128, self.NSTG, 1024], F32)
            self.wb = sb("wb", [128, self.NWB, 1024], BF16)
            self.sm = sb("sm", [128, SMALL_N], F32)
            self.sc = sb("sc", [128, 8], F32)
            self.modA = sb("mod", [128, 24], F32)
            self.modB = sb("modB", [128, 24], F32)
            self.mod = self.modA
            self.modk = "mod"
            self.pre_ada = None
            self.ones = sb("ones", [128, 128], F32)
            self.epsc = sb("epsc", [128, 1], F32)
            self.dummy = sb("dummy_ep", [128, 1], F32)
            self.NNT = 2
            self.sq = sb("sq", [128, self.NNT, TT], BF16)
            self.onesb16 = sb("onesb16", [128, 128], BF16)
            self.rstd = sb("rstd", [128, TT], F32)
            self.ps = st.enter_context(nc.psum_tensor("ps", [128, 8, TT], F32))

            engines = ["pe", "act", "dve", "pool", "sp"]
            sems = {e: st.enter_context(nc.semaphore(f"sem_{e}")) for e in engines if e != "sp"}
            dsems = [st.enter_context(nc.semaphore(f"dsem{i}")) for i in range(Sched.NDS)]

            xr = xin.rearrange("(c p) t -> p c t", p=128)
            self.dma(self.sm[:, :], small_in, writes=["sm"])
            for tt in range(NTT):
                ts = slice(tt * TT, (tt + 1) * TT)
                self.dma(self.xt[:, :, ts], xr[:, :, ts], writes=[("x", kc, tt) for kc in range(NCH)])
            ones = self.ones
            self.op("pool", lambda e: e.memset(ones[:, :], 1.0), writes=["ones"])
            epsc = self.epsc
            self.op("pool", lambda e: e.memset(epsc[:, :], EPS), writes=["epsc"])
            o16 = self.onesb16
            self.op("pool", lambda e: e.memset(o16[:, :], 1.0), writes=["onesb16"])
            sc = self.sc
            cin = self.small("c")
            self.op("act", lambda e: e.activation(out=sc[:, :], in_=cin, func=AF.Silu), reads=["sm"], writes=["sc"])

            self.tables_done = False
            for li, l in enumerate(L):
                if li > 0:
                    self.barrier()
                self.next_ada = (L[li + 1], wd[f"l{L[li + 1]}_ada_w"]) if li + 1 < len(L) else None
                if l in (0, 3):
                    self.alloc_conv()
                    if li + 1 < len(L) and L[li + 1] == 1:
                        self.emit_attn_tables()
                    self.emit_conv_layer(l, wd[f"l{l}_ada_w"], wd[f"l{l}_w_in"], wd[f"l{l}_w_out"])
                if l == 1:
                    self.alloc_attn()
                    self.emit_attn_layer(wd["l1_ada_w"], wd["l1_w_in"], wd["l1_w_outk"])
                if l == 2:
                    self.alloc_s5()
                    self.emit_s5_layer(wd["l2_ada_w"], wd["l2_w_in"], wd["l2_w_glu"], wd["l2_w_out"], self.s5l1_in, self.s5l2_in)
            if self.do_final:
                self.emit_norm(self.small("final_g"), None, self.xt, "x")
            xo = xout.rearrange("(c p) t -> p c t", p=128)
            for kc in range(NCH):
                self.dma(xo[:, kc, :], self.xt[:, kc, :], reads=[("x", kc, tt) for tt in range(NTT)])

            run_engine = self.sch.emit(nc, engines, sems, dsems)
            with nc.Block() as block:
                @block.sync
                def _(e):
                    run_engine("sp", e)

                @block.tensor
                def _(e):
                    run_engine("pe", e)

                @block.scalar
                def _(e):
                    run_engine("act", e)

                @block.vector
                def _(e):
                    run_engine("dve", e)

                @block.gpsimd
                def _(e):
                    run_engine("pool", e)
        return nc


def prep_weights(inp, layers):
    w = {}
    for l in layers:
        w[f"l{l}_ada_w"] = tile_w(np.asarray(inp[f"l{l}_ada_w"], np.float32))
        if l in (0, 3):
            w[f"l{l}_w_in"] = tile_w(np.asarray(inp[f"l{l}_w_in"], np.float32))
            w[f"l{l}_w_out"] = tile_w(np.asarray(inp[f"l{l}_w_out"], np.float32))
        if l == 2:
            w["l2_w_in"] = tile_w(np.asarray(inp["l2_w_in"], np.float32))
            w["l2_w_glu"] = tile_w(np.asarray(inp["l2_w_glu"], np.float32))
            w["l2_w_out"] = tile_w(np.asarray(inp["l2_w_out"], np.float32))
            w.update(s5_host_inputs(inp))
        if l == 1:
            w["l1_w_in"] = tile_w(np.asarray(inp["l1_w_in"], np.float32))
            w["l1_w_outk"] = np.ascontiguousarray(np.asarray(inp["l1_w_out"], np.float32).reshape(8, 128, 1024))
            w["rel_bias"] = np.ascontiguousarray(np.asarray(inp["rel_bias"], np.float32))
            oh, negm = attn_static()
            w["attn_oh"] = oh
            w["attn_negm"] = negm
    return w


def run_layers(inp, x, layers, do_final):
    bld = Builder(layers, do_final)
    nc = bld.build()
    w = prep_weights(inp, layers)
    in_maps = []
    for b in range(NCORES):
        m = {"xT": np.ascontiguousarray(x[b].T), "small": pack_small(inp, b)}
        m.update(w)
        in_maps.append(m)
    res = run_bass_kernel_spmd(nc, in_maps, core_ids=list(range(NCORES)))
    out = np.stack([np.ascontiguousarray(r["outT"].T) for r in res.results], axis=0)
    return out.astype(np.float32)


def kernel(**inputs):
    inp = {k: np.asarray(v) for k, v in inputs.items()}
    x = np.asarray(inp["x"], np.float32)
    return run_layers(inp, x, [0, 1, 2, 3], True)
```
